# Optimizing a Trainium2 kernel written in Bass

```python
import math
import jax, jax.numpy as jnp
from jax import lax
import numpy as np

D_MODEL = 1024
BATCH = 8
SEQ = 4096
DEPTH = 1
DEC_BATCH = 32
DEC_SEQ = 2048
PAST_LEN = 128

HEAD_DIM = 64
N_META = 16
GRID_W = 64
Q_BLOCK = 128
ROPE_THETA = 10000.0
NORM_EPS = 1e-6
MIX_WIDTH = D_MODEL
A_WIDTH = MIX_WIDTH // 2
A_Q_HEADS = A_WIDTH // HEAD_DIM
A_KV_HEADS = A_Q_HEADS // 4
A_GROUP = A_Q_HEADS // A_KV_HEADS
B_WIDTH = MIX_WIDTH - A_WIDTH
B_V_DIM = 2 * HEAD_DIM
B_HEADS = B_WIDTH // B_V_DIM
A_Q_COLS = A_Q_HEADS * HEAD_DIM
A_KV_COLS = A_KV_HEADS * HEAD_DIM
B_QK_COLS = B_HEADS * 2 * HEAD_DIM
B_V_COLS = B_HEADS * B_V_DIM
IN_COLS = A_Q_COLS + 2 * A_KV_COLS + 2 * B_QK_COLS + B_V_COLS
D_FF = ((8 * D_MODEL // 3 + 255) // 256) * 256
CONV_WIDTH = 3

kernel_name = "hymba_gqa_axial_diffattn_convglu_encoder"


def lambda_init_fn(layer):
    return 0.8 - 0.6 * math.exp(-0.3 * layer)


def rms_norm(x, g):
    x32 = x.astype(jnp.float32)
    y = x32 * lax.rsqrt(jnp.mean(x32 * x32, axis=-1, keepdims=True) + NORM_EPS)
    return (y * g.astype(jnp.float32)).astype(x.dtype)


def rope_rotate(x, angles):
    half = x.shape[-1] // 2
    x32 = x.astype(jnp.float32)
    x1, x2 = x32[..., :half], x32[..., half:]
    shape = (1, angles.shape[0]) + (1,) * (x.ndim - 3) + (half,)
    cos = jnp.cos(angles).reshape(shape)
    sin = jnp.sin(angles).reshape(shape)
    return jnp.concatenate([x1 * cos - x2 * sin, x2 * cos + x1 * sin], axis=-1).astype(x.dtype)


def axial_angles(n_tokens):
    n_rows = n_tokens // GRID_W
    t = jnp.arange(n_rows * GRID_W)
    row = (t // GRID_W).astype(jnp.float32)
    col = (t % GRID_W).astype(jnp.float32)
    axis_dim = HEAD_DIM // 2
    inv = ROPE_THETA ** (-jnp.arange(0, axis_dim, 2, dtype=jnp.float32) / axis_dim)
    ang = jnp.concatenate([row[:, None] * inv[None], col[:, None] * inv[None]], axis=-1)
    meta = jnp.zeros((N_META, HEAD_DIM // 2), jnp.float32)
    return jnp.concatenate([meta, ang], axis=0)


def linear_angles(n_total):
    pos = jnp.arange(n_total, dtype=jnp.float32)
    inv = ROPE_THETA ** (-jnp.arange(0, HEAD_DIM, 2, dtype=jnp.float32) / HEAD_DIM)
    return pos[:, None] * inv[None]


def sweep_query_blocks(q, block_fn):
    b, l = q.shape[0], q.shape[1]
    n_real = l - N_META
    nb = n_real // Q_BLOCK
    out_meta = block_fn(q[:, :N_META])
    qb = q[:, N_META:].reshape((b, nb, Q_BLOCK) + q.shape[2:])
    qb = jnp.moveaxis(qb, 1, 0)
    ob = lax.map(block_fn, qb)
    ob = jnp.moveaxis(ob, 0, 1).reshape((b, n_real) + ob.shape[3:])
    return jnp.concatenate([out_meta, ob], axis=1)


def mixer_sublayer(x, ang_a, ang_b, layer, g_mix, w_in, g_qnorm_a, g_knorm_a,
                   lambda_q1, lambda_k1, lambda_q2, lambda_k2, g_subln, w_out):
    b, l, _ = x.shape
    n = rms_norm(x, g_mix)
    proj = n @ w_in
    cuts = np.cumsum([A_Q_COLS, A_KV_COLS, A_KV_COLS, B_QK_COLS, B_QK_COLS]).tolist()
    qa, ka, va, qb, kb, vb = jnp.split(proj, cuts, axis=-1)
    scale = HEAD_DIM ** -0.5

    qa = qa.reshape(b, l, A_KV_HEADS, A_GROUP, HEAD_DIM)
    ka = ka.reshape(b, l, A_KV_HEADS, HEAD_DIM)
    va = va.reshape(b, l, A_KV_HEADS, HEAD_DIM)
    qa = rope_rotate(rms_norm(qa, g_qnorm_a), ang_a)
    ka = rope_rotate(rms_norm(ka, g_knorm_a), ang_a)

    def gqa_block(qblk):
        s = jnp.einsum('bqhgd,bkhd->bhgqk', qblk, ka, preferred_element_type=jnp.float32) * scale
        p = jax.nn.softmax(s, axis=-1).astype(va.dtype)
        return jnp.einsum('bhgqk,bkhd->bqhgd', p, va)

    oa = sweep_query_blocks(qa, gqa_block).reshape(b, l, A_WIDTH)

    qb = rope_rotate(qb.reshape(b, l, B_HEADS, 2, HEAD_DIM), ang_b)
    kb = rope_rotate(kb.reshape(b, l, B_HEADS, 2, HEAD_DIM), ang_b)
    vb = vb.reshape(b, l, B_HEADS, B_V_DIM)
    lam_init = lambda_init_fn(layer)
    lam = (jnp.exp(jnp.sum(lambda_q1.astype(jnp.float32) * lambda_k1.astype(jnp.float32)))
           - jnp.exp(jnp.sum(lambda_q2.astype(jnp.float32) * lambda_k2.astype(jnp.float32)))
           + lam_init)

    def diff_block(qblk):
        s = jnp.einsum('bqhcd,bkhcd->bhcqk', qblk, kb, preferred_element_type=jnp.float32) * scale
        p = jax.nn.softmax(s, axis=-1)
        a = (p[:, :, 0] - lam * p[:, :, 1]).astype(vb.dtype)
        return jnp.einsum('bhqk,bkhe->bqhe', a, vb)

    ob = sweep_query_blocks(qb, diff_block)
    ob = (rms_norm(ob, g_subln) * (1.0 - lam_init)).astype(x.dtype).reshape(b, l, B_WIDTH)

    mix = jnp.concatenate([oa, ob], axis=-1)
    return x + mix @ w_out


def conv_glu_sublayer(x, g_ffn, w_ff_gate, w_ff_up, conv_w, conv_b, w_ff_down):
    n = rms_norm(x, g_ffn)
    gate = n @ w_ff_gate
    pad = jnp.pad(gate, ((0, 0), (1, 1), (0, 0)))
    gate = pad[:, :-2] * conv_w[0] + pad[:, 1:-1] * conv_w[1] + pad[:, 2:] * conv_w[2] + conv_b
    return x + (jax.nn.gelu(gate, approximate=False) * (n @ w_ff_up)) @ w_ff_down


def trunk(x, meta_tokens, g_mix, w_in, g_qnorm_a, g_knorm_a, lambda_q1, lambda_k1,
          lambda_q2, lambda_k2, g_subln, w_out, g_ffn, w_ff_gate, w_ff_up, conv_w,
          conv_b, w_ff_down, g_final):
    b, s, d = x.shape
    meta = jnp.broadcast_to(meta_tokens[None].astype(x.dtype), (b, N_META, d))
    h = jnp.concatenate([meta, x], axis=1)
    ang_a = axial_angles(s)
    ang_b = linear_angles(s + N_META)
    for layer in range(DEPTH):
        h = mixer_sublayer(h, ang_a, ang_b, layer, g_mix[layer], w_in[layer],
                           g_qnorm_a[layer], g_knorm_a[layer], lambda_q1[layer],
                           lambda_k1[layer], lambda_q2[layer], lambda_k2[layer],
                           g_subln[layer], w_out[layer])
        h = conv_glu_sublayer(h, g_ffn[layer], w_ff_gate[layer], w_ff_up[layer],
                              conv_w[layer], conv_b[layer], w_ff_down[layer])
    h = rms_norm(h, g_final)
    return h[:, N_META:]


def setup_inputs(seed: int = 0) -> dict:
    key = jax.random.key(seed)
    ks = jax.random.split(key, 20)
    f32 = jnp.float32

    def nrm(k, shape, scale):
        return jax.random.normal(k, shape, f32) * scale

    def gain(k, shape):
        return 1.0 + 0.02 * jax.random.normal(k, shape, f32)

    return {
        "x_prompt": nrm(ks[0], (BATCH, SEQ, D_MODEL), 1.0),
        "x_sample": nrm(ks[1], (DEC_BATCH, DEC_SEQ, D_MODEL), 1.0),
        "meta_tokens": nrm(ks[2], (N_META, D_MODEL), 1.0),
        "g_mix": gain(ks[3], (DEPTH, D_MODEL)),
        "w_in": nrm(ks[4], (DEPTH, D_MODEL, IN_COLS), D_MODEL ** -0.5),
        "g_qnorm_a": gain(ks[5], (DEPTH, HEAD_DIM)),
        "g_knorm_a": gain(ks[6], (DEPTH, HEAD_DIM)),
        "lambda_q1": nrm(ks[7], (DEPTH, HEAD_DIM), 0.1),
        "lambda_k1": nrm(ks[8], (DEPTH, HEAD_DIM), 0.1),
        "lambda_q2": nrm(ks[9], (DEPTH, HEAD_DIM), 0.1),
        "lambda_k2": nrm(ks[10], (DEPTH, HEAD_DIM), 0.1),
        "g_subln": gain(ks[11], (DEPTH, B_V_DIM)),
        "w_out": nrm(ks[12], (DEPTH, MIX_WIDTH, D_MODEL), MIX_WIDTH ** -0.5),
        "g_ffn": gain(ks[13], (DEPTH, D_MODEL)),
        "w_ff_gate": nrm(ks[14], (DEPTH, D_MODEL, D_FF), D_MODEL ** -0.5),
        "w_ff_up": nrm(ks[15], (DEPTH, D_MODEL, D_FF), D_MODEL ** -0.5),
        "conv_w": nrm(ks[16], (DEPTH, CONV_WIDTH, D_FF), CONV_WIDTH ** -0.5),
        "conv_b": nrm(ks[17], (DEPTH, D_FF), 0.01),
        "w_ff_down": nrm(ks[18], (DEPTH, D_FF, D_MODEL), D_FF ** -0.5),
        "g_final": gain(ks[19], (D_MODEL,)),
    }


def reference(x_prompt, x_sample, meta_tokens, g_mix, w_in, g_qnorm_a, g_knorm_a,
              lambda_q1, lambda_k1, lambda_q2, lambda_k2, g_subln, w_out, g_ffn,
              w_ff_gate, w_ff_up, conv_w, conv_b, w_ff_down, g_final):
    y_prompt = trunk(x_prompt, meta_tokens, g_mix, w_in, g_qnorm_a, g_knorm_a,
                     lambda_q1, lambda_k1, lambda_q2, lambda_k2, g_subln, w_out,
                     g_ffn, w_ff_gate, w_ff_up, conv_w, conv_b, w_ff_down, g_final)
    y_sample = trunk(x_sample, meta_tokens, g_mix, w_in, g_qnorm_a, g_knorm_a,
                     lambda_q1, lambda_k1, lambda_q2, lambda_k2, g_subln, w_out,
                     g_ffn, w_ff_gate, w_ff_up, conv_w, conv_b, w_ff_down, g_final)
    return (y_prompt, y_sample)
```

```python
import math
from contextlib import ExitStack
import numpy as np
import concourse.bass as bass
import concourse.mybir as mybir
from concourse.bass_utils import run_bass_kernel_spmd

F32 = mybir.dt.float32
BF16 = mybir.dt.bfloat16
ALU = mybir.AluOpType
AF = mybir.ActivationFunctionType
AX = mybir.AxisListType

D = 1024
NMETA = 16
DFF = 2816
NCH = 22
EPS = 1e-6
LAM_INIT = 0.8 - 0.6 * math.exp(0.0)
SEQ_P = 4096
SEQ_S = 2048
NSAMP = 4
LMAX = SEQ_P + NMETA
ARENA_BYTES = 206 * 1024
NSLOT = 106
SL_A0, SL_A1, SL_Q, SL_QR, SL_O, SL_G, SL_U, SL_D = 0, 8, 16, 24, 32, 40, 62, 84

ENGS = ("pe", "act", "dve", "pool", "sp")


class Op:
    __slots__ = ("eng", "fn", "deps", "signal", "count", "dma_key", "dma_n", "is_dma")

    def __init__(self, eng, fn, is_dma, dma_key, dma_n):
        self.eng = eng
        self.fn = fn
        self.deps = []
        self.signal = False
        self.count = 0
        self.is_dma = is_dma
        self.dma_key = dma_key
        self.dma_n = dma_n


class Sched:
    def __init__(self):
        self.ops = {e: [] for e in ENGS}
        self.last_writer = {}
        self.readers = {}
        self.bar = []
        self.bar_pending = {e: False for e in ENGS}
        self.last_dma = {}

    def op(self, eng, fn, reads=(), writes=(), dma_key=None, dma_n=1):
        is_dma = dma_key is not None
        o = Op(eng, fn, is_dma, dma_key, dma_n)
        deps = {}
        lw = self.last_writer
        for b in reads:
            w = lw.get(b)
            if w is not None:
                deps[id(w)] = w
        for b in writes:
            w = lw.get(b)
            if w is not None:
                deps[id(w)] = w
            rs = self.readers.get(b)
            if rs:
                for r in rs:
                    deps[id(r)] = r
        if self.bar_pending[eng]:
            self.bar_pending[eng] = False
            for d in self.bar:
                deps[id(d)] = d
        for d in deps.values():
            if (not d.is_dma) and d.eng == "pe" and eng == "pe" and not is_dma:
                continue
            d.signal = True
            o.deps.append(d)
        for b in writes:
            lw[b] = o
            self.readers[b] = []
        for b in reads:
            rs = self.readers.setdefault(b, [])
            if not is_dma:
                for i, r in enumerate(rs):
                    if (not r.is_dma) and r.eng == eng:
                        rs[i] = o
                        break
                else:
                    rs.append(o)
            else:
                rs.append(o)
        self.ops[eng].append(o)
        if is_dma:
            self.last_dma[dma_key] = o
        return o

    def barrier(self):
        bar = []
        for e in ENGS:
            for o in reversed(self.ops[e]):
                if not o.is_dma:
                    bar.append(o)
                    break
        bar.extend(self.last_dma.values())
        self.bar = bar
        self.bar_pending = {e: True for e in ENGS}
        self.last_writer = {}
        self.readers = {}

    def emit(self, nc, stack):
        dma_cnt = {}
        for e in ENGS:
            c = 0
            for o in self.ops[e]:
                if o.is_dma:
                    dma_cnt[o.dma_key] = dma_cnt.get(o.dma_key, 0) + 16 * o.dma_n
                    o.count = dma_cnt[o.dma_key]
                elif o.signal:
                    c += 1
                    o.count = c
        esem = {e: stack.enter_context(nc.semaphore("s_" + e)) for e in ENGS if e != "sp"}
        dsem = {k: stack.enter_context(nc.semaphore("d_%d" % i)) for i, k in enumerate(dma_cnt)}
        block = stack.enter_context(nc.Block())

        def run(e):
            def body(eng):
                known = {}
                for o in self.ops[e]:
                    need = {}
                    for d in o.deps:
                        key = ("d", d.dma_key) if d.is_dma else ("e", d.eng)
                        if d.count > need.get(key, 0):
                            need[key] = d.count
                    for key, v in need.items():
                        if known.get(key, 0) >= v:
                            continue
                        known[key] = v
                        sem = dsem[key[1]] if key[0] == "d" else esem[key[1]]
                        eng.wait_ge(sem, v)
                    r = o.fn(eng)
                    if o.is_dma:
                        if not isinstance(r, (list, tuple)):
                            r = [r]
                        assert len(r) == o.dma_n
                        for ins in r:
                            ins.then_inc(dsem[o.dma_key], 16)
                    elif o.signal:
                        r.then_inc(esem[e], 1)
                last = {}
                for o in self.ops[e]:
                    if o.is_dma:
                        last[o.dma_key] = o.count
                for k, v in last.items():
                    if known.get(("d", k), 0) < v:
                        eng.wait_ge(dsem[k], v)
            return body

        block.tensor(run("pe"))
        block.scalar(run("act"))
        block.vector(run("dve"))
        block.gpsimd(run("pool"))
        block.sync(run("sp"))


class Arena:
    def __init__(self, ap, base=0):
        self.ap = ap
        self.off = base

    def f32(self, n):
        self.off = (self.off + 31) // 32 * 32
        a = self.ap[:, self.off // 4:self.off // 4 + n]
        self.off += 4 * n
        assert self.off <= ARENA_BYTES, self.off
        return a

    def bf16(self, n):
        n2 = (n + 1) // 2
        return self.f32(n2).bitcast(BF16)[:, 0:n]


def build_program(seq_cfg):
    nc = bass.Bass("TRN2", target_bir_lowering=False)

    def din(name, shape):
        return nc.dram_tensor(name, list(shape), F32, kind="ExternalInput").ap()

    xp = din("xp", [SEQ_P, D])
    xs = din("xs", [NSAMP, SEQ_S, D])
    meta_d = din("meta", [NMETA, D])
    w_in_d = din("w_in", [D, 2304])
    w_out_d = din("w_out", [D, D])
    w_gate_d = din("w_gate", [D, DFF])
    w_up_d = din("w_up", [D, DFF])
    w_down_d = din("w_down", [DFF, D])
    g_mix_d = din("g_mix", [8, 128])
    g_ffn_d = din("g_ffn", [8, 128])
    gq_d = din("gq", [1, 64])
    gk_d = din("gk", [1, 64])
    lam_d = din("lam", [4, 64])
    gsub_d = din("gsub", [1, 128])
    convw_d = din("convw", [66, 128])
    convb_d = din("convb", [22, 128])
    gfin_d = din("gfin", [1, D])
    tabs_d = din("tabs", [4, 128, LMAX])
    identf_d = din("identf", [128, 128])
    onesbd_d = din("onesbd", [128, 128])
    rmat_d = din("rmat", [128, 128])
    yp = nc.dram_tensor("yp", [SEQ_P, D], F32, kind="ExternalOutput").ap()
    ys = nc.dram_tensor("ys", [NSAMP, SEQ_S, D], F32, kind="ExternalOutput").ap()
    W = nc.dram_tensor("wscr", [NSLOT, 128, 1024], BF16, kind="Internal").ap()

    S = Sched()
    dcnt = [0]

    def dump(name, ap, keys):
        if not DBG_ON:
            return
        dcnt[0] += 1
        t = nc.dram_tensor("dbg_" + name, list(ap.shape), ap.dtype, kind="ExternalOutput").ap()
        S.op("sp", lambda e: e.dma_start(out=t, in_=ap), reads=keys, dma_key=("dbg", dcnt[0]))
    with ExitStack() as st:
        arena_t = st.enter_context(nc.sbuf_tensor("arena", [128, ARENA_BYTES // 4], F32))
        psum_t = st.enter_context(nc.psum_tensor("psum", [128, 4096], F32))
        PS = psum_t[:]
        AR = Arena(arena_t[:])

        def bank(b):
            return PS[:, b * 512:(b + 1) * 512]

        bank_ctr = [0]

        def next_bank():
            b = bank_ctr[0] % 8
            bank_ctr[0] += 1
            return b

        identb = AR.bf16(128)
        onesbd = AR.bf16(128)
        rmat = AR.bf16(128)
        cols = AR.f32(128)
        ncols = AR.f32(128)
        gfin = AR.f32(D)
        gsub = AR.f32(128)
        lams = AR.f32(8)
        epsc = AR.f32(1)
        CONST_END = AR.off
        identf = AR.f32(128)
        onesf = AR.f32(128)
        lamt = AR.f32(256)
        stg = AR.f32(128)
        SETUP_END = AR.off

        def dma(fn, reads=(), writes=(), key=None, n=1):
            S.op("sp", fn, reads=reads, writes=writes, dma_key=key, dma_n=n)

        dma(lambda e: e.dma_start(out=identf, in_=identf_d[:, :]), writes=["identf"], key="c_identf")
        dma(lambda e: e.dma_start(out=onesf, in_=onesbd_d[:, :]), writes=["onesf"], key="c_onesf")
        dma(lambda e: e.dma_start(out=gfin, in_=gfin_d[0:1, :].partition_broadcast(128)), writes=["gfin"], key="c_gfin")
        dma(lambda e: e.dma_start(out=gsub, in_=gsub_d[0:1, :].partition_broadcast(128)), writes=["gsub"], key="c_gsub")
        dma(lambda e: [e.dma_start(out=lamt[:, i * 64:(i + 1) * 64], in_=lam_d[i:i + 1, :].partition_broadcast(128))
                       for i in range(4)], writes=["lamt"], key="c_lamt", n=4)
        S.op("pool", lambda e: e.memset(stg, 0.0), writes=["stg"])
        S.op("pool", lambda e: e.memset(epsc, EPS), writes=["epsc"])

        def stg_loads(e):
            r = []
            r.append(e.dma_start(out=stg[0:66, :], in_=convw_d[:, :]))
            r.append(e.dma_start(out=stg[66:88, :], in_=convb_d[:, :]))
            r.append(e.dma_start(out=stg[88:96, :], in_=g_mix_d[:, :]))
            r.append(e.dma_start(out=stg[96:104, :], in_=g_ffn_d[:, :]))
            for row, src in ((104, gq_d), (106, gk_d)):
                for h in range(2):
                    r.append(e.dma_start(out=stg[row:row + 1, h * 64:(h + 1) * 64], in_=src[0:1, :]))
                    r.append(e.dma_start(out=stg[row + 1:row + 2, h * 64:h * 64 + 32], in_=src[0:1, 32:64]))
                    r.append(e.dma_start(out=stg[row + 1:row + 2, h * 64 + 32:h * 64 + 64], in_=src[0:1, 0:32]))
            return r
        dma(stg_loads, reads=[], writes=["stg"], key="c_stg", n=16)
        S.op("dve", lambda e: e.tensor_copy(out=identb, in_=identf), reads=["identf"], writes=["identb"])
        S.op("dve", lambda e: e.tensor_copy(out=onesbd, in_=onesf), reads=["onesf"], writes=["onesbd"])
        dma(lambda e: e.dma_start(out=onesf, in_=rmat_d[:, :]), reads=["onesbd"], writes=["onesf"], key="c_onesf")
        S.op("dve", lambda e: e.tensor_copy(out=rmat, in_=onesf), reads=["onesf"], writes=["rmat"])
        S.op("pe", lambda e: e.transpose(out=bank(0)[:, 0:128], in_=stg, identity=identf), reads=["stg", "identf"], writes=["ps0"])
        S.op("dve", lambda e: e.tensor_copy(out=cols, in_=bank(0)[:, 0:128]), writes=["ps0", "cols"])
        S.op("dve", lambda e: e.scalar_tensor_tensor(out=lamt[:, 0:64], in0=lamt[:, 0:64], scalar=1.0, in1=lamt[:, 64:128],
                                                      op0=ALU.mult, op1=ALU.mult, accum_out=lams[:, 0:1]),
             reads=["lamt"], writes=["lamt", "lams"])
        S.op("dve", lambda e: e.scalar_tensor_tensor(out=lamt[:, 128:192], in0=lamt[:, 128:192], scalar=1.0, in1=lamt[:, 192:256],
                                                      op0=ALU.mult, op1=ALU.mult, accum_out=lams[:, 1:2]),
             reads=["lamt", "lams"], writes=["lamt", "lams"])
        S.op("act", lambda e: e.activation(out=lams[:, 2:4], in_=lams[:, 0:2], func=AF.Exp), reads=["lams"], writes=["lams"])
        S.op("dve", lambda e: e.tensor_tensor(out=lams[:, 4:5], in0=lams[:, 2:3], in1=lams[:, 3:4], op=ALU.subtract),
             reads=["lams"], writes=["lams"])
        S.op("dve", lambda e: e.tensor_scalar(out=lams[:, 5:6], in0=lams[:, 4:5], scalar1=LAM_INIT, scalar2=-1.0,
                                               op0=ALU.add, op1=ALU.mult), reads=["lams"], writes=["lams"])
        neglam = lams[:, 5:6]
        dump("cols", cols, ["cols"])
        dump("lams", lams, ["lams"])

        def col(i):
            return cols[:, i:i + 1]

        S.op("dve", lambda e: e.tensor_scalar(out=ncols, in0=cols, scalar1=-1.0, scalar2=None, op0=ALU.mult),
             reads=["cols"], writes=["cols"])

        PA = Arena(arena_t[:], SETUP_END)
        NPB = 4
        wf = [PA.f32(DFF) for _ in range(NPB)]
        wb = [PA.bf16(4096) for _ in range(NPB)]
        pe_rr = [0]

        def ew(dst, src, gi, neg, rk, wk):
            eng = ("dve", "act")[pe_rr[0] % 2]
            pe_rr[0] += 1
            if gi is None:
                if eng == "dve":
                    S.op("dve", lambda e: e.tensor_copy(out=dst, in_=src), reads=[rk], writes=[wk])
                else:
                    S.op("act", lambda e: e.activation(out=dst, in_=src, func=AF.Copy), reads=[rk], writes=[wk])
                return
            g = (ncols if neg else cols)[:, gi:gi + 1]
            if eng == "dve":
                S.op("dve", lambda e: e.tensor_scalar(out=dst, in0=src, scalar1=g, scalar2=None, op0=ALU.mult),
                     reads=[rk, "cols"], writes=[wk])
            else:
                S.op("act", lambda e: e.activation(out=dst, in_=src, func=AF.Copy, scale=g), reads=[rk, "cols"], writes=[wk])

        pidx = [0]

        def prep_next():
            b = pidx[0] % NPB
            pidx[0] += 1
            return b

        def cp(dst, src, g, rk, wk):
            ew(dst, src, g, False, rk, wk)

        def cpneg(dst, src, g, rk, wk):
            ew(dst, src, g, True, rk, wk)

        def rot(dst, src, g, nh, rk, wk):
            dv = dst.rearrange("p (h t e) -> p h t e", h=nh, t=2)
            sv = src.rearrange("p (h t e) -> p h t e", h=nh, t=2)
            cpneg(dv[:, :, 0, :], sv[:, :, 1, :], g, rk, wk)
            cp(dv[:, :, 1, :], sv[:, :, 0, :], g, rk, wk)

        for k in range(8):
            b = prep_next()
            f, o = wf[b], wb[b]
            fk, ok = "wf%d" % b, "wb%d" % b
            dma(lambda e, f=f, k=k: e.dma_start(out=f[:, 0:2304], in_=w_in_d[k * 128:(k + 1) * 128, :]), writes=[fk], key=fk)
            g = 88 + k
            A0, A1, Q, QR = (o[:, i * 1024:(i + 1) * 1024] for i in range(4))
            cp(A0[:, 0:128], f[:, 512:640], g, fk, ok)
            rot(A0[:, 128:256], f[:, 512:640], g, 2, fk, ok)
            cp(A0[:, 256:384], f[:, 640:768], g, fk, ok)
            cp(A0[:, 384:896], f[:, 1792:2304], g, fk, ok)
            cp(A0[:, 896:1024], f[:, 0:128], g, fk, ok)
            cp(A1[:, 0:512], f[:, 1280:1792], g, fk, ok)
            rot(A1[:, 512:1024], f[:, 1280:1792], g, 8, fk, ok)
            qd = Q[:, 0:512].rearrange("p (j t d) -> p j t d", j=4, t=2)
            qs = f[:, 0:512].rearrange("p (t j d) -> p j t d", t=2, j=4)
            for t in range(2):
                cp(qd[:, :, t, :], qs[:, :, t, :], g, fk, ok)
            cp(Q[:, 512:1024], f[:, 768:1280], g, fk, ok)
            qrd = QR[:, 0:512].rearrange("p (j t h e) -> p j t h e", j=4, t=2, h=2)
            qrs = f[:, 0:512].rearrange("p (t j h e) -> p j t h e", t=2, j=4, h=2)
            for t in range(2):
                cpneg(qrd[:, :, t, 0, :], qrs[:, :, t, 1, :], g, fk, ok)
                cp(qrd[:, :, t, 1, :], qrs[:, :, t, 0, :], g, fk, ok)
            rot(QR[:, 512:1024], f[:, 768:1280], g, 8, fk, ok)
            dma(lambda e, o=o, k=k: [e.dma_start(out=W[base + k, :, :], in_=o[:, i * 1024:(i + 1) * 1024])
                                     for i, base in enumerate((SL_A0, SL_A1, SL_Q, SL_QR))],
                reads=[ok], writes=["W"], key=ok, n=4)
        S2_BASE = ARENA_BYTES - (2 * DFF * 4 + DFF * 2)
        PA2 = Arena(arena_t[:], S2_BASE)
        wf2 = [PA2.f32(DFF) for _ in range(2)]
        wb2 = PA2.bf16(DFF)
        prep_steps = []
        p2 = [0]

        def add_step(src_ap, ncols, gi, store_fn):
            def step():
                b = p2[0] % 2
                p2[0] += 1
                f, o = wf2[b], wb2
                fk, ok = "wf2_%d" % b, "wb2"
                dma(lambda e: e.dma_start(out=f[:, 0:ncols], in_=src_ap), writes=[fk], key=fk)
                ew(o[:, 0:ncols], f[:, 0:ncols], gi, False, fk, ok)
                dma(lambda e: store_fn(e, o), reads=[ok], writes=["W"], key=ok)
            prep_steps.append(step)

        for j in range(8):
            add_step(w_out_d[j * 128:(j + 1) * 128, :], 1024, None,
                     lambda e, o, j=j: e.dma_start(out=W[SL_O + j, :, :], in_=o[:, 0:1024]))
        for (wd, base) in ((w_gate_d, SL_G), (w_up_d, SL_U)):
            for k in range(8):
                add_step(wd[k * 128:(k + 1) * 128, :], DFF, 96 + k,
                         lambda e, o, k=k, base=base: e.dma_start(
                             out=W[base:base + NCH, :, k * 128:(k + 1) * 128].rearrange("c p j -> p c j"),
                             in_=o[:, 0:DFF].rearrange("p (c j) -> p c j", j=128)))
        for c in range(NCH):
            add_step(w_down_d[c * 128:(c + 1) * 128, :], 1024, None,
                     lambda e, o, c=c: e.dma_start(out=W[SL_D + c, :, :], in_=o[:, 0:1024]))

        def rstd_from_ss(ss_ap, n, scale, key, rows=128):
            S.op("act", lambda e: e.activation(out=ss_ap, in_=ss_ap, func=AF.Ln, scale=scale, bias=epsc[0:rows, :]),
                 reads=[key, "epsc"], writes=[key])
            S.op("act", lambda e: e.activation(out=ss_ap, in_=ss_ap, func=AF.Exp, scale=-0.5), reads=[key], writes=[key])

        def norm_T(srcs, dstT, dkeyf, NTt, ss, sskey, nb, junk, sq_eng):
            n = len(srcs)
            for i, (xa, xk, rows) in enumerate(srcs):
                if sq_eng == "act":
                    S.op("act", lambda e, xa=xa, i=i, rows=rows: e.activation(out=junk[0:rows, :], in_=xa, func=AF.Square,
                                                                               accum_out=ss[0:rows, i:i + 1]),
                         reads=[xk], writes=["junk", sskey])
                else:
                    S.op("dve", lambda e, xa=xa, i=i, rows=rows: e.scalar_tensor_tensor(
                        out=junk[0:rows, :], in0=xa, scalar=1.0, in1=xa, op0=ALU.mult, op1=ALU.mult,
                        accum_out=ss[0:rows, i:i + 1]), reads=[xk], writes=["junk", sskey])
            rows0 = srcs[0][2]
            rstd_from_ss(ss[0:rows0, 0:n], n, 1.0 / D, sskey, rows0)
            for i, (xa, xk, rows) in enumerate(srcs):
                nbi = nb[i % len(nb)]
                nk = "nb%d" % (i % len(nb))
                S.op("dve", lambda e, xa=xa, i=i, rows=rows, nbi=nbi: e.tensor_scalar(
                    out=nbi[0:rows, :], in0=xa, scalar1=ss[0:rows, i:i + 1], scalar2=None, op0=ALU.mult),
                    reads=[xk, sskey], writes=[nk])
                pb = next_bank()
                psb = bank(pb).bitcast(BF16)
                for j in range(8):
                    S.op("pe", lambda e, j=j, rows=rows, nbi=nbi, psb=psb: e.transpose(
                        out=psb[:, j * 128:j * 128 + rows], in_=nbi[0:rows, j * 128:(j + 1) * 128],
                        identity=identb[0:rows, 0:rows]), reads=[nk, "identb"], writes=["ps%d" % pb])
                ev = "act" if sq_eng == "act" else "dve"
                src_v = psb.rearrange("p (j t) -> p j t", j=8)[:, :, 0:rows]
                dst_v = dstT[:, :, i * 128:i * 128 + rows]
                if ev == "act":
                    S.op("act", lambda e, src_v=src_v, dst_v=dst_v: e.activation(out=dst_v, in_=src_v, func=AF.Copy),
                         writes=["ps%d" % pb] + dkeyf(i))
                else:
                    S.op("dve", lambda e, src_v=src_v, dst_v=dst_v: e.tensor_copy(out=dst_v, in_=src_v),
                         writes=["ps%d" % pb] + dkeyf(i))

        def proj_fm(wslots, wkeys, c0, srcT, skeys, ntok, pb, col_off=0):
            for k in range(8):
                S.op("pe", lambda e, k=k: e.matmul(bank(pb)[:, col_off:col_off + ntok], lhsT=wslots[k][:, c0:c0 + 128],
                                                   rhs=srcT[:, k, 0:ntok], start=(k == 0), stop=(k == 7)),
                     reads=[wkeys[k]] + skeys, writes=["ps%d" % pb])

        def qk_post(kind, pP, pR, ntok, tabs, dsts, gcol, grcol, T):
            P = bank(pP)[:, 0:ntok]
            R = bank(pR)[:, 0:ntok]
            tA, tB, sqb, rq = T["tA"], T["tB"], T["sqb"], T["rq"]
            if kind == "A":
                cosT, sinT = tabs[:, 0, 0:ntok], tabs[:, 1, 0:ntok]
                S.op("act", lambda e: e.activation(out=sqb[:, 0:ntok], in_=P, func=AF.Square), writes=["ps%d" % pP, "sqb"])
                pS = next_bank()
                S.op("pe", lambda e: e.matmul(bank(pS)[:, 0:ntok], lhsT=onesbd, rhs=sqb[:, 0:ntok], start=True, stop=True),
                     reads=["sqb", "onesbd"], writes=["ps%d" % pS])
                S.op("act", lambda e: e.activation(out=rq[:, 0:ntok], in_=bank(pS)[:, 0:ntok], func=AF.Ln, scale=1.0 / 64,
                                                   bias=epsc), reads=["epsc"], writes=["ps%d" % pS, "rq"])
                S.op("act", lambda e: e.activation(out=rq[:, 0:ntok], in_=rq[:, 0:ntok], func=AF.Exp, scale=-0.5),
                     reads=["rq"], writes=["rq"])
                S.op("dve", lambda e: e.scalar_tensor_tensor(out=tA[:, 0:ntok], in0=P, scalar=col(gcol), in1=cosT,
                                                              op0=ALU.mult, op1=ALU.mult),
                     reads=["tabs", "cols"], writes=["ps%d" % pP, "tA"])
                S.op("dve", lambda e: e.scalar_tensor_tensor(out=tB[:, 0:ntok], in0=R, scalar=col(grcol), in1=sinT,
                                                              op0=ALU.mult, op1=ALU.mult),
                     reads=["tabs", "cols"], writes=["ps%d" % pR, "tB"])
                S.op("pool", lambda e: e.tensor_tensor(out=tA[:, 0:ntok], in0=tA[:, 0:ntok], in1=tB[:, 0:ntok], op=ALU.add),
                     reads=["tB"], writes=["tA"])
                for (dst, p0, p1, dkey) in dsts:
                    S.op("pool", lambda e, dst=dst, p0=p0, p1=p1: e.tensor_tensor(out=dst, in0=tA[p0:p1, 0:ntok], in1=rq[p0:p1, 0:ntok],
                                                                                  op=ALU.mult), reads=["tA", "rq"], writes=[dkey])
            else:
                cosT, sinT = tabs[:, 2, 0:ntok], tabs[:, 3, 0:ntok]
                S.op("dve", lambda e: e.tensor_tensor(out=tA[:, 0:ntok], in0=P, in1=cosT, op=ALU.mult),
                     reads=["tabs"], writes=["ps%d" % pP, "tA"])
                S.op("dve", lambda e: e.tensor_tensor(out=tB[:, 0:ntok], in0=R, in1=sinT, op=ALU.mult),
                     reads=["tabs"], writes=["ps%d" % pR, "tB"])
                for (dst, p0, p1, dkey) in dsts:
                    S.op("pool", lambda e, dst=dst, p0=p0, p1=p1: e.tensor_tensor(out=dst, in0=tA[p0:p1, 0:ntok], in1=tB[p0:p1, 0:ntok],
                                                                                  op=ALU.add), reads=["tA", "tB"], writes=[dkey])

        def do_sequence(sidx, xsrc, ydst, SEQ, NT, init):
            L = SEQ + NMETA
            Lp = L + (L % 2)
            nsb = NT // 128
            ntile = SEQ // NT
            nchunk = 1 + SEQ // 128
            sid = "q%d" % sidx

            KT, VA, VB, KV_END = phase_A(xsrc, SEQ, 512, L, Lp, 4, SEQ // 512, nchunk, init)
            phase_B(xsrc, ydst, SEQ, NT, L, Lp, nsb, ntile, nchunk, KT, VA, VB, KV_END, init)

        def phase_A(xsrc, SEQ, NT, L, Lp, nsb, ntile, nchunk, init):
            S.barrier()
            A = Arena(arena_t[:], CONST_END)
            KT = A.bf16(5 * Lp).rearrange("p (j l) -> p j l", j=5)
            VAf = A.bf16(nchunk * 2 * 72)
            VBf = A.bf16(nchunk * 4 * 136)
            VA = VAf.rearrange("p (c h e) -> p c h e", c=nchunk, h=2)
            VB = VBf.rearrange("p (c h e) -> p c h e", c=nchunk, h=4)
            KV_END = A.off
            wA = A.bf16(16 * 1024).rearrange("p (s c) -> p s c", s=16)
            xr = A.f32(4 * D).rearrange("p (s c) -> p s c", s=4)
            nT = [A.bf16(8 * NT).rearrange("p (k t) -> p k t", k=8) for _ in range(2)]
            nb = [A.bf16(D) for _ in range(2)]
            junk = A.bf16(D)
            tabs = A.f32(4 * NT).rearrange("p (a t) -> p a t", a=4)
            T = {"tA": A.f32(NT), "tB": A.f32(NT), "sqb": A.bf16(NT), "rq": A.f32(NT)}
            ss = A.f32(8)
            PbA = [A.bf16(NT) for _ in range(2)]
            if prep_steps:
                assert A.off <= S2_BASE, (A.off, S2_BASE)
            per_tile = (len(prep_steps) + ntile) // (ntile + 1) if prep_steps else 0

            if init:
                S.op("pool", lambda e: e.memset(VAf, 1.0), writes=["VAall"])
                S.op("pool", lambda e: e.memset(VBf, 1.0), writes=["VBall"])
            for s in range(16):
                dma(lambda e, s=s: e.dma_start(out=wA[:, s, :], in_=W[s, :, :]), reads=["W"], writes=[("wA", s)], key=("wA", s))
            wA0 = [wA[:, k, :] for k in range(8)]
            wA1 = [wA[:, 8 + k, :] for k in range(8)]
            kA0 = [("wA", k) for k in range(8)]
            kA1 = [("wA", 8 + k) for k in range(8)]

            xctr = [0]
            tiles = [("meta", 0)] + [("real", i) for i in range(ntile)]

            def prepare(kind, ti):
                if kind == "meta":
                    ntok, col0 = NMETA, 0
                    sbs = [(meta_d[:, :], NMETA, 0)]
                else:
                    ntok, col0 = NT, NMETA + ti * NT
                    sbs = [(xsrc[ti * NT + s * 128: ti * NT + (s + 1) * 128, :], 128, 1 + ti * nsb + s) for s in range(nsb)]
                srcs = []
                for (src, rows, chunk) in sbs:
                    sl = xctr[0] % 4
                    xctr[0] += 1
                    dma(lambda e, sl=sl, src=src, rows=rows: e.dma_start(out=xr[0:rows, sl, :], in_=src),
                        writes=[("x", sl)], key=("x", sl))
                    srcs.append((xr[0:rows, sl, :], ("x", sl), rows))
                nTb = nT[(ti + 1) % 2 if kind == "real" else 0]
                nTk = "nT%d" % ((ti + 1) % 2 if kind == "real" else 0)
                norm_T(srcs, nTb, (lambda i, nTk=nTk: [(nTk, i)]), NT, ss, "ss", nb, junk, "act")
                nTkeys = [(nTk, i) for i in range(len(srcs))]
                return (kind, ti, ntok, col0, sbs, nTb, nTk, nTkeys)

            def process(kind, ti, ntok, col0, sbs, nTb, nTk, nTkeys):
                dma(lambda e, col0=col0, ntok=ntok: e.dma_start(
                    out=tabs[:, :, 0:ntok], in_=tabs_d[:, :, col0:col0 + ntok].rearrange("a p n -> p a n")),
                    writes=["tabs"], key="tabs")
                def k_finish(jt, pP, kind=kind, ti=ti, ntok=ntok, col0=col0):
                    pR = next_bank()
                    pb, pbk = PbA[jt % 2], "pbA%d" % (jt % 2)
                    S.op("pe", lambda e: e.matmul(bank(pR)[:, 0:ntok], lhsT=rmat, rhs=pb[:, 0:ntok], start=True, stop=True),
                         reads=[pbk, "rmat"], writes=["ps%d" % pR])
                    qk_post("A" if jt == 0 else "B", pP, pR, ntok, tabs,
                            [(KT[:, jt, col0:col0 + ntok], 0, 128, ("K", kind, ti))], 106, 107, T)
                prev = None
                for jt in range(5):
                    pP = next_bank()
                    if jt == 0:
                        proj_fm(wA0, kA0, 0, nTb, nTkeys, ntok, pP)
                    else:
                        proj_fm(wA1, kA1, (jt - 1) * 128, nTb, nTkeys, ntok, pP)
                    S.op("act", lambda e, jt=jt, pP=pP, ntok=ntok: e.activation(out=PbA[jt % 2][:, 0:ntok], in_=bank(pP)[:, 0:ntok],
                                                                                func=AF.Copy), writes=["ps%d" % pP, "pbA%d" % (jt % 2)])
                    if prev is not None:
                        k_finish(*prev)
                    prev = (jt, pP)
                k_finish(*prev)
                for i, (src, rows, chunk) in enumerate(sbs):
                    pa, pbk = next_bank(), next_bank()
                    for k in range(8):
                        S.op("pe", lambda e, k=k, i=i, rows=rows, pa=pa, nTb=nTb: e.matmul(
                            bank(pa)[0:rows, 0:128], lhsT=nTb[:, k, i * 128:i * 128 + rows], rhs=wA0[k][:, 256:384],
                            start=(k == 0), stop=(k == 7)), reads=[kA0[k], (nTk, i)], writes=["ps%d" % pa])
                    for k in range(8):
                        S.op("pe", lambda e, k=k, i=i, rows=rows, pbk=pbk, nTb=nTb: e.matmul(
                            bank(pbk)[0:rows, 0:512], lhsT=nTb[:, k, i * 128:i * 128 + rows], rhs=wA0[k][:, 384:896],
                            start=(k == 0), stop=(k == 7)), reads=[kA0[k], (nTk, i)], writes=["ps%d" % pbk])
                    S.op("act", lambda e, rows=rows, chunk=chunk, pa=pa: e.activation(
                        out=VA[0:rows, chunk, :, 0:64], in_=bank(pa)[0:rows, 0:128].rearrange("p (h e) -> p h e", h=2),
                        func=AF.Copy), reads=["VAall"], writes=["ps%d" % pa, ("VA", chunk)])
                    S.op("act", lambda e, rows=rows, chunk=chunk, pbk=pbk: e.activation(
                        out=VB[0:rows, chunk, :, 0:128], in_=bank(pbk)[0:rows, 0:512].rearrange("p (h e) -> p h e", h=4),
                        func=AF.Copy), reads=["VBall"], writes=["ps%d" % pbk, ("VB", chunk)])
                for _ in range(per_tile):
                    if prep_steps:
                        prep_steps.pop(0)()

            stt = prepare(*tiles[0])
            for tidx in range(len(tiles)):
                nxt = prepare(*tiles[tidx + 1]) if tidx + 1 < len(tiles) else None
                process(*stt)
                stt = nxt

            while prep_steps:
                prep_steps.pop(0)()

            dump("KT", KT, [("K", k_, t_) for (k_, t_) in tiles])
            dump("wA", wA, [("wA", s_) for s_ in range(16)])
            dump("tA", T["tA"], ["tA"])
            dump("tB", T["tB"], ["tB"])
            dump("tabsA", tabs, ["tabs"])
            dump("VA", VA, [("VA", c_) for c_ in range(nchunk)])
            dump("VB", VB, [("VB", c_) for c_ in range(nchunk)])
            dump("nTA", nT[0], ["nT0", ("nT0", 0), ("nT0", 1)])
            return KT, VA, VB, KV_END

        def phase_B(xsrc, ydst, SEQ, NT, L, Lp, nsb, ntile, nchunk, KT, VA, VB, KV_END, init):
            S.barrier()
            B = Arena(arena_t[:], KV_END)
            NW = 8
            NPT = 3 if NT <= 256 else 2
            NX = 5
            PREFETCH_X = (2 * nsb + 1 <= NX)
            wr = B.bf16(NW * 1024).rearrange("p (s c) -> p s c", s=NW)
            act_bytes = NCH * NT * 2
            sqt_off = (act_bytes + 2047) // 2048 * 2048
            znt_off = 8 * 2048
            zbytes = max(znt_off + 8 * NT * 2, sqt_off + nsb * 512 * 4)
            nz = (zbytes + 2047) // 2048
            Zf = B.f32(nz * 512)
            Zq = Zf[:, 0:8 * 512].bitcast(BF16).rearrange("p (s c) -> p s c", s=8)
            act = Zf[:, 0:act_bytes // 4].bitcast(BF16).rearrange("p (c t) -> p c t", c=NCH)
            sqt = Zf[:, sqt_off // 4:sqt_off // 4 + nsb * 512]

            def zkeys(lo, hi):
                return [("Z", z) for z in range(lo // 2048, (hi - 1) // 2048 + 1)]

            def act_keys(c):
                return zkeys(c * NT * 2, (c + 1) * NT * 2)
            sqt_keys = zkeys(sqt_off, sqt_off + nsb * 512 * 4)
            ZnT = Zf[:, znt_off // 4:znt_off // 4 + 4 * NT].bitcast(BF16).rearrange("p (k t) -> p k t", k=8)
            znt_keys = zkeys(znt_off, znt_off + 8 * NT * 2)
            xr = B.f32(NX * D).rearrange("p (s c) -> p s c", s=NX)
            nT1 = B.bf16(8 * NT).rearrange("p (k t) -> p k t", k=8)
            QTf = B.bf16(16 * NT)
            QT = QTf.rearrange("p (j t) -> p j t", j=16)
            PT = [B.bf16(1024) for _ in range(NPT)]
            mix = B.bf16(nsb * D).rearrange("p (s c) -> p s c", s=nsb)
            oB = B.f32(nsb * 512).rearrange("p (s h e) -> p s h e", s=nsb, h=4)
            tB2 = B.f32(nsb * 128).rearrange("p (s e) -> p s e", s=nsb)
            nb = [B.bf16(D) for _ in range(2)]
            junk = PT[0]
            tabs = B.f32(4 * NT).rearrange("p (a t) -> p a t", a=4)
            T = {"tA": B.f32(NT), "tB": B.f32(NT), "sqb": B.bf16(NT), "rq": B.f32(NT)}
            pt_alias = False
            if NPT == 2 and NT * 4 >= 2048:
                PT.append(T["rq"][:, 0:512].bitcast(BF16))
                NPT = 3
                pt_alias = True

            def ptkeys(i):
                return [("PT", i)] + (["rq"] if (pt_alias and i == 2) else [])
            ff0 = B.off
            GW = NT + 132
            GU = [B.bf16(2 * GW).rearrange("p (a t) -> p a t", a=2) for _ in range(3)]
            cvall = B.f32(2 * NT)
            geall = B.bf16(2 * NT)
            cv = [cvall[:, i * NT:(i + 1) * NT] for i in range(2)]
            ge = [geall[:, i * NT:(i + 1) * NT] for i in range(2)]
            if B.off - ff0 < 4 * D:
                B.f32((4 * D - (B.off - ff0)) // 4 + 8)
            xm = arena_t[:][:, ff0 // 4: ff0 // 4 + D]
            GUcf = B.bf16(NCH * 2 * 130)
            GUc = GUcf.rearrange("p (c a t) -> p c a t", c=NCH, a=2)
            ss = B.f32(8)
            ssB = B.f32(4 * nsb)
            rs = B.f32(8)
            ssF = B.f32(8)
            print("phase B arena end: %d B of %d (NT=%d L=%d)" % (B.off, ARENA_BYTES, NT, L))

            S.op("dve", lambda e: e.memset(QTf, 0.0), writes=[("QT", j) for j in range(16)])
            S.op("dve", lambda e: e.memset(GUcf, 0.0), writes=[("Gc", c) for c in range(NCH)])

            wctr = [0]

            def wload(slot_idx):
                sl = wctr[0] % NW
                wctr[0] += 1
                dma(lambda e, sl=sl, slot_idx=slot_idx: e.dma_start(out=wr[:, sl, :], in_=W[slot_idx, :, :]),
                    reads=["W"], writes=[("w", sl)], key=("w", sl))
                return wr[:, sl, :], ("w", sl)

            def next_nT():
                return nT1, "nT1"

            def xslot(g):
                return g % NX

            def load_x_tile(ti):
                for s in range(nsb):
                    g = ti * nsb + s
                    sl = xslot(g)
                    dma(lambda e, sl=sl, ti=ti, s=s: e.dma_start(out=xr[:, sl, :], in_=xsrc[ti * NT + s * 128: ti * NT + (s + 1) * 128, :]),
                        writes=[("x", sl)], key=("x", sl))

            def attention(ntok, qrows, nsbq):
                hcs = []
                for h in range(8):
                    hcs.append(("A", h, h % 4, 0, (h // 4) * 64, VA, h // 4, 64))
                for hb in range(4):
                    for cp_ in range(2):
                        hcs.append(("B", (hb, cp_), 4 + hb, 1 + hb, cp_ * 64, VB, hb, 128))
                cstride = NT if ntok == NT else ntok
                G = 1024 // cstride
                groups = [[0]]
                real = list(range(1, nchunk))
                for i in range(0, len(real), G):
                    groups.append(real[i:i + G])
                steps = [(hi, gi) for hi in range(len(hcs)) for gi in range(len(groups))]

                def crow(c):
                    return NMETA if c == 0 else 128

                def ccol(c):
                    return 0 if c == 0 else NMETA + (c - 1) * 128

                def sb_base(si):
                    return (si % 2) * 1024

                def acc_base(hi):
                    return 2048 + (hi % 2) * 1024

                def acc_region(hi, dv, sb):
                    w = dv + 1
                    per = 512 // w
                    return acc_base(hi) + (sb // per) * 512 + (sb % per) * w

                def emit_qk(si):
                    hi, gi = steps[si]
                    _, _, qt, kt, r0, _, _, _ = hcs[hi]
                    base = sb_base(si)
                    for li, c in enumerate(groups[gi]):
                        rows = crow(c)
                        o0 = base + li * cstride
                        bk = o0 // 512
                        qi = 2 * qt + r0 // 64
                        S.op("pe", lambda e, rows=rows, o0=o0, kt=kt, c=c, qi=qi: e.matmul(
                            PS[0:rows, o0:o0 + ntok], lhsT=KT[:, kt, ccol(c):ccol(c) + rows],
                            rhs=QT[:, qi, 0:ntok], start=True, stop=True),
                            reads=[("QT", qi)], writes=["ps%d" % bk])

                def emit_exp(si):
                    hi, gi = steps[si]
                    grp = groups[gi]
                    rows = crow(grp[0])
                    base = sb_base(si)
                    pt = PT[si % NPT]
                    n = len(grp)
                    src = PS[0:rows, base:base + n * cstride].rearrange("p (g t) -> p g t", g=n)[:, :, 0:ntok]
                    dst = pt[0:rows, 0:n * cstride].rearrange("p (g t) -> p g t", g=n)[:, :, 0:ntok]
                    bks = sorted(set((base + li * cstride) // 512 for li in range(n)))
                    S.op("act", lambda e, src=src, dst=dst: e.activation(out=dst, in_=src, func=AF.Exp, scale=0.125),
                         writes=["ps%d" % b for b in bks] + ptkeys(si % NPT))

                def emit_pv(si):
                    hi, gi = steps[si]
                    kind, _, _, _, _, Vs, vh, dv = hcs[hi]
                    pt = PT[si % NPT]
                    for li, c in enumerate(groups[gi]):
                        rows = crow(c)
                        for sb in range(nsbq):
                            o0 = acc_region(hi, dv, sb)
                            bk = o0 // 512
                            first_in_bank = (gi == 0 and li == 0 and (o0 % 512) == 0)
                            last = (gi == len(groups) - 1 and li == len(groups[gi]) - 1)
                            S.op("pe", lambda e, rows=rows, o0=o0, li=li, sb=sb, c=c, dv=dv, vh=vh, Vs=Vs, pt=pt,
                                 fb=first_in_bank, last=last: e.matmul(
                                PS[0:qrows, o0:o0 + dv + 1], lhsT=pt[0:rows, li * cstride + sb * 128: li * cstride + sb * 128 + qrows],
                                rhs=Vs[0:rows, c, vh, 0:dv + 1], start=fb, stop=last, skip_group_check=True),
                                reads=ptkeys(si % NPT), writes=["ps%d" % bk])

                def emit_evac(hi):
                    kind, hid, _, _, _, _, _, dv = hcs[hi]
                    w = dv + 1
                    per = 512 // w
                    segs = []
                    sb = 0
                    while sb < nsbq:
                        cnt = min(per - (sb % per), nsbq - sb)
                        segs.append((sb, cnt))
                        sb += cnt
                    for (sb0, cnt) in segs:
                        o0 = acc_region(hi, dv, sb0)
                        bk = "ps%d" % (o0 // 512)
                        v = PS[0:qrows, o0:o0 + cnt * w].rearrange("p (s e) -> p s e", s=cnt)
                        rsv = rs[0:qrows, sb0:sb0 + cnt]
                        S.op("dve", lambda e, v=v, rsv=rsv, dv=dv: e.reciprocal(out=rsv, in_=v[:, :, dv]),
                             writes=[bk, "rs"])
                        if kind == "A":
                            h = hid
                            dst = mix[0:qrows, sb0:sb0 + cnt, h * 64:(h + 1) * 64]
                            S.op("dve", lambda e, v=v, rsv=rsv, dst=dst, cnt=cnt: e.tensor_tensor(
                                out=dst, in0=v[:, :, 0:64], in1=rsv[:, :, None].broadcast_to([qrows, cnt, 64]), op=ALU.mult),
                                reads=["rs"], writes=[bk, ("mix", h // 2)])
                        else:
                            hb, cp_ = hid
                            dst = oB[0:qrows, sb0:sb0 + cnt, hb, :]
                            if cp_ == 0:
                                S.op("dve", lambda e, v=v, rsv=rsv, dst=dst, cnt=cnt: e.tensor_tensor(
                                    out=dst, in0=v[:, :, 0:128], in1=rsv[:, :, None].broadcast_to([qrows, cnt, 128]), op=ALU.mult),
                                    reads=["rs"], writes=[bk, ("oB", hb)])
                            else:
                                S.op("dve", lambda e, rsv=rsv: e.tensor_scalar(out=rsv, in0=rsv, scalar1=neglam[0:qrows, :],
                                                                               scalar2=None, op0=ALU.mult),
                                     reads=["lams"], writes=["rs"])
                                t2 = tB2[0:qrows, sb0:sb0 + cnt, :]
                                S.op("dve", lambda e, v=v, rsv=rsv, t2=t2, cnt=cnt: e.tensor_tensor(
                                    out=t2, in0=v[:, :, 0:128], in1=rsv[:, :, None].broadcast_to([qrows, cnt, 128]), op=ALU.mult),
                                    reads=["rs"], writes=[bk, "tB2"])
                                S.op("pool", lambda e, dst=dst, t2=t2: e.tensor_tensor(out=dst, in0=dst, in1=t2, op=ALU.add),
                                     reads=["tB2"], writes=[("oB", hb)])

                ng = len(groups)
                emit_qk(0)
                for si in range(len(steps)):
                    if si + 1 < len(steps):
                        emit_qk(si + 1)
                    emit_exp(si)
                    emit_pv(si)
                    if steps[si][1] == ng - 1:
                        emit_evac(steps[si][0])
                n4 = nsbq * 4
                oBf = oB[0:qrows, 0:nsbq, :, :].rearrange("p s h e -> p (s h) e")
                sqv = sqt[0:qrows, 0:n4 * 128].rearrange("p (a e) -> p a e", a=n4)
                S.op("dve", lambda e: e.tensor_tensor(out=sqv, in0=oBf, in1=oBf, op=ALU.mult),
                     reads=[("oB", h) for h in range(4)], writes=sqt_keys)
                S.op("dve", lambda e: e.tensor_reduce(out=ssB[0:qrows, 0:n4], in_=sqv, axis=AX.X, op=ALU.add),
                     reads=sqt_keys, writes=["ssB"])
                rstd_from_ss(ssB[0:qrows, 0:n4], n4, 1.0 / 128, "ssB", qrows)
                S.op("dve", lambda e: e.tensor_tensor(out=oBf, in0=oBf, in1=ssB[0:qrows, 0:n4, None].broadcast_to([qrows, n4, 128]),
                                                      op=ALU.mult), reads=["ssB"], writes=[("oB", h) for h in range(4)])
                for sb in range(nsbq):
                    S.op("dve", lambda e, sb=sb: e.scalar_tensor_tensor(
                        out=mix[0:qrows, sb, 512:1024].rearrange("p (h e) -> p h e", h=4), in0=oB[0:qrows, sb, :, :],
                        scalar=1.0 - LAM_INIT, in1=gsub[0:qrows, None, :].broadcast_to([qrows, 4, 128]),
                        op0=ALU.mult, op1=ALU.mult), reads=[("oB", h) for h in range(4)] + ["gsub"],
                        writes=[("mix", 4 + h) for h in range(4)])

            gctr = [0]

            def ffn(mode, n2T, n2keys, ncur, wsbs, gbase, ybase):
                if mode != "flush":
                    pend = None
                    for c in range(NCH):
                        wg, wgk = wload(SL_G + c)
                        if mode == "real":
                            wu, wuk = wload(SL_U + c)
                        pG = next_bank()
                        for k in range(8):
                            S.op("pe", lambda e, k=k, wg=wg, pG=pG: e.matmul(
                                bank(pG)[:, 0:ncur], lhsT=wg[:, k * 128:(k + 1) * 128], rhs=n2T[:, k, 0:ncur],
                                start=(k == 0), stop=(k == 7)), reads=[wgk] + n2keys, writes=["ps%d" % pG])
                        if mode == "meta":
                            S.op("dve", lambda e, c=c, pG=pG: e.tensor_copy(out=GUc[:, c, 0, 128:129], in_=bank(pG)[:, NMETA - 1:NMETA]),
                                 writes=["ps%d" % pG, ("Gc", c)])
                            continue
                        pU = next_bank()
                        for k in range(8):
                            S.op("pe", lambda e, k=k, wu=wu, pU=pU: e.matmul(
                                bank(pU)[:, 0:ncur], lhsT=wu[:, k * 128:(k + 1) * 128], rhs=n2T[:, k, 0:ncur],
                                start=(k == 0), stop=(k == 7)), reads=[wuk] + n2keys, writes=["ps%d" % pU])
                        st = ffn_chunk_a(c, pG, pU, NT)
                        if pend is not None:
                            ffn_chunk_b(*pend)
                        pend = st
                    if pend is not None:
                        ffn_chunk_b(*pend)
                else:
                    gs = 2 * NT // 128
                    for c0 in range(0, NCH, gs):
                        n = min(gs, NCH - c0)
                        Gv = GUc[:, c0:c0 + n, 0, :]
                        Uv = GUc[:, c0:c0 + n, 1, 1:129]
                        t = cvall[:, 0:n * 128].rearrange("p (c t) -> p c t", c=n)
                        u = sqt[:, 0:n * 128].rearrange("p (c t) -> p c t", c=n)
                        gv = geall[:, 0:n * 128].rearrange("p (c t) -> p c t", c=n)
                        gck = [("Gc", c) for c in range(c0, c0 + n)]

                        def wb(base, c0=c0, n=n):
                            return cols[:, base + c0:base + c0 + n, None].broadcast_to([128, n, 128])
                        S.op("dve", lambda e, Gv=Gv, t=t, wb=wb: e.tensor_tensor(out=t, in0=Gv[:, :, 0:128], in1=wb(0), op=ALU.mult),
                             reads=gck + ["cols"], writes=["cv0", "cv1"])
                        for tap in (1, 2):
                            S.op("dve", lambda e, Gv=Gv, u=u, wb=wb, tap=tap: e.tensor_tensor(
                                out=u, in0=Gv[:, :, tap:tap + 128], in1=wb(22 * tap), op=ALU.mult), reads=gck + ["cols"], writes=sqt_keys)
                            S.op("dve", lambda e, t=t, u=u: e.tensor_tensor(out=t, in0=t, in1=u, op=ALU.add),
                                 reads=sqt_keys, writes=["cv0", "cv1"])
                        S.op("dve", lambda e, t=t, wb=wb: e.tensor_tensor(out=t, in0=t, in1=wb(66), op=ALU.add),
                             reads=["cols"], writes=["cv0", "cv1"])
                        S.op("act", lambda e, t=t, gv=gv: e.activation(out=gv, in_=t, func=AF.Gelu), reads=["cv0", "cv1"],
                             writes=["ge0", "ge1"])
                        ak = []
                        for c in range(c0, c0 + n):
                            ak += act_keys(c)
                        S.op("pool", lambda e, gv=gv, Uv=Uv, c0=c0, n=n: e.tensor_tensor(out=act[:, c0:c0 + n, 0:128], in0=gv, in1=Uv,
                                                                                       op=ALU.mult),
                             reads=["ge0", "ge1"] + gck, writes=sorted(set(ak)))
                if mode == "meta":
                    return
                banks = {}
                for w in wsbs:
                    for half in range(2):
                        banks[(w, half)] = next_bank()
                for c in range(NCH):
                    wd, wdk = wload(SL_D + c)
                    for w in wsbs:
                        for half in range(2):
                            pb = banks[(w, half)]
                            S.op("pe", lambda e, c=c, w=w, half=half, pb=pb, wd=wd: e.matmul(
                                bank(pb)[:, 0:512], lhsT=act[:, c, w * 128:(w + 1) * 128], rhs=wd[:, half * 512:(half + 1) * 512],
                                start=(c == 0), stop=(c == NCH - 1)), reads=[wdk] + act_keys(c), writes=["ps%d" % pb])
                fin = []
                for wi, w in enumerate(wsbs):
                    sl = xslot(gbase + w)
                    xa = xr[:, sl, :]
                    xk = ("x", sl)
                    fin.append((wi, w, sl, xa, xk))
                    for half in range(2):
                        pb = banks[(w, half)]
                        S.op("dve", lambda e, xa=xa, half=half, pb=pb: e.tensor_tensor(
                            out=xa[:, half * 512:(half + 1) * 512], in0=bank(pb)[:, 0:512], in1=xa[:, half * 512:(half + 1) * 512],
                            op=ALU.add), writes=["ps%d" % pb, xk])
                for (wi, w, sl, xa, xk) in fin:
                    S.op("act", lambda e, xa=xa, wi=wi: e.activation(out=junk, in_=xa, func=AF.Square, accum_out=ssF[:, wi:wi + 1]),
                         reads=[xk], writes=["junk", ("ssF", wi)])
                nw = len(fin)
                if nw:
                    sk = [("ssF", wi) for wi in range(nw)]
                    S.op("act", lambda e: e.activation(out=ssF[:, 0:nw], in_=ssF[:, 0:nw], func=AF.Ln, scale=1.0 / D, bias=epsc),
                         reads=["epsc"], writes=sk)
                    S.op("act", lambda e: e.activation(out=ssF[:, 0:nw], in_=ssF[:, 0:nw], func=AF.Exp, scale=-0.5), writes=sk)
                for (wi, w, sl, xa, xk) in fin:
                    S.op("dve", lambda e, xa=xa, wi=wi: e.scalar_tensor_tensor(
                        out=xa, in0=xa, scalar=ssF[:, wi:wi + 1], in1=gfin, op0=ALU.mult, op1=ALU.mult),
                        reads=[("ssF", wi), "gfin"], writes=[xk])
                    r0 = ybase + w * 128
                    dma(lambda e, xa=xa, r0=r0: e.dma_start(out=ydst[r0:r0 + 128, :], in_=xa), reads=[xk], key=("y", sl))

            def ffn_chunk_a(c, pG, pU, nwin):
                i3 = gctr[0] % 3
                i2 = gctr[0] % 2
                gctr[0] += 1
                gu, cvb, geb = GU[i3], cv[i2], ge[i2]
                kc, kg, ku, ck, ek = ("gu", i3, "c"), ("gu", i3, "g"), ("gu", i3, "u"), "cv%d" % i2, "ge%d" % i2
                S.op("pool", lambda e: e.tensor_copy(out=gu[:, :, 0:129], in_=GUc[:, c, :, 0:129]), reads=[("Gc", c)], writes=[kc])
                if pG is not None:
                    S.op("act", lambda e: e.activation(out=gu[:, 0, 129:129 + NT], in_=bank(pG)[:, 0:NT], func=AF.Copy),
                         writes=["ps%d" % pG, kg])
                    S.op("dve", lambda e: e.tensor_copy(out=gu[:, 1, 129:129 + NT], in_=bank(pU)[:, 0:NT]),
                         writes=["ps%d" % pU, ku])
                    S.op("pool", lambda e: e.tensor_copy(out=GUc[:, c, :, 0:129], in_=gu[:, :, NT:NT + 129]),
                         reads=[kc, kg, ku], writes=[("Gc", c)])
                else:
                    S.op("pool", lambda e: e.memset(gu[:, 0, 129:131], 0.0), writes=[kg])
                return (c, nwin, gu, cvb, geb, kc, kg, ku, ck, ek)

            def ffn_chunk_b(c, nwin, gu, cvb, geb, kc, kg, ku, ck, ek):
                w0, w1, w2, bb = col(c), col(22 + c), col(44 + c), col(66 + c)
                Gb = gu[:, 0, :]
                S.op("dve", lambda e: e.tensor_scalar(out=cvb[:, 0:nwin], in0=Gb[:, 0:nwin], scalar1=w0, scalar2=bb,
                                                      op0=ALU.mult, op1=ALU.add), reads=[kc, kg, "cols"], writes=[ck])
                S.op("dve", lambda e: e.scalar_tensor_tensor(out=cvb[:, 0:nwin], in0=Gb[:, 1:nwin + 1], scalar=w1, in1=cvb[:, 0:nwin],
                                                             op0=ALU.mult, op1=ALU.add), reads=[kc, kg, "cols"], writes=[ck])
                S.op("dve", lambda e: e.scalar_tensor_tensor(out=cvb[:, 0:nwin], in0=Gb[:, 2:nwin + 2], scalar=w2, in1=cvb[:, 0:nwin],
                                                             op0=ALU.mult, op1=ALU.add), reads=[kc, kg, "cols"], writes=[ck])
                S.op("act", lambda e: e.activation(out=geb[:, 0:nwin], in_=cvb[:, 0:nwin], func=AF.Gelu), reads=[ck], writes=[ek])
                S.op("pool", lambda e: e.tensor_tensor(out=act[:, c, 0:nwin], in0=geb[:, 0:nwin], in1=gu[:, 1, 1:nwin + 1], op=ALU.mult),
                     reads=[ek, kc, ku], writes=act_keys(c))

            def mixer_and_norm(kind, ti, srcs, ntok, qrows, nsbq, col0):
                nTb = ZnT
                norm_T(srcs, nTb, (lambda i: znt_keys), NT, ss, "ss", nb, junk, "act")
                nTkeys = znt_keys
                dma(lambda e: e.dma_start(out=tabs[:, :, 0:ntok], in_=tabs_d[:, :, col0:col0 + ntok].rearrange("a p n -> p a n")),
                    writes=["tabs"], key="tabs")
                wq = [wload(SL_Q + k) for k in range(8)]
                Pb = [Zf[:, i * 512:(i + 1) * 512].bitcast(BF16)[:, 0:NT] for i in range(2)]

                def q_finish(j, pP):
                    pR = next_bank()
                    pb, pbk = Pb[j % 2], ("Z", j % 2)
                    S.op("pe", lambda e: e.matmul(bank(pR)[:, 0:ntok], lhsT=rmat, rhs=pb[:, 0:ntok], start=True, stop=True),
                         reads=[pbk, "rmat"], writes=["ps%d" % pR])
                    qk_post("A" if j < 4 else "B", pP, pR, ntok, tabs,
                            [(QT[0:64, 2 * j, 0:ntok], 0, 64, ("QT", 2 * j)), (QT[64:128, 2 * j + 1, 0:ntok], 64, 128, ("QT", 2 * j + 1))],
                            104, 105, T)
                prev = None
                for j in range(8):
                    pP = next_bank()
                    proj_fm([w[0] for w in wq], [w[1] for w in wq], j * 128, nTb, nTkeys, ntok, pP)
                    S.op("act", lambda e, j=j, pP=pP: e.activation(out=Pb[j % 2][:, 0:ntok], in_=bank(pP)[:, 0:ntok], func=AF.Copy),
                         writes=["ps%d" % pP, ("Z", j % 2)])
                    if prev is not None:
                        q_finish(*prev)
                    prev = (j, pP)
                q_finish(*prev)
                if kind == "real" and ti == 0:
                    dump("QT", QT, [("QT", j_) for j_ in range(16)])
                attention(ntok, qrows, nsbq)
                if kind == "real" and ti == 0:
                    dump("mix", mix, [("mix", j_) for j_ in range(8)])
                mTb, mTk = next_nT()
                for sb in range(nsbq):
                    pb = next_bank()
                    psb = bank(pb).bitcast(BF16)
                    for j in range(8):
                        S.op("pe", lambda e, j=j, sb=sb, psb=psb: e.transpose(
                            out=psb[:, j * 128:j * 128 + qrows], in_=mix[0:qrows, sb, j * 128:(j + 1) * 128],
                            identity=identb[0:qrows, 0:qrows]), reads=[("mix", j), "identb"], writes=["ps%d" % pb])
                    S.op("dve", lambda e, sb=sb, psb=psb: e.tensor_copy(
                        out=mTb[:, :, sb * 128:sb * 128 + qrows], in_=psb.rearrange("p (j t) -> p j t", j=8)[:, :, 0:qrows]),
                        writes=["ps%d" % pb, (mTk, sb)])
                wo = [wload(SL_O + j) for j in range(8)]
                for sb, (xa, xk, rows) in enumerate(srcs):
                    for half in range(2):
                        pb = next_bank()
                        for j in range(8):
                            S.op("pe", lambda e, j=j, sb=sb, half=half, pb=pb, rows=rows: e.matmul(
                                bank(pb)[0:rows, 0:512], lhsT=mTb[:, j, sb * 128:sb * 128 + rows],
                                rhs=wo[j][0][:, half * 512:(half + 1) * 512], start=(j == 0), stop=(j == 7)),
                                reads=[wo[j][1], (mTk, sb)], writes=["ps%d" % pb])
                        S.op("dve", lambda e, xa=xa, half=half, pb=pb, rows=rows: e.tensor_tensor(
                            out=xa[:, half * 512:(half + 1) * 512], in0=bank(pb)[0:rows, 0:512],
                            in1=xa[:, half * 512:(half + 1) * 512], op=ALU.add), writes=["ps%d" % pb, xk])
                if kind == "real" and ti == 0:
                    dump("h1", srcs[0][0], [srcs[0][1]])
                n2b, n2k = next_nT()
                norm_T(srcs, n2b, (lambda i, n2k=n2k: [(n2k, i)]), NT, ss, "ss", nb, junk, "act")
                if kind == "real" and ti == 0:
                    dump("n2T", n2b, [(n2k, i_) for i_ in range(len(srcs))])
                return n2b, [(n2k, i) for i in range(len(srcs))]

            dma(lambda e: e.dma_start(out=xm[0:NMETA, :], in_=meta_d[:, :]), writes=["xm"], key="xm")
            if PREFETCH_X:
                load_x_tile(0)
            n2b, n2keys = mixer_and_norm("meta", 0, [(xm[0:NMETA, :], "xm", NMETA)], NMETA, NMETA, 1, 0)
            ffn("meta", n2b, n2keys, NMETA, [], 0, 0)
            S.barrier()
            for ti in range(ntile):
                if PREFETCH_X:
                    if ti + 1 < ntile:
                        load_x_tile(ti + 1)
                else:
                    load_x_tile(ti)
                srcs = [(xr[:, xslot(ti * nsb + s), :], ("x", xslot(ti * nsb + s)), 128) for s in range(nsb)]
                n2b, n2keys = mixer_and_norm("real", ti, srcs, NT, 128, nsb, NMETA + ti * NT)
                wsbs = list(range(nsb)) if ti > 0 else list(range(1, nsb))
                ffn("real", n2b, n2keys, NT, wsbs, ti * nsb - 1, ti * NT - 128)
            ffn("flush", None, None, 0, [0], ntile * nsb - 1, SEQ - 128)

        seen = set()
        for sidx, (kind, i, NT) in enumerate(seq_cfg):
            init = (kind, NT) not in seen
            seen.add((kind, NT))
            if kind == "p":
                do_sequence(sidx, xp, yp, SEQ_P, NT, init)
            else:
                do_sequence(sidx, xs[i], ys[i], SEQ_S, NT, init)

        S.emit(nc, st)
    return nc


def _rope_tables():
    f32 = np.float32
    theta = f32(10000.0)
    t = np.arange(SEQ_P)
    row = (t // 64).astype(f32)
    colp = (t % 64).astype(f32)
    inv16 = (theta ** (-(np.arange(0, 32, 2, dtype=f32)) / f32(32))).astype(f32)
    ang = np.concatenate([row[:, None] * inv16[None], colp[:, None] * inv16[None]], axis=-1).astype(f32)
    ang_a = np.concatenate([np.zeros((NMETA, 32), f32), ang], axis=0)
    pos = np.arange(LMAX, dtype=f32)
    inv32 = (theta ** (-(np.arange(0, 64, 2, dtype=f32)) / f32(64))).astype(f32)
    ang_b = (pos[:, None] * inv32[None]).astype(f32)
    tabs = np.empty((4, 128, LMAX), f32)
    for a, src in enumerate((np.cos(ang_a), np.sin(ang_a), np.cos(ang_b), np.sin(ang_b))):
        tabs[a] = np.tile(src.T.astype(f32), (4, 1))
    return tabs


def _rot_matrix():
    r = np.zeros((128, 128), np.float32)
    for m in range(128):
        h, i = divmod(m, 64)
        if i < 32:
            r[h * 64 + i + 32, m] = -1.0
        else:
            r[h * 64 + i - 32, m] = 1.0
    return r


_CACHE = {}
DBG_ON = False


def kernel(x_prompt, x_sample, meta_tokens, g_mix, w_in, g_qnorm_a, g_knorm_a, lambda_q1, lambda_k1,
           lambda_q2, lambda_k2, g_subln, w_out, g_ffn, w_ff_gate, w_ff_up, conv_w, conv_b, w_ff_down, g_final):
    seq_cfg = [("p", 0, 256)] + [("s", i, 512) for i in range(NSAMP)]
    if "nc" not in _CACHE:
        _CACHE["nc"] = build_program(seq_cfg)
    nc = _CACHE["nc"]
    f = _f
    shared = make_shared(meta_tokens, g_mix, w_in, g_qnorm_a, g_knorm_a, lambda_q1, lambda_k1, lambda_q2, lambda_k2,
                         g_subln, w_out, g_ffn, w_ff_gate, w_ff_up, conv_w, conv_b, w_ff_down, g_final)
    xpf, xsf = f(x_prompt), f(x_sample)
    in_maps = []
    for c in range(8):
        m = dict(shared)
        m["xp"] = xpf[c]
        m["xs"] = xsf[c * NSAMP:(c + 1) * NSAMP]
        in_maps.append(m)
    res = run_bass_kernel_spmd(nc, in_maps, core_ids=list(range(8)))
    y_prompt = np.stack([np.asarray(res.results[c]["yp"], dtype=np.float32) for c in range(8)], axis=0)
    y_sample = np.concatenate([np.asarray(res.results[c]["ys"], dtype=np.float32) for c in range(8)], axis=0)
    return (y_prompt, y_sample)


def _f(a):
    return np.ascontiguousarray(np.asarray(a, dtype=np.float32))


def make_shared(meta_tokens, g_mix, w_in, g_qnorm_a, g_knorm_a, lambda_q1, lambda_k1, lambda_q2, lambda_k2,
                g_subln, w_out, g_ffn, w_ff_gate, w_ff_up, conv_w, conv_b, w_ff_down, g_final):
    f = _f
    if "tabs" not in _CACHE:
        _CACHE["tabs"] = _rope_tables()
    onesbd = np.zeros((128, 128), np.float32)
    onesbd[:64, :64] = 1.0
    onesbd[64:, 64:] = 1.0
    return {
        "meta": f(meta_tokens), "w_in": f(w_in[0]), "w_out": f(w_out[0]), "w_gate": f(w_ff_gate[0]), "w_up": f(w_ff_up[0]),
        "w_down": f(w_ff_down[0]), "g_mix": f(g_mix[0]).reshape(8, 128), "g_ffn": f(g_ffn[0]).reshape(8, 128),
        "gq": f(g_qnorm_a[0]).reshape(1, 64), "gk": f(g_knorm_a[0]).reshape(1, 64),
        "lam": np.stack([f(lambda_q1[0]), f(lambda_k1[0]), f(lambda_q2[0]), f(lambda_k2[0])], axis=0),
        "gsub": f(g_subln[0]).reshape(1, 128), "convw": f(conv_w[0]).reshape(66, 128), "convb": f(conv_b[0]).reshape(22, 128),
        "gfin": f(g_final).reshape(1, D), "tabs": _CACHE["tabs"], "identf": np.eye(128, dtype=np.float32), "onesbd": onesbd,
        "rmat": _rot_matrix(),
    }
```

```python
import math
from contextlib import ExitStack
import numpy as np
import concourse.bass as bass
import concourse.mybir as mybir
from concourse.bass_utils import run_bass_kernel_spmd

F32 = mybir.dt.float32
BF16 = mybir.dt.bfloat16
ALU = mybir.AluOpType
AF = mybir.ActivationFunctionType
AX = mybir.AxisListType

D = 1024
NMETA = 16
DFF = 2816
NCH = 22
EPS = 1e-6
LAM_INIT = 0.8 - 0.6 * math.exp(0.0)
SEQ_P = 4096
SEQ_S = 2048
NSAMP = 4
LMAX = SEQ_P + NMETA
ARENA_BYTES = 206 * 1024
NSLOT = 106
SL_A0, SL_A1, SL_Q, SL_QR, SL_O, SL_G, SL_U, SL_D = 0, 8, 16, 24, 32, 40, 62, 84

ENGS = ("pe", "act", "dve", "pool", "sp")


class Op:
    __slots__ = ("eng", "fn", "deps", "signal", "count", "dma_key", "dma_n", "is_dma")

    def __init__(self, eng, fn, is_dma, dma_key, dma_n):
        self.eng = eng
        self.fn = fn
        self.deps = []
        self.signal = False
        self.count = 0
        self.is_dma = is_dma
        self.dma_key = dma_key
        self.dma_n = dma_n


class Sched:
    def __init__(self):
        self.ops = {e: [] for e in ENGS}
        self.last_writer = {}
        self.readers = {}
        self.bar = []
        self.bar_pending = {e: False for e in ENGS}
        self.last_dma = {}

    def op(self, eng, fn, reads=(), writes=(), dma_key=None, dma_n=1):
        is_dma = dma_key is not None
        o = Op(eng, fn, is_dma, dma_key, dma_n)
        deps = {}
        lw = self.last_writer
        for b in reads:
            w = lw.get(b)
            if w is not None:
                deps[id(w)] = w
        for b in writes:
            w = lw.get(b)
            if w is not None:
                deps[id(w)] = w
            rs = self.readers.get(b)
            if rs:
                for r in rs:
                    deps[id(r)] = r
        if self.bar_pending[eng]:
            self.bar_pending[eng] = False
            for d in self.bar:
                deps[id(d)] = d
        for d in deps.values():
            if (not d.is_dma) and d.eng == "pe" and eng == "pe" and not is_dma:
                continue
            d.signal = True
            o.deps.append(d)
        for b in writes:
            lw[b] = o
            self.readers[b] = []
        for b in reads:
            rs = self.readers.setdefault(b, [])
            if not is_dma:
                for i, r in enumerate(rs):
                    if (not r.is_dma) and r.eng == eng:
                        rs[i] = o
                        break
                else:
                    rs.append(o)
            else:
                rs.append(o)
        self.ops[eng].append(o)
        if is_dma:
            self.last_dma[dma_key] = o
        return o

    def barrier(self):
        bar = []
        for e in ENGS:
            for o in reversed(self.ops[e]):
                if not o.is_dma:
                    bar.append(o)
                    break
        bar.extend(self.last_dma.values())
        self.bar = bar
        self.bar_pending = {e: True for e in ENGS}
        self.last_writer = {}
        self.readers = {}

    def emit(self, nc, stack):
        dma_cnt = {}
        for e in ENGS:
            c = 0
            for o in self.ops[e]:
                if o.is_dma:
                    dma_cnt[o.dma_key] = dma_cnt.get(o.dma_key, 0) + 16 * o.dma_n
                    o.count = dma_cnt[o.dma_key]
                elif o.signal:
                    c += 1
                    o.count = c
        esem = {e: stack.enter_context(nc.semaphore("s_" + e)) for e in ENGS if e != "sp"}
        dsem = {k: stack.enter_context(nc.semaphore("d_%d" % i)) for i, k in enumerate(dma_cnt)}
        block = stack.enter_context(nc.Block())

        def run(e):
            def body(eng):
                known = {}
                for o in self.ops[e]:
                    need = {}
                    for d in o.deps:
                        key = ("d", d.dma_key) if d.is_dma else ("e", d.eng)
                        if d.count > need.get(key, 0):
                            need[key] = d.count
                    for key, v in need.items():
                        if known.get(key, 0) >= v:
                            continue
                        known[key] = v
                        sem = dsem[key[1]] if key[0] == "d" else esem[key[1]]
                        eng.wait_ge(sem, v)
                    r = o.fn(eng)
                    if o.is_dma:
                        if not isinstance(r, (list, tuple)):
                            r = [r]
                        assert len(r) == o.dma_n
                        for ins in r:
                            ins.then_inc(dsem[o.dma_key], 16)
                    elif o.signal:
                        r.then_inc(esem[e], 1)
                last = {}
                for o in self.ops[e]:
                    if o.is_dma:
                        last[o.dma_key] = o.count
                for k, v in last.items():
                    if known.get(("d", k), 0) < v:
                        eng.wait_ge(dsem[k], v)
            return body

        block.tensor(run("pe"))
        block.scalar(run("act"))
        block.vector(run("dve"))
        block.gpsimd(run("pool"))
        block.sync(run("sp"))


class Arena:
    def __init__(self, ap, base=0):
        self.ap = ap
        self.off = base

    def f32(self, n):
        self.off = (self.off + 31) // 32 * 32
        a = self.ap[:, self.off // 4:self.off // 4 + n]
        self.off += 4 * n
        assert self.off <= ARENA_BYTES, self.off
        return a

    def bf16(self, n):
        n2 = (n + 1) // 2
        return self.f32(n2).bitcast(BF16)[:, 0:n]


def build_program(seq_cfg):
    nc = bass.Bass("TRN2", target_bir_lowering=False)

    def din(name, shape):
        return nc.dram_tensor(name, list(shape), F32, kind="ExternalInput").ap()

    xp = din("xp", [SEQ_P, D])
    xs = din("xs", [NSAMP, SEQ_S, D])
    meta_d = din("meta", [NMETA, D])
    w_in_d = din("w_in", [D, 2304])
    w_out_d = din("w_out", [D, D])
    w_gate_d = din("w_gate", [D, DFF])
    w_up_d = din("w_up", [D, DFF])
    w_down_d = din("w_down", [DFF, D])
    g_mix_d = din("g_mix", [8, 128])
    g_ffn_d = din("g_ffn", [8, 128])
    gq_d = din("gq", [1, 64])
    gk_d = din("gk", [1, 64])
    lam_d = din("lam", [4, 64])
    gsub_d = din("gsub", [1, 128])
    convw_d = din("convw", [66, 128])
    convb_d = din("convb", [22, 128])
    gfin_d = din("gfin", [1, D])
    tabs_d = din("tabs", [4, 128, LMAX])
    identf_d = din("identf", [128, 128])
    onesbd_d = din("onesbd", [128, 128])
    rmat_d = din("rmat", [128, 128])
    yp = nc.dram_tensor("yp", [SEQ_P, D], F32, kind="ExternalOutput").ap()
    ys = nc.dram_tensor("ys", [NSAMP, SEQ_S, D], F32, kind="ExternalOutput").ap()
    W = nc.dram_tensor("wscr", [NSLOT, 128, 1024], BF16, kind="Internal").ap()

    S = Sched()
    dcnt = [0]

    def dump(name, ap, keys):
        if not DBG_ON:
            return
        dcnt[0] += 1
        t = nc.dram_tensor("dbg_" + name, list(ap.shape), ap.dtype, kind="ExternalOutput").ap()
        S.op("sp", lambda e: e.dma_start(out=t, in_=ap), reads=keys, dma_key=("dbg", dcnt[0]))
    with ExitStack() as st:
        arena_t = st.enter_context(nc.sbuf_tensor("arena", [128, ARENA_BYTES // 4], F32))
        psum_t = st.enter_context(nc.psum_tensor("psum", [128, 4096], F32))
        PS = psum_t[:]
        AR = Arena(arena_t[:])

        def bank(b):
            return PS[:, b * 512:(b + 1) * 512]

        bank_ctr = [0]

        def next_bank():
            b = bank_ctr[0] % 8
            bank_ctr[0] += 1
            return b

        identb = AR.bf16(128)
        onesbd = AR.bf16(128)
        rmat = AR.bf16(128)
        cols = AR.f32(128)
        ncols = AR.f32(128)
        gfin = AR.f32(D)
        gsub = AR.f32(128)
        lams = AR.f32(8)
        epsc = AR.f32(1)
        CONST_END = AR.off
        identf = AR.f32(128)
        onesf = AR.f32(128)
        lamt = AR.f32(256)
        stg = AR.f32(128)
        SETUP_END = AR.off

        def dma(fn, reads=(), writes=(), key=None, n=1):
            S.op("sp", fn, reads=reads, writes=writes, dma_key=key, dma_n=n)

        dma(lambda e: e.dma_start(out=identf, in_=identf_d[:, :]), writes=["identf"], key="c_identf")
        dma(lambda e: e.dma_start(out=onesf, in_=onesbd_d[:, :]), writes=["onesf"], key="c_onesf")
        dma(lambda e: e.dma_start(out=gfin, in_=gfin_d[0:1, :].partition_broadcast(128)), writes=["gfin"], key="c_gfin")
        dma(lambda e: e.dma_start(out=gsub, in_=gsub_d[0:1, :].partition_broadcast(128)), writes=["gsub"], key="c_gsub")
        dma(lambda e: [e.dma_start(out=lamt[:, i * 64:(i + 1) * 64], in_=lam_d[i:i + 1, :].partition_broadcast(128))
                       for i in range(4)], writes=["lamt"], key="c_lamt", n=4)
        S.op("pool", lambda e: e.memset(stg, 0.0), writes=["stg"])
        S.op("pool", lambda e: e.memset(epsc, EPS), writes=["epsc"])

        def stg_loads(e):
            r = []
            r.append(e.dma_start(out=stg[0:66, :], in_=convw_d[:, :]))
            r.append(e.dma_start(out=stg[66:88, :], in_=convb_d[:, :]))
            r.append(e.dma_start(out=stg[88:96, :], in_=g_mix_d[:, :]))
            r.append(e.dma_start(out=stg[96:104, :], in_=g_ffn_d[:, :]))
            for row, src in ((104, gq_d), (106, gk_d)):
                for h in range(2):
                    r.append(e.dma_start(out=stg[row:row + 1, h * 64:(h + 1) * 64], in_=src[0:1, :]))
                    r.append(e.dma_start(out=stg[row + 1:row + 2, h * 64:h * 64 + 32], in_=src[0:1, 32:64]))
                    r.append(e.dma_start(out=stg[row + 1:row + 2, h * 64 + 32:h * 64 + 64], in_=src[0:1, 0:32]))
            return r
        dma(stg_loads, reads=[], writes=["stg"], key="c_stg", n=16)
        S.op("dve", lambda e: e.tensor_copy(out=identb, in_=identf), reads=["identf"], writes=["identb"])
        S.op("dve", lambda e: e.tensor_copy(out=onesbd, in_=onesf), reads=["onesf"], writes=["onesbd"])
        dma(lambda e: e.dma_start(out=onesf, in_=rmat_d[:, :]), reads=["onesbd"], writes=["onesf"], key="c_onesf")
        S.op("dve", lambda e: e.tensor_copy(out=rmat, in_=onesf), reads=["onesf"], writes=["rmat"])
        S.op("pe", lambda e: e.transpose(out=bank(0)[:, 0:128], in_=stg, identity=identf), reads=["stg", "identf"], writes=["ps0"])
        S.op("dve", lambda e: e.tensor_copy(out=cols, in_=bank(0)[:, 0:128]), writes=["ps0", "cols"])
        S.op("dve", lambda e: e.scalar_tensor_tensor(out=lamt[:, 0:64], in0=lamt[:, 0:64], scalar=1.0, in1=lamt[:, 64:128],
                                                      op0=ALU.mult, op1=ALU.mult, accum_out=lams[:, 0:1]),
             reads=["lamt"], writes=["lamt", "lams"])
        S.op("dve", lambda e: e.scalar_tensor_tensor(out=lamt[:, 128:192], in0=lamt[:, 128:192], scalar=1.0, in1=lamt[:, 192:256],
                                                      op0=ALU.mult, op1=ALU.mult, accum_out=lams[:, 1:2]),
             reads=["lamt", "lams"], writes=["lamt", "lams"])
        S.op("act", lambda e: e.activation(out=lams[:, 2:4], in_=lams[:, 0:2], func=AF.Exp), reads=["lams"], writes=["lams"])
        S.op("dve", lambda e: e.tensor_tensor(out=lams[:, 4:5], in0=lams[:, 2:3], in1=lams[:, 3:4], op=ALU.subtract),
             reads=["lams"], writes=["lams"])
        S.op("dve", lambda e: e.tensor_scalar(out=lams[:, 5:6], in0=lams[:, 4:5], scalar1=LAM_INIT, scalar2=-1.0,
                                               op0=ALU.add, op1=ALU.mult), reads=["lams"], writes=["lams"])
        neglam = lams[:, 5:6]
        dump("cols", cols, ["cols"])
        dump("lams", lams, ["lams"])

        def col(i):
            return cols[:, i:i + 1]

        S.op("dve", lambda e: e.tensor_scalar(out=ncols, in0=cols, scalar1=-1.0, scalar2=None, op0=ALU.mult),
             reads=["cols"], writes=["cols"])

        PA = Arena(arena_t[:], SETUP_END)
        NPB = 4
        wf = [PA.f32(DFF) for _ in range(NPB)]
        wb = [PA.bf16(4096) for _ in range(NPB)]
        pe_rr = [0]

        def ew(dst, src, gi, neg, rk, wk):
            eng = ("dve", "act")[pe_rr[0] % 2]
            pe_rr[0] += 1
            if gi is None:
                if eng == "dve":
                    S.op("dve", lambda e: e.tensor_copy(out=dst, in_=src), reads=[rk], writes=[wk])
                else:
                    S.op("act", lambda e: e.activation(out=dst, in_=src, func=AF.Copy), reads=[rk], writes=[wk])
                return
            g = (ncols if neg else cols)[:, gi:gi + 1]
            if eng == "dve":
                S.op("dve", lambda e: e.tensor_scalar(out=dst, in0=src, scalar1=g, scalar2=None, op0=ALU.mult),
                     reads=[rk, "cols"], writes=[wk])
            else:
                S.op("act", lambda e: e.activation(out=dst, in_=src, func=AF.Copy, scale=g), reads=[rk, "cols"], writes=[wk])

        pidx = [0]

        def prep_next():
            b = pidx[0] % NPB
            pidx[0] += 1
            return b

        def cp(dst, src, g, rk, wk):
            ew(dst, src, g, False, rk, wk)

        def cpneg(dst, src, g, rk, wk):
            ew(dst, src, g, True, rk, wk)

        def rot(dst, src, g, nh, rk, wk):
            dv = dst.rearrange("p (h t e) -> p h t e", h=nh, t=2)
            sv = src.rearrange("p (h t e) -> p h t e", h=nh, t=2)
            cpneg(dv[:, :, 0, :], sv[:, :, 1, :], g, rk, wk)
            cp(dv[:, :, 1, :], sv[:, :, 0, :], g, rk, wk)

        for k in range(8):
            b = prep_next()
            f, o = wf[b], wb[b]
            fk, ok = "wf%d" % b, "wb%d" % b
            dma(lambda e, f=f, k=k: e.dma_start(out=f[:, 0:2304], in_=w_in_d[k * 128:(k + 1) * 128, :]), writes=[fk], key=fk)
            g = 88 + k
            A0, A1, Q, QR = (o[:, i * 1024:(i + 1) * 1024] for i in range(4))
            cp(A0[:, 0:128], f[:, 512:640], g, fk, ok)
            rot(A0[:, 128:256], f[:, 512:640], g, 2, fk, ok)
            cp(A0[:, 256:384], f[:, 640:768], g, fk, ok)
            cp(A0[:, 384:896], f[:, 1792:2304], g, fk, ok)
            cp(A0[:, 896:1024], f[:, 0:128], g, fk, ok)
            cp(A1[:, 0:512], f[:, 1280:1792], g, fk, ok)
            rot(A1[:, 512:1024], f[:, 1280:1792], g, 8, fk, ok)
            qd = Q[:, 0:512].rearrange("p (j t d) -> p j t d", j=4, t=2)
            qs = f[:, 0:512].rearrange("p (t j d) -> p j t d", t=2, j=4)
            for t in range(2):
                cp(qd[:, :, t, :], qs[:, :, t, :], g, fk, ok)
            cp(Q[:, 512:1024], f[:, 768:1280], g, fk, ok)
            qrd = QR[:, 0:512].rearrange("p (j t h e) -> p j t h e", j=4, t=2, h=2)
            qrs = f[:, 0:512].rearrange("p (t j h e) -> p j t h e", t=2, j=4, h=2)
            for t in range(2):
                cpneg(qrd[:, :, t, 0, :], qrs[:, :, t, 1, :], g, fk, ok)
                cp(qrd[:, :, t, 1, :], qrs[:, :, t, 0, :], g, fk, ok)
            rot(QR[:, 512:1024], f[:, 768:1280], g, 8, fk, ok)
            dma(lambda e, o=o, k=k: [e.dma_start(out=W[base + k, :, :], in_=o[:, i * 1024:(i + 1) * 1024])
                                     for i, base in enumerate((SL_A0, SL_A1, SL_Q, SL_QR))],
                reads=[ok], writes=["W"], key=ok, n=4)
        S2_BASE = ARENA_BYTES - (2 * DFF * 4 + DFF * 2)
        PA2 = Arena(arena_t[:], S2_BASE)
        wf2 = [PA2.f32(DFF) for _ in range(2)]
        wb2 = PA2.bf16(DFF)
        prep_steps = []
        p2 = [0]

        def add_step(src_ap, ncols, gi, store_fn):
            def step():
                b = p2[0] % 2
                p2[0] += 1
                f, o = wf2[b], wb2
                fk, ok = "wf2_%d" % b, "wb2"
                dma(lambda e: e.dma_start(out=f[:, 0:ncols], in_=src_ap), writes=[fk], key=fk)
                ew(o[:, 0:ncols], f[:, 0:ncols], gi, False, fk, ok)
                dma(lambda e: store_fn(e, o), reads=[ok], writes=["W"], key=ok)
            prep_steps.append(step)

        for j in range(8):
            add_step(w_out_d[j * 128:(j + 1) * 128, :], 1024, None,
                     lambda e, o, j=j: e.dma_start(out=W[SL_O + j, :, :], in_=o[:, 0:1024]))
        for (wd, base) in ((w_gate_d, SL_G), (w_up_d, SL_U)):
            for k in range(8):
                add_step(wd[k * 128:(k + 1) * 128, :], DFF, 96 + k,
                         lambda e, o, k=k, base=base: e.dma_start(
                             out=W[base:base + NCH, :, k * 128:(k + 1) * 128].rearrange("c p j -> p c j"),
                             in_=o[:, 0:DFF].rearrange("p (c j) -> p c j", j=128)))
        for c in range(NCH):
            add_step(w_down_d[c * 128:(c + 1) * 128, :], 1024, None,
                     lambda e, o, c=c: e.dma_start(out=W[SL_D + c, :, :], in_=o[:, 0:1024]))

        def rstd_from_ss(ss_ap, n, scale, key, rows=128):
            S.op("act", lambda e: e.activation(out=ss_ap, in_=ss_ap, func=AF.Ln, scale=scale, bias=epsc[0:rows, :]),
                 reads=[key, "epsc"], writes=[key])
            S.op("act", lambda e: e.activation(out=ss_ap, in_=ss_ap, func=AF.Exp, scale=-0.5), reads=[key], writes=[key])

        def norm_T(srcs, dstT, dkeyf, NTt, ss, sskey, nb, junk, sq_eng):
            n = len(srcs)
            for i, (xa, xk, rows) in enumerate(srcs):
                if sq_eng == "act":
                    S.op("act", lambda e, xa=xa, i=i, rows=rows: e.activation(out=junk[0:rows, :], in_=xa, func=AF.Square,
                                                                               accum_out=ss[0:rows, i:i + 1]),
                         reads=[xk], writes=["junk", sskey])
                else:
                    S.op("dve", lambda e, xa=xa, i=i, rows=rows: e.scalar_tensor_tensor(
                        out=junk[0:rows, :], in0=xa, scalar=1.0, in1=xa, op0=ALU.mult, op1=ALU.mult,
                        accum_out=ss[0:rows, i:i + 1]), reads=[xk], writes=["junk", sskey])
            rows0 = srcs[0][2]
            rstd_from_ss(ss[0:rows0, 0:n], n, 1.0 / D, sskey, rows0)
            for i, (xa, xk, rows) in enumerate(srcs):
                nbi = nb[i % len(nb)]
                nk = "nb%d" % (i % len(nb))
                S.op("dve", lambda e, xa=xa, i=i, rows=rows, nbi=nbi: e.tensor_scalar(
                    out=nbi[0:rows, :], in0=xa, scalar1=ss[0:rows, i:i + 1], scalar2=None, op0=ALU.mult),
                    reads=[xk, sskey], writes=[nk])
                pb = next_bank()
                psb = bank(pb).bitcast(BF16)
                for j in range(8):
                    S.op("pe", lambda e, j=j, rows=rows, nbi=nbi, psb=psb: e.transpose(
                        out=psb[:, j * 128:j * 128 + rows], in_=nbi[0:rows, j * 128:(j + 1) * 128],
                        identity=identb[0:rows, 0:rows]), reads=[nk, "identb"], writes=["ps%d" % pb])
                ev = "act" if sq_eng == "act" else "dve"
                src_v = psb.rearrange("p (j t) -> p j t", j=8)[:, :, 0:rows]
                dst_v = dstT[:, :, i * 128:i * 128 + rows]
                if ev == "act":
                    S.op("act", lambda e, src_v=src_v, dst_v=dst_v: e.activation(out=dst_v, in_=src_v, func=AF.Copy),
                         writes=["ps%d" % pb] + dkeyf(i))
                else:
                    S.op("dve", lambda e, src_v=src_v, dst_v=dst_v: e.tensor_copy(out=dst_v, in_=src_v),
                         writes=["ps%d" % pb] + dkeyf(i))

        def proj_fm(wslots, wkeys, c0, srcT, skeys, ntok, pb, col_off=0):
            for k in range(8):
                S.op("pe", lambda e, k=k: e.matmul(bank(pb)[:, col_off:col_off + ntok], lhsT=wslots[k][:, c0:c0 + 128],
                                                   rhs=srcT[:, k, 0:ntok], start=(k == 0), stop=(k == 7)),
                     reads=[wkeys[k]] + skeys, writes=["ps%d" % pb])

        def qk_post(kind, pP, pR, ntok, tabs, dsts, gcol, grcol, T):
            P = bank(pP)[:, 0:ntok]
            R = bank(pR)[:, 0:ntok]
            tA, tB, sqb, rq = T["tA"], T["tB"], T["sqb"], T["rq"]
            kk = T.get("keys", {})
            ktA, ktB, ksq, krq = kk.get("tA", "tA"), kk.get("tB", "tB"), kk.get("sqb", "sqb"), kk.get("rq", "rq")
            if kind == "A":
                cosT, sinT = tabs[:, 0, 0:ntok], tabs[:, 1, 0:ntok]
                S.op("act", lambda e: e.activation(out=sqb[:, 0:ntok], in_=P, func=AF.Square), writes=["ps%d" % pP, ksq])
                pS = next_bank()
                S.op("pe", lambda e: e.matmul(bank(pS)[:, 0:ntok], lhsT=onesbd, rhs=sqb[:, 0:ntok], start=True, stop=True),
                     reads=[ksq, "onesbd"], writes=["ps%d" % pS])
                S.op("act", lambda e: e.activation(out=rq[:, 0:ntok], in_=bank(pS)[:, 0:ntok], func=AF.Ln, scale=1.0 / 64,
                                                   bias=epsc), reads=["epsc"], writes=["ps%d" % pS, krq])
                S.op("act", lambda e: e.activation(out=rq[:, 0:ntok], in_=rq[:, 0:ntok], func=AF.Exp, scale=-0.5),
                     reads=[krq], writes=[krq])
                S.op("dve", lambda e: e.scalar_tensor_tensor(out=tA[:, 0:ntok], in0=P, scalar=col(gcol), in1=cosT,
                                                              op0=ALU.mult, op1=ALU.mult),
                     reads=["tabs", "cols"], writes=["ps%d" % pP, ktA])
                S.op("dve", lambda e: e.scalar_tensor_tensor(out=tB[:, 0:ntok], in0=R, scalar=col(grcol), in1=sinT,
                                                              op0=ALU.mult, op1=ALU.mult),
                     reads=["tabs", "cols"], writes=["ps%d" % pR, ktB])
                S.op("pool", lambda e: e.tensor_tensor(out=tA[:, 0:ntok], in0=tA[:, 0:ntok], in1=tB[:, 0:ntok], op=ALU.add),
                     reads=[ktB], writes=[ktA])
                for (dst, p0, p1, dkey) in dsts:
                    S.op("pool", lambda e, dst=dst, p0=p0, p1=p1: e.tensor_tensor(out=dst, in0=tA[p0:p1, 0:ntok], in1=rq[p0:p1, 0:ntok],
                                                                                  op=ALU.mult), reads=[ktA, krq], writes=[dkey])
            else:
                cosT, sinT = tabs[:, 2, 0:ntok], tabs[:, 3, 0:ntok]
                S.op("dve", lambda e: e.tensor_tensor(out=tA[:, 0:ntok], in0=P, in1=cosT, op=ALU.mult),
                     reads=["tabs"], writes=["ps%d" % pP, ktA])
                S.op("dve", lambda e: e.tensor_tensor(out=tB[:, 0:ntok], in0=R, in1=sinT, op=ALU.mult),
                     reads=["tabs"], writes=["ps%d" % pR, ktB])
                for (dst, p0, p1, dkey) in dsts:
                    S.op("pool", lambda e, dst=dst, p0=p0, p1=p1: e.tensor_tensor(out=dst, in0=tA[p0:p1, 0:ntok], in1=tB[p0:p1, 0:ntok],
                                                                                  op=ALU.add), reads=[ktA, ktB], writes=[dkey])

        def do_sequence(sidx, xsrc, ydst, SEQ, NT, init):
            L = SEQ + NMETA
            Lp = L + (L % 2)
            nsb = NT // 128
            ntile = SEQ // NT
            nchunk = 1 + SEQ // 128
            sid = "q%d" % sidx

            KT, VA, VB, KV_END = phase_A(xsrc, SEQ, 512, L, Lp, 4, SEQ // 512, nchunk, init)
            phase_B(xsrc, ydst, SEQ, NT, L, Lp, nsb, ntile, nchunk, KT, VA, VB, KV_END, init)

        def phase_A(xsrc, SEQ, NT, L, Lp, nsb, ntile, nchunk, init):
            S.barrier()
            A = Arena(arena_t[:], CONST_END)
            KT = A.bf16(5 * Lp).rearrange("p (j l) -> p j l", j=5)
            VAf = A.bf16(nchunk * 2 * 72)
            VBf = A.bf16(nchunk * 4 * 136)
            VA = VAf.rearrange("p (c h e) -> p c h e", c=nchunk, h=2)
            VB = VBf.rearrange("p (c h e) -> p c h e", c=nchunk, h=4)
            KV_END = A.off
            wA = A.bf16(16 * 1024).rearrange("p (s c) -> p s c", s=16)
            xr = A.f32(4 * D).rearrange("p (s c) -> p s c", s=4)
            nT = [A.bf16(8 * NT).rearrange("p (k t) -> p k t", k=8) for _ in range(2)]
            nb = [A.bf16(D) for _ in range(2)]
            junk = A.bf16(D)
            tabs = A.f32(4 * NT).rearrange("p (a t) -> p a t", a=4)
            T = {"tA": A.f32(NT), "tB": A.f32(NT), "sqb": A.bf16(NT), "rq": A.f32(NT)}
            ss = A.f32(8)
            PbA = [A.bf16(NT) for _ in range(2)]
            TA2 = None
            if A.off + 14 * NT + 256 <= (S2_BASE if prep_steps else ARENA_BYTES):
                TA2 = {"tA": A.f32(NT), "tB": A.f32(NT), "sqb": A.bf16(NT), "rq": A.f32(NT),
                       "keys": {"tA": "tA2", "tB": "tB2", "rq": "rq2", "sqb": "sqb2"}}
            if prep_steps:
                assert A.off <= S2_BASE, (A.off, S2_BASE)
            per_tile = (len(prep_steps) + ntile) // (ntile + 1) if prep_steps else 0

            if init:
                S.op("pool", lambda e: e.memset(VAf, 1.0), writes=["VAall"])
                S.op("pool", lambda e: e.memset(VBf, 1.0), writes=["VBall"])
            for s in range(16):
                dma(lambda e, s=s: e.dma_start(out=wA[:, s, :], in_=W[s, :, :]), reads=["W"], writes=[("wA", s)], key=("wA", s))
            wA0 = [wA[:, k, :] for k in range(8)]
            wA1 = [wA[:, 8 + k, :] for k in range(8)]
            kA0 = [("wA", k) for k in range(8)]
            kA1 = [("wA", 8 + k) for k in range(8)]

            xctr = [0]
            tiles = [("meta", 0)] + [("real", i) for i in range(ntile)]

            def prepare(kind, ti):
                if kind == "meta":
                    ntok, col0 = NMETA, 0
                    sbs = [(meta_d[:, :], NMETA, 0)]
                else:
                    ntok, col0 = NT, NMETA + ti * NT
                    sbs = [(xsrc[ti * NT + s * 128: ti * NT + (s + 1) * 128, :], 128, 1 + ti * nsb + s) for s in range(nsb)]
                srcs = []
                for (src, rows, chunk) in sbs:
                    sl = xctr[0] % 4
                    xctr[0] += 1
                    dma(lambda e, sl=sl, src=src, rows=rows: e.dma_start(out=xr[0:rows, sl, :], in_=src),
                        writes=[("x", sl)], key=("x", sl))
                    srcs.append((xr[0:rows, sl, :], ("x", sl), rows))
                nTb = nT[(ti + 1) % 2 if kind == "real" else 0]
                nTk = "nT%d" % ((ti + 1) % 2 if kind == "real" else 0)
                norm_T(srcs, nTb, (lambda i, nTk=nTk: [(nTk, i)]), NT, ss, "ss", nb, junk, "act")
                nTkeys = [(nTk, i) for i in range(len(srcs))]
                return (kind, ti, ntok, col0, sbs, nTb, nTk, nTkeys)

            def process(kind, ti, ntok, col0, sbs, nTb, nTk, nTkeys):
                dma(lambda e, col0=col0, ntok=ntok: e.dma_start(
                    out=tabs[:, :, 0:ntok], in_=tabs_d[:, :, col0:col0 + ntok].rearrange("a p n -> p a n")),
                    writes=["tabs"], key="tabs")
                def k_finish(jt, pP, kind=kind, ti=ti, ntok=ntok, col0=col0):
                    pR = next_bank()
                    pb, pbk = PbA[jt % 2], "pbA%d" % (jt % 2)
                    S.op("pe", lambda e: e.matmul(bank(pR)[:, 0:ntok], lhsT=rmat, rhs=pb[:, 0:ntok], start=True, stop=True),
                         reads=[pbk, "rmat"], writes=["ps%d" % pR])
                    qk_post("A" if jt == 0 else "B", pP, pR, ntok, tabs,
                            [(KT[:, jt, col0:col0 + ntok], 0, 128, ("K", kind, ti))], 106, 107,
                            T if (jt % 2 == 0 or TA2 is None) else TA2)
                prev = None
                for jt in range(5):
                    pP = next_bank()
                    if jt == 0:
                        proj_fm(wA0, kA0, 0, nTb, nTkeys, ntok, pP)
                    else:
                        proj_fm(wA1, kA1, (jt - 1) * 128, nTb, nTkeys, ntok, pP)
                    S.op("act", lambda e, jt=jt, pP=pP, ntok=ntok: e.activation(out=PbA[jt % 2][:, 0:ntok], in_=bank(pP)[:, 0:ntok],
                                                                                func=AF.Copy), writes=["ps%d" % pP, "pbA%d" % (jt % 2)])
                    if prev is not None:
                        k_finish(*prev)
                    prev = (jt, pP)
                k_finish(*prev)
                for i, (src, rows, chunk) in enumerate(sbs):
                    pa, pbk = next_bank(), next_bank()
                    for k in range(8):
                        S.op("pe", lambda e, k=k, i=i, rows=rows, pa=pa, nTb=nTb: e.matmul(
                            bank(pa)[0:rows, 0:128], lhsT=nTb[:, k, i * 128:i * 128 + rows], rhs=wA0[k][:, 256:384],
                            start=(k == 0), stop=(k == 7)), reads=[kA0[k], (nTk, i)], writes=["ps%d" % pa])
                    for k in range(8):
                        S.op("pe", lambda e, k=k, i=i, rows=rows, pbk=pbk, nTb=nTb: e.matmul(
                            bank(pbk)[0:rows, 0:512], lhsT=nTb[:, k, i * 128:i * 128 + rows], rhs=wA0[k][:, 384:896],
                            start=(k == 0), stop=(k == 7)), reads=[kA0[k], (nTk, i)], writes=["ps%d" % pbk])
                    S.op("act", lambda e, rows=rows, chunk=chunk, pa=pa: e.activation(
                        out=VA[0:rows, chunk, :, 0:64], in_=bank(pa)[0:rows, 0:128].rearrange("p (h e) -> p h e", h=2),
                        func=AF.Copy), reads=["VAall"], writes=["ps%d" % pa, ("VA", chunk)])
                    S.op("act", lambda e, rows=rows, chunk=chunk, pbk=pbk: e.activation(
                        out=VB[0:rows, chunk, :, 0:128], in_=bank(pbk)[0:rows, 0:512].rearrange("p (h e) -> p h e", h=4),
                        func=AF.Copy), reads=["VBall"], writes=["ps%d" % pbk, ("VB", chunk)])
                for _ in range(per_tile):
                    if prep_steps:
                        prep_steps.pop(0)()

            stt = prepare(*tiles[0])
            for tidx in range(len(tiles)):
                nxt = prepare(*tiles[tidx + 1]) if tidx + 1 < len(tiles) else None
                process(*stt)
                stt = nxt

            while prep_steps:
                prep_steps.pop(0)()

            dump("KT", KT, [("K", k_, t_) for (k_, t_) in tiles])
            dump("wA", wA, [("wA", s_) for s_ in range(16)])
            dump("tA", T["tA"], ["tA"])
            dump("tB", T["tB"], ["tB"])
            dump("tabsA", tabs, ["tabs"])
            dump("VA", VA, [("VA", c_) for c_ in range(nchunk)])
            dump("VB", VB, [("VB", c_) for c_ in range(nchunk)])
            dump("nTA", nT[0], ["nT0", ("nT0", 0), ("nT0", 1)])
            return KT, VA, VB, KV_END

        def phase_B(xsrc, ydst, SEQ, NT, L, Lp, nsb, ntile, nchunk, KT, VA, VB, KV_END, init):
            S.barrier()
            B = Arena(arena_t[:], KV_END)
            NW = 8
            NPT = 3 if NT <= 256 else 2
            NX = 5
            PREFETCH_X = (2 * nsb + 1 <= NX)
            wr = B.bf16(NW * 1024).rearrange("p (s c) -> p s c", s=NW)
            act_bytes = NCH * NT * 2
            sqt_off = (act_bytes + 2047) // 2048 * 2048
            znt_off = 8 * 2048
            zbytes = max(znt_off + 8 * NT * 2, sqt_off + nsb * 512 * 4)
            nz = (zbytes + 2047) // 2048
            Zf = B.f32(nz * 512)
            Zq = Zf[:, 0:8 * 512].bitcast(BF16).rearrange("p (s c) -> p s c", s=8)
            act = Zf[:, 0:act_bytes // 4].bitcast(BF16).rearrange("p (c t) -> p c t", c=NCH)
            sqt = Zf[:, sqt_off // 4:sqt_off // 4 + nsb * 512]

            def zkeys(lo, hi):
                return [("Z", z) for z in range(lo // 2048, (hi - 1) // 2048 + 1)]

            def act_keys(c):
                return zkeys(c * NT * 2, (c + 1) * NT * 2)
            sqt_keys = zkeys(sqt_off, sqt_off + nsb * 512 * 4)
            ZnT = Zf[:, znt_off // 4:znt_off // 4 + 4 * NT].bitcast(BF16).rearrange("p (k t) -> p k t", k=8)
            znt_keys = zkeys(znt_off, znt_off + 8 * NT * 2)
            xr = B.f32(NX * D).rearrange("p (s c) -> p s c", s=NX)
            nT1 = B.bf16(8 * NT).rearrange("p (k t) -> p k t", k=8)
            QTf = B.bf16(16 * NT)
            QT = QTf.rearrange("p (j t) -> p j t", j=16)
            PT = [B.bf16(1024) for _ in range(NPT)]
            mix = B.bf16(nsb * D).rearrange("p (s c) -> p s c", s=nsb)
            oB = B.f32(nsb * 512).rearrange("p (s h e) -> p s h e", s=nsb, h=4)
            tB2 = B.f32(nsb * 128).rearrange("p (s e) -> p s e", s=nsb)
            nb = [B.bf16(D) for _ in range(2)]
            junk = PT[0]
            tabs = B.f32(4 * NT).rearrange("p (a t) -> p a t", a=4)
            T = {"tA": B.f32(NT), "tB": B.f32(NT), "sqb": B.bf16(NT), "rq": B.f32(NT)}
            pt_alias = False
            if NPT == 2 and NT * 4 >= 2048:
                PT.append(T["rq"][:, 0:512].bitcast(BF16))
                NPT = 3
                pt_alias = True

            def ptkeys(i):
                return [("PT", i)] + (["rq"] if (pt_alias and i == 2) else [])
            ff0 = B.off
            GW = NT + 132
            GU = [B.bf16(2 * GW).rearrange("p (a t) -> p a t", a=2) for _ in range(3)]
            cvall = B.f32(2 * NT)
            geall = B.bf16(2 * NT)
            cv = [cvall[:, i * NT:(i + 1) * NT] for i in range(2)]
            ge = [geall[:, i * NT:(i + 1) * NT] for i in range(2)]
            if B.off - ff0 < 4 * D:
                B.f32((4 * D - (B.off - ff0)) // 4 + 8)
            xm = arena_t[:][:, ff0 // 4: ff0 // 4 + D]
            GUcf = B.bf16(NCH * 2 * 130)
            GUc = GUcf.rearrange("p (c a t) -> p c a t", c=NCH, a=2)
            ss = B.f32(8)
            ssB = B.f32(4 * nsb)
            rs = B.f32(8)
            ssF = B.f32(8)
            print("phase B arena end: %d B of %d (NT=%d L=%d)" % (B.off, ARENA_BYTES, NT, L))

            S.op("dve", lambda e: e.memset(QTf, 0.0), writes=[("QT", j) for j in range(16)])
            S.op("dve", lambda e: e.memset(GUcf, 0.0), writes=[("Gc", c) for c in range(NCH)])

            wctr = [0]

            def wload(slot_idx):
                sl = wctr[0] % NW
                wctr[0] += 1
                dma(lambda e, sl=sl, slot_idx=slot_idx: e.dma_start(out=wr[:, sl, :], in_=W[slot_idx, :, :]),
                    reads=["W"], writes=[("w", sl)], key=("w", sl))
                return wr[:, sl, :], ("w", sl)

            def next_nT():
                return nT1, "nT1"

            def xslot(g):
                return g % NX

            def load_x_tile(ti):
                for s in range(nsb):
                    g = ti * nsb + s
                    sl = xslot(g)
                    dma(lambda e, sl=sl, ti=ti, s=s: e.dma_start(out=xr[:, sl, :], in_=xsrc[ti * NT + s * 128: ti * NT + (s + 1) * 128, :]),
                        writes=[("x", sl)], key=("x", sl))

            def attention(ntok, qrows, nsbq):
                hcs = []
                for h in range(8):
                    hcs.append(("A", h, h % 4, 0, (h // 4) * 64, VA, h // 4, 64))
                for hb in range(4):
                    for cp_ in range(2):
                        hcs.append(("B", (hb, cp_), 4 + hb, 1 + hb, cp_ * 64, VB, hb, 128))
                cstride = NT if ntok == NT else ntok
                G = 1024 // cstride
                groups = [[0]]
                real = list(range(1, nchunk))
                for i in range(0, len(real), G):
                    groups.append(real[i:i + G])
                steps = [(hi, gi) for hi in range(len(hcs)) for gi in range(len(groups))]

                def crow(c):
                    return NMETA if c == 0 else 128

                def ccol(c):
                    return 0 if c == 0 else NMETA + (c - 1) * 128

                def sb_base(si):
                    return (si % 2) * 1024

                def acc_base(hi):
                    return 2048 + (hi % 2) * 1024

                def acc_region(hi, dv, sb):
                    w = dv + 1
                    per = 512 // w
                    return acc_base(hi) + (sb // per) * 512 + (sb % per) * w

                def emit_qk(si):
                    hi, gi = steps[si]
                    _, _, qt, kt, r0, _, _, _ = hcs[hi]
                    base = sb_base(si)
                    for li, c in enumerate(groups[gi]):
                        rows = crow(c)
                        o0 = base + li * cstride
                        bk = o0 // 512
                        qi = 2 * qt + r0 // 64
                        S.op("pe", lambda e, rows=rows, o0=o0, kt=kt, c=c, qi=qi: e.matmul(
                            PS[0:rows, o0:o0 + ntok], lhsT=KT[:, kt, ccol(c):ccol(c) + rows],
                            rhs=QT[:, qi, 0:ntok], start=True, stop=True),
                            reads=[("QT", qi)], writes=["ps%d" % bk])

                def emit_exp(si):
                    hi, gi = steps[si]
                    grp = groups[gi]
                    rows = crow(grp[0])
                    base = sb_base(si)
                    pt = PT[si % NPT]
                    n = len(grp)
                    src = PS[0:rows, base:base + n * cstride].rearrange("p (g t) -> p g t", g=n)[:, :, 0:ntok]
                    dst = pt[0:rows, 0:n * cstride].rearrange("p (g t) -> p g t", g=n)[:, :, 0:ntok]
                    bks = sorted(set((base + li * cstride) // 512 for li in range(n)))
                    S.op("act", lambda e, src=src, dst=dst: e.activation(out=dst, in_=src, func=AF.Exp, scale=0.125),
                         writes=["ps%d" % b for b in bks] + ptkeys(si % NPT))

                def emit_pv(si):
                    hi, gi = steps[si]
                    kind, _, _, _, _, Vs, vh, dv = hcs[hi]
                    pt = PT[si % NPT]
                    for li, c in enumerate(groups[gi]):
                        rows = crow(c)
                        for sb in range(nsbq):
                            o0 = acc_region(hi, dv, sb)
                            bk = o0 // 512
                            first_in_bank = (gi == 0 and li == 0 and (o0 % 512) == 0)
                            last = (gi == len(groups) - 1 and li == len(groups[gi]) - 1)
                            S.op("pe", lambda e, rows=rows, o0=o0, li=li, sb=sb, c=c, dv=dv, vh=vh, Vs=Vs, pt=pt,
                                 fb=first_in_bank, last=last: e.matmul(
                                PS[0:qrows, o0:o0 + dv + 1], lhsT=pt[0:rows, li * cstride + sb * 128: li * cstride + sb * 128 + qrows],
                                rhs=Vs[0:rows, c, vh, 0:dv + 1], start=fb, stop=last, skip_group_check=True),
                                reads=ptkeys(si % NPT), writes=["ps%d" % bk])

                def emit_evac(hi):
                    kind, hid, _, _, _, _, _, dv = hcs[hi]
                    w = dv + 1
                    per = 512 // w
                    segs = []
                    sb = 0
                    while sb < nsbq:
                        cnt = min(per - (sb % per), nsbq - sb)
                        segs.append((sb, cnt))
                        sb += cnt
                    for (sb0, cnt) in segs:
                        o0 = acc_region(hi, dv, sb0)
                        bk = "ps%d" % (o0 // 512)
                        v = PS[0:qrows, o0:o0 + cnt * w].rearrange("p (s e) -> p s e", s=cnt)
                        rsv = rs[0:qrows, sb0:sb0 + cnt]
                        S.op("dve", lambda e, v=v, rsv=rsv, dv=dv: e.reciprocal(out=rsv, in_=v[:, :, dv]),
                             writes=[bk, "rs"])
                        if kind == "A":
                            h = hid
                            dst = mix[0:qrows, sb0:sb0 + cnt, h * 64:(h + 1) * 64]
                            S.op("dve", lambda e, v=v, rsv=rsv, dst=dst, cnt=cnt: e.tensor_tensor(
                                out=dst, in0=v[:, :, 0:64], in1=rsv[:, :, None].broadcast_to([qrows, cnt, 64]), op=ALU.mult),
                                reads=["rs"], writes=[bk, ("mix", h // 2)])
                        else:
                            hb, cp_ = hid
                            dst = oB[0:qrows, sb0:sb0 + cnt, hb, :]
                            if cp_ == 0:
                                S.op("dve", lambda e, v=v, rsv=rsv, dst=dst, cnt=cnt: e.tensor_tensor(
                                    out=dst, in0=v[:, :, 0:128], in1=rsv[:, :, None].broadcast_to([qrows, cnt, 128]), op=ALU.mult),
                                    reads=["rs"], writes=[bk, ("oB", hb)])
                            else:
                                S.op("dve", lambda e, rsv=rsv: e.tensor_scalar(out=rsv, in0=rsv, scalar1=neglam[0:qrows, :],
                                                                               scalar2=None, op0=ALU.mult),
                                     reads=["lams"], writes=["rs"])
                                t2 = tB2[0:qrows, sb0:sb0 + cnt, :]
                                S.op("dve", lambda e, v=v, rsv=rsv, t2=t2, cnt=cnt: e.tensor_tensor(
                                    out=t2, in0=v[:, :, 0:128], in1=rsv[:, :, None].broadcast_to([qrows, cnt, 128]), op=ALU.mult),
                                    reads=["rs"], writes=[bk, "tB2"])
                                S.op("pool", lambda e, dst=dst, t2=t2: e.tensor_tensor(out=dst, in0=dst, in1=t2, op=ALU.add),
                                     reads=["tB2"], writes=[("oB", hb)])

                ng = len(groups)
                emit_qk(0)
                for si in range(len(steps)):
                    if si + 1 < len(steps):
                        emit_qk(si + 1)
                    emit_exp(si)
                    emit_pv(si)
                    if steps[si][1] == ng - 1:
                        emit_evac(steps[si][0])
                n4 = nsbq * 4
                oBf = oB[0:qrows, 0:nsbq, :, :].rearrange("p s h e -> p (s h) e")
                sqv = sqt[0:qrows, 0:n4 * 128].rearrange("p (a e) -> p a e", a=n4)
                S.op("dve", lambda e: e.tensor_tensor(out=sqv, in0=oBf, in1=oBf, op=ALU.mult),
                     reads=[("oB", h) for h in range(4)], writes=sqt_keys)
                S.op("dve", lambda e: e.tensor_reduce(out=ssB[0:qrows, 0:n4], in_=sqv, axis=AX.X, op=ALU.add),
                     reads=sqt_keys, writes=["ssB"])
                rstd_from_ss(ssB[0:qrows, 0:n4], n4, 1.0 / 128, "ssB", qrows)
                S.op("dve", lambda e: e.tensor_tensor(out=oBf, in0=oBf, in1=ssB[0:qrows, 0:n4, None].broadcast_to([qrows, n4, 128]),
                                                      op=ALU.mult), reads=["ssB"], writes=[("oB", h) for h in range(4)])
                for sb in range(nsbq):
                    S.op("dve", lambda e, sb=sb: e.scalar_tensor_tensor(
                        out=mix[0:qrows, sb, 512:1024].rearrange("p (h e) -> p h e", h=4), in0=oB[0:qrows, sb, :, :],
                        scalar=1.0 - LAM_INIT, in1=gsub[0:qrows, None, :].broadcast_to([qrows, 4, 128]),
                        op0=ALU.mult, op1=ALU.mult), reads=[("oB", h) for h in range(4)] + ["gsub"],
                        writes=[("mix", 4 + h) for h in range(4)])

            gctr = [0]

            def ffn(mode, n2T, n2keys, ncur, wsbs, gbase, ybase):
                if mode != "flush":
                    pend = None
                    for c in range(NCH):
                        wg, wgk = wload(SL_G + c)
                        if mode == "real":
                            wu, wuk = wload(SL_U + c)
                        pG = next_bank()
                        for k in range(8):
                            S.op("pe", lambda e, k=k, wg=wg, pG=pG: e.matmul(
                                bank(pG)[:, 0:ncur], lhsT=wg[:, k * 128:(k + 1) * 128], rhs=n2T[:, k, 0:ncur],
                                start=(k == 0), stop=(k == 7)), reads=[wgk] + n2keys, writes=["ps%d" % pG])
                        if mode == "meta":
                            S.op("dve", lambda e, c=c, pG=pG: e.tensor_copy(out=GUc[:, c, 0, 128:129], in_=bank(pG)[:, NMETA - 1:NMETA]),
                                 writes=["ps%d" % pG, ("Gc", c)])
                            continue
                        pU = next_bank()
                        for k in range(8):
                            S.op("pe", lambda e, k=k, wu=wu, pU=pU: e.matmul(
                                bank(pU)[:, 0:ncur], lhsT=wu[:, k * 128:(k + 1) * 128], rhs=n2T[:, k, 0:ncur],
                                start=(k == 0), stop=(k == 7)), reads=[wuk] + n2keys, writes=["ps%d" % pU])
                        st = ffn_chunk_a(c, pG, pU, NT)
                        if pend is not None:
                            ffn_chunk_b(*pend)
                        pend = st
                    if pend is not None:
                        ffn_chunk_b(*pend)
                else:
                    gs = 2 * NT // 128
                    for c0 in range(0, NCH, gs):
                        n = min(gs, NCH - c0)
                        Gv = GUc[:, c0:c0 + n, 0, :]
                        Uv = GUc[:, c0:c0 + n, 1, 1:129]
                        t = cvall[:, 0:n * 128].rearrange("p (c t) -> p c t", c=n)
                        u = sqt[:, 0:n * 128].rearrange("p (c t) -> p c t", c=n)
                        gv = geall[:, 0:n * 128].rearrange("p (c t) -> p c t", c=n)
                        gck = [("Gc", c) for c in range(c0, c0 + n)]

                        def wb(base, c0=c0, n=n):
                            return cols[:, base + c0:base + c0 + n, None].broadcast_to([128, n, 128])
                        S.op("dve", lambda e, Gv=Gv, t=t, wb=wb: e.tensor_tensor(out=t, in0=Gv[:, :, 0:128], in1=wb(0), op=ALU.mult),
                             reads=gck + ["cols"], writes=["cv0", "cv1"])
                        for tap in (1, 2):
                            S.op("dve", lambda e, Gv=Gv, u=u, wb=wb, tap=tap: e.tensor_tensor(
                                out=u, in0=Gv[:, :, tap:tap + 128], in1=wb(22 * tap), op=ALU.mult), reads=gck + ["cols"], writes=sqt_keys)
                            S.op("dve", lambda e, t=t, u=u: e.tensor_tensor(out=t, in0=t, in1=u, op=ALU.add),
                                 reads=sqt_keys, writes=["cv0", "cv1"])
                        S.op("dve", lambda e, t=t, wb=wb: e.tensor_tensor(out=t, in0=t, in1=wb(66), op=ALU.add),
                             reads=["cols"], writes=["cv0", "cv1"])
                        S.op("act", lambda e, t=t, gv=gv: e.activation(out=gv, in_=t, func=AF.Gelu), reads=["cv0", "cv1"],
                             writes=["ge0", "ge1"])
                        ak = []
                        for c in range(c0, c0 + n):
                            ak += act_keys(c)
                        S.op("pool", lambda e, gv=gv, Uv=Uv, c0=c0, n=n: e.tensor_tensor(out=act[:, c0:c0 + n, 0:128], in0=gv, in1=Uv,
                                                                                       op=ALU.mult),
                             reads=["ge0", "ge1"] + gck, writes=sorted(set(ak)))
                if mode == "meta":
                    return
                banks = {}
                for w in wsbs:
                    for half in range(2):
                        banks[(w, half)] = next_bank()
                for c in range(NCH):
                    wd, wdk = wload(SL_D + c)
                    for w in wsbs:
                        for half in range(2):
                            pb = banks[(w, half)]
                            S.op("pe", lambda e, c=c, w=w, half=half, pb=pb, wd=wd: e.matmul(
                                bank(pb)[:, 0:512], lhsT=act[:, c, w * 128:(w + 1) * 128], rhs=wd[:, half * 512:(half + 1) * 512],
                                start=(c == 0), stop=(c == NCH - 1)), reads=[wdk] + act_keys(c), writes=["ps%d" % pb])
                fin = []
                for wi, w in enumerate(wsbs):
                    sl = xslot(gbase + w)
                    xa = xr[:, sl, :]
                    xk = ("x", sl)
                    fin.append((wi, w, sl, xa, xk))
                    for half in range(2):
                        pb = banks[(w, half)]
                        S.op("dve", lambda e, xa=xa, half=half, pb=pb: e.tensor_tensor(
                            out=xa[:, half * 512:(half + 1) * 512], in0=bank(pb)[:, 0:512], in1=xa[:, half * 512:(half + 1) * 512],
                            op=ALU.add), writes=["ps%d" % pb, xk])
                for (wi, w, sl, xa, xk) in fin:
                    S.op("act", lambda e, xa=xa, wi=wi: e.activation(out=junk, in_=xa, func=AF.Square, accum_out=ssF[:, wi:wi + 1]),
                         reads=[xk], writes=["junk", ("ssF", wi)])
                nw = len(fin)
                if nw:
                    sk = [("ssF", wi) for wi in range(nw)]
                    S.op("act", lambda e: e.activation(out=ssF[:, 0:nw], in_=ssF[:, 0:nw], func=AF.Ln, scale=1.0 / D, bias=epsc),
                         reads=["epsc"], writes=sk)
                    S.op("act", lambda e: e.activation(out=ssF[:, 0:nw], in_=ssF[:, 0:nw], func=AF.Exp, scale=-0.5), writes=sk)
                for (wi, w, sl, xa, xk) in fin:
                    S.op("dve", lambda e, xa=xa, wi=wi: e.scalar_tensor_tensor(
                        out=xa, in0=xa, scalar=ssF[:, wi:wi + 1], in1=gfin, op0=ALU.mult, op1=ALU.mult),
                        reads=[("ssF", wi), "gfin"], writes=[xk])
                    r0 = ybase + w * 128
                    dma(lambda e, xa=xa, r0=r0: e.dma_start(out=ydst[r0:r0 + 128, :], in_=xa), reads=[xk], key=("y", sl))

            def ffn_chunk_a(c, pG, pU, nwin):
                i3 = gctr[0] % 3
                i2 = gctr[0] % 2
                gctr[0] += 1
                gu, cvb, geb = GU[i3], cv[i2], ge[i2]
                kc, kg, ku, ck, ek = ("gu", i3, "c"), ("gu", i3, "g"), ("gu", i3, "u"), "cv%d" % i2, "ge%d" % i2
                S.op("pool", lambda e: e.tensor_copy(out=gu[:, :, 0:129], in_=GUc[:, c, :, 0:129]), reads=[("Gc", c)], writes=[kc])
                if pG is not None:
                    S.op("act", lambda e: e.activation(out=gu[:, 0, 129:129 + NT], in_=bank(pG)[:, 0:NT], func=AF.Copy),
                         writes=["ps%d" % pG, kg])
                    S.op("dve", lambda e: e.tensor_copy(out=gu[:, 1, 129:129 + NT], in_=bank(pU)[:, 0:NT]),
                         writes=["ps%d" % pU, ku])
                    S.op("pool", lambda e: e.tensor_copy(out=GUc[:, c, :, 0:129], in_=gu[:, :, NT:NT + 129]),
                         reads=[kc, kg, ku], writes=[("Gc", c)])
                else:
                    S.op("pool", lambda e: e.memset(gu[:, 0, 129:131], 0.0), writes=[kg])
                return (c, nwin, gu, cvb, geb, kc, kg, ku, ck, ek)

            def ffn_chunk_b(c, nwin, gu, cvb, geb, kc, kg, ku, ck, ek):
                w0, w1, w2, bb = col(c), col(22 + c), col(44 + c), col(66 + c)
                Gb = gu[:, 0, :]
                S.op("dve", lambda e: e.tensor_scalar(out=cvb[:, 0:nwin], in0=Gb[:, 0:nwin], scalar1=w0, scalar2=bb,
                                                      op0=ALU.mult, op1=ALU.add), reads=[kc, kg, "cols"], writes=[ck])
                S.op("dve", lambda e: e.scalar_tensor_tensor(out=cvb[:, 0:nwin], in0=Gb[:, 1:nwin + 1], scalar=w1, in1=cvb[:, 0:nwin],
                                                             op0=ALU.mult, op1=ALU.add), reads=[kc, kg, "cols"], writes=[ck])
                S.op("dve", lambda e: e.scalar_tensor_tensor(out=cvb[:, 0:nwin], in0=Gb[:, 2:nwin + 2], scalar=w2, in1=cvb[:, 0:nwin],
                                                             op0=ALU.mult, op1=ALU.add), reads=[kc, kg, "cols"], writes=[ck])
                S.op("act", lambda e: e.activation(out=geb[:, 0:nwin], in_=cvb[:, 0:nwin], func=AF.Gelu), reads=[ck], writes=[ek])
                S.op("pool", lambda e: e.tensor_tensor(out=act[:, c, 0:nwin], in0=geb[:, 0:nwin], in1=gu[:, 1, 1:nwin + 1], op=ALU.mult),
                     reads=[ek, kc, ku], writes=act_keys(c))

            def mixer_and_norm(kind, ti, srcs, ntok, qrows, nsbq, col0):
                nTb = ZnT
                norm_T(srcs, nTb, (lambda i: znt_keys), NT, ss, "ss", nb, junk, "act")
                nTkeys = znt_keys
                dma(lambda e: e.dma_start(out=tabs[:, :, 0:ntok], in_=tabs_d[:, :, col0:col0 + ntok].rearrange("a p n -> p a n")),
                    writes=["tabs"], key="tabs")
                wq = [wload(SL_Q + k) for k in range(8)]
                Pb = [Zf[:, i * 512:(i + 1) * 512].bitcast(BF16)[:, 0:NT] for i in range(2)]
                T2 = {"tA": Zf[:, 2 * 512:2 * 512 + NT], "tB": Zf[:, 3 * 512:3 * 512 + NT], "rq": Zf[:, 4 * 512:4 * 512 + NT],
                      "sqb": Zf[:, 5 * 512:6 * 512].bitcast(BF16)[:, 0:NT],
                      "keys": {"tA": ("Z", 2), "tB": ("Z", 3), "rq": ("Z", 4), "sqb": ("Z", 5)}}

                def q_finish(j, pP):
                    pR = next_bank()
                    pb, pbk = Pb[j % 2], ("Z", j % 2)
                    S.op("pe", lambda e: e.matmul(bank(pR)[:, 0:ntok], lhsT=rmat, rhs=pb[:, 0:ntok], start=True, stop=True),
                         reads=[pbk, "rmat"], writes=["ps%d" % pR])
                    qk_post("A" if j < 4 else "B", pP, pR, ntok, tabs,
                            [(QT[0:64, 2 * j, 0:ntok], 0, 64, ("QT", 2 * j)), (QT[64:128, 2 * j + 1, 0:ntok], 64, 128, ("QT", 2 * j + 1))],
                            104, 105, T if j % 2 == 0 else T2)
                prev = None
                for j in range(8):
                    pP = next_bank()
                    proj_fm([w[0] for w in wq], [w[1] for w in wq], j * 128, nTb, nTkeys, ntok, pP)
                    S.op("act", lambda e, j=j, pP=pP: e.activation(out=Pb[j % 2][:, 0:ntok], in_=bank(pP)[:, 0:ntok], func=AF.Copy),
                         writes=["ps%d" % pP, ("Z", j % 2)])
                    if prev is not None:
                        q_finish(*prev)
                    prev = (j, pP)
                q_finish(*prev)
                if kind == "real" and ti == 0:
                    dump("QT", QT, [("QT", j_) for j_ in range(16)])
                attention(ntok, qrows, nsbq)
                if kind == "real" and ti == 0:
                    dump("mix", mix, [("mix", j_) for j_ in range(8)])
                mTb, mTk = next_nT()
                for sb in range(nsbq):
                    pb = next_bank()
                    psb = bank(pb).bitcast(BF16)
                    for j in range(8):
                        S.op("pe", lambda e, j=j, sb=sb, psb=psb: e.transpose(
                            out=psb[:, j * 128:j * 128 + qrows], in_=mix[0:qrows, sb, j * 128:(j + 1) * 128],
                            identity=identb[0:qrows, 0:qrows]), reads=[("mix", j), "identb"], writes=["ps%d" % pb])
                    S.op("dve", lambda e, sb=sb, psb=psb: e.tensor_copy(
                        out=mTb[:, :, sb * 128:sb * 128 + qrows], in_=psb.rearrange("p (j t) -> p j t", j=8)[:, :, 0:qrows]),
                        writes=["ps%d" % pb, (mTk, sb)])
                wo = [wload(SL_O + j) for j in range(8)]
                for sb, (xa, xk, rows) in enumerate(srcs):
                    for half in range(2):
                        pb = next_bank()
                        for j in range(8):
                            S.op("pe", lambda e, j=j, sb=sb, half=half, pb=pb, rows=rows: e.matmul(
                                bank(pb)[0:rows, 0:512], lhsT=mTb[:, j, sb * 128:sb * 128 + rows],
                                rhs=wo[j][0][:, half * 512:(half + 1) * 512], start=(j == 0), stop=(j == 7)),
                                reads=[wo[j][1], (mTk, sb)], writes=["ps%d" % pb])
                        S.op("dve", lambda e, xa=xa, half=half, pb=pb, rows=rows: e.tensor_tensor(
                            out=xa[:, half * 512:(half + 1) * 512], in0=bank(pb)[0:rows, 0:512],
                            in1=xa[:, half * 512:(half + 1) * 512], op=ALU.add), writes=["ps%d" % pb, xk])
                if kind == "real" and ti == 0:
                    dump("h1", srcs[0][0], [srcs[0][1]])
                n2b, n2k = next_nT()
                norm_T(srcs, n2b, (lambda i, n2k=n2k: [(n2k, i)]), NT, ss, "ss", nb, junk, "act")
                if kind == "real" and ti == 0:
                    dump("n2T", n2b, [(n2k, i_) for i_ in range(len(srcs))])
                return n2b, [(n2k, i) for i in range(len(srcs))]

            dma(lambda e: e.dma_start(out=xm[0:NMETA, :], in_=meta_d[:, :]), writes=["xm"], key="xm")
            if PREFETCH_X:
                load_x_tile(0)
            n2b, n2keys = mixer_and_norm("meta", 0, [(xm[0:NMETA, :], "xm", NMETA)], NMETA, NMETA, 1, 0)
            ffn("meta", n2b, n2keys, NMETA, [], 0, 0)
            S.barrier()
            for ti in range(ntile):
                if PREFETCH_X:
                    if ti + 1 < ntile:
                        load_x_tile(ti + 1)
                else:
                    load_x_tile(ti)
                srcs = [(xr[:, xslot(ti * nsb + s), :], ("x", xslot(ti * nsb + s)), 128) for s in range(nsb)]
                n2b, n2keys = mixer_and_norm("real", ti, srcs, NT, 128, nsb, NMETA + ti * NT)
                wsbs = list(range(nsb)) if ti > 0 else list(range(1, nsb))
                ffn("real", n2b, n2keys, NT, wsbs, ti * nsb - 1, ti * NT - 128)
            ffn("flush", None, None, 0, [0], ntile * nsb - 1, SEQ - 128)

        seen = set()
        for sidx, (kind, i, NT) in enumerate(seq_cfg):
            init = (kind, NT) not in seen
            seen.add((kind, NT))
            if kind == "p":
                do_sequence(sidx, xp, yp, SEQ_P, NT, init)
            else:
                do_sequence(sidx, xs[i], ys[i], SEQ_S, NT, init)

        S.emit(nc, st)
    return nc


def _rope_tables():
    f32 = np.float32
    theta = f32(10000.0)
    t = np.arange(SEQ_P)
    row = (t // 64).astype(f32)
    colp = (t % 64).astype(f32)
    inv16 = (theta ** (-(np.arange(0, 32, 2, dtype=f32)) / f32(32))).astype(f32)
    ang = np.concatenate([row[:, None] * inv16[None], colp[:, None] * inv16[None]], axis=-1).astype(f32)
    ang_a = np.concatenate([np.zeros((NMETA, 32), f32), ang], axis=0)
    pos = np.arange(LMAX, dtype=f32)
    inv32 = (theta ** (-(np.arange(0, 64, 2, dtype=f32)) / f32(64))).astype(f32)
    ang_b = (pos[:, None] * inv32[None]).astype(f32)
    tabs = np.empty((4, 128, LMAX), f32)
    for a, src in enumerate((np.cos(ang_a), np.sin(ang_a), np.cos(ang_b), np.sin(ang_b))):
        tabs[a] = np.tile(src.T.astype(f32), (4, 1))
    return tabs


def _rot_matrix():
    r = np.zeros((128, 128), np.float32)
    for m in range(128):
        h, i = divmod(m, 64)
        if i < 32:
            r[h * 64 + i + 32, m] = -1.0
        else:
            r[h * 64 + i - 32, m] = 1.0
    return r


_CACHE = {}
DBG_ON = False


def kernel(x_prompt, x_sample, meta_tokens, g_mix, w_in, g_qnorm_a, g_knorm_a, lambda_q1, lambda_k1,
           lambda_q2, lambda_k2, g_subln, w_out, g_ffn, w_ff_gate, w_ff_up, conv_w, conv_b, w_ff_down, g_final):
    seq_cfg = [("p", 0, 256)] + [("s", i, 512) for i in range(NSAMP)]
    if "nc" not in _CACHE:
        _CACHE["nc"] = build_program(seq_cfg)
    nc = _CACHE["nc"]
    f = _f
    shared = make_shared(meta_tokens, g_mix, w_in, g_qnorm_a, g_knorm_a, lambda_q1, lambda_k1, lambda_q2, lambda_k2,
                         g_subln, w_out, g_ffn, w_ff_gate, w_ff_up, conv_w, conv_b, w_ff_down, g_final)
    xpf, xsf = f(x_prompt), f(x_sample)
    in_maps = []
    for c in range(8):
        m = dict(shared)
        m["xp"] = xpf[c]
        m["xs"] = xsf[c * NSAMP:(c + 1) * NSAMP]
        in_maps.append(m)
    res = run_bass_kernel_spmd(nc, in_maps, core_ids=list(range(8)))
    y_prompt = np.stack([np.asarray(res.results[c]["yp"], dtype=np.float32) for c in range(8)], axis=0)
    y_sample = np.concatenate([np.asarray(res.results[c]["ys"], dtype=np.float32) for c in range(8)], axis=0)
    return (y_prompt, y_sample)


def _f(a):
    return np.ascontiguousarray(np.asarray(a, dtype=np.float32))


def make_shared(meta_tokens, g_mix, w_in, g_qnorm_a, g_knorm_a, lambda_q1, lambda_k1, lambda_q2, lambda_k2,
                g_subln, w_out, g_ffn, w_ff_gate, w_ff_up, conv_w, conv_b, w_ff_down, g_final):
    f = _f
    if "tabs" not in _CACHE:
        _CACHE["tabs"] = _rope_tables()
    onesbd = np.zeros((128, 128), np.float32)
    onesbd[:64, :64] = 1.0
    onesbd[64:, 64:] = 1.0
    return {
        "meta": f(meta_tokens), "w_in": f(w_in[0]), "w_out": f(w_out[0]), "w_gate": f(w_ff_gate[0]), "w_up": f(w_ff_up[0]),
        "w_down": f(w_ff_down[0]), "g_mix": f(g_mix[0]).reshape(8, 128), "g_ffn": f(g_ffn[0]).reshape(8, 128),
        "gq": f(g_qnorm_a[0]).reshape(1, 64), "gk": f(g_knorm_a[0]).reshape(1, 64),
        "lam": np.stack([f(lambda_q1[0]), f(lambda_k1[0]), f(lambda_q2[0]), f(lambda_k2[0])], axis=0),
        "gsub": f(g_subln[0]).reshape(1, 128), "convw": f(conv_w[0]).reshape(66, 128), "convb": f(conv_b[0]).reshape(22, 128),
        "gfin": f(g_final).reshape(1, D), "tabs": _CACHE["tabs"], "identf": np.eye(128, dtype=np.float32), "onesbd": onesbd,
        "rmat": _rot_matrix(),
    }
```

```python
import math
from contextlib import ExitStack
import numpy as np
import concourse.bass as bass
import concourse.mybir as mybir
from concourse.bass_utils import run_bass_kernel_spmd

F32 = mybir.dt.float32
BF16 = mybir.dt.bfloat16
ALU = mybir.AluOpType
AF = mybir.ActivationFunctionType
AX = mybir.AxisListType

D = 1024
NMETA = 16
DFF = 2816
NCH = 22
EPS = 1e-6
LAM_INIT = 0.8 - 0.6 * math.exp(0.0)
SEQ_P = 4096
SEQ_S = 2048
NSAMP = 4
LMAX = SEQ_P + NMETA
ARENA_BYTES = 206 * 1024
NSLOT = 106
SL_A0, SL_A1, SL_Q, SL_QR, SL_O, SL_G, SL_U, SL_D = 0, 8, 16, 24, 32, 40, 62, 84

ENGS = ("pe", "act", "dve", "pool", "sp")


class Op:
    __slots__ = ("eng", "fn", "deps", "signal", "count", "dma_key", "dma_n", "is_dma")

    def __init__(self, eng, fn, is_dma, dma_key, dma_n):
        self.eng = eng
        self.fn = fn
        self.deps = []
        self.signal = False
        self.count = 0
        self.is_dma = is_dma
        self.dma_key = dma_key
        self.dma_n = dma_n


class Sched:
    def __init__(self):
        self.ops = {e: [] for e in ENGS}
        self.last_writer = {}
        self.readers = {}
        self.bar = []
        self.bar_pending = {e: False for e in ENGS}
        self.last_dma = {}

    def op(self, eng, fn, reads=(), writes=(), dma_key=None, dma_n=1):
        is_dma = dma_key is not None
        o = Op(eng, fn, is_dma, dma_key, dma_n)
        deps = {}
        lw = self.last_writer
        for b in reads:
            w = lw.get(b)
            if w is not None:
                deps[id(w)] = w
        for b in writes:
            w = lw.get(b)
            if w is not None:
                deps[id(w)] = w
            rs = self.readers.get(b)
            if rs:
                for r in rs:
                    deps[id(r)] = r
        if self.bar_pending[eng]:
            self.bar_pending[eng] = False
            for d in self.bar:
                deps[id(d)] = d
        for d in deps.values():
            if (not d.is_dma) and d.eng == "pe" and eng == "pe" and not is_dma:
                continue
            d.signal = True
            o.deps.append(d)
        for b in writes:
            lw[b] = o
            self.readers[b] = []
        for b in reads:
            rs = self.readers.setdefault(b, [])
            if not is_dma:
                for i, r in enumerate(rs):
                    if (not r.is_dma) and r.eng == eng:
                        rs[i] = o
                        break
                else:
                    rs.append(o)
            else:
                rs.append(o)
        self.ops[eng].append(o)
        if is_dma:
            self.last_dma[dma_key] = o
        return o

    def barrier(self):
        bar = []
        for e in ENGS:
            for o in reversed(self.ops[e]):
                if not o.is_dma:
                    bar.append(o)
                    break
        bar.extend(self.last_dma.values())
        self.bar = bar
        self.bar_pending = {e: True for e in ENGS}
        self.last_writer = {}
        self.readers = {}

    def emit(self, nc, stack):
        dma_cnt = {}
        for e in ENGS:
            c = 0
            for o in self.ops[e]:
                if o.is_dma:
                    dma_cnt[o.dma_key] = dma_cnt.get(o.dma_key, 0) + 16 * o.dma_n
                    o.count = dma_cnt[o.dma_key]
                elif o.signal:
                    c += 1
                    o.count = c
        esem = {e: stack.enter_context(nc.semaphore("s_" + e)) for e in ENGS if e != "sp"}
        dsem = {k: stack.enter_context(nc.semaphore("d_%d" % i)) for i, k in enumerate(dma_cnt)}
        block = stack.enter_context(nc.Block())

        def run(e):
            def body(eng):
                known = {}
                for o in self.ops[e]:
                    need = {}
                    for d in o.deps:
                        key = ("d", d.dma_key) if d.is_dma else ("e", d.eng)
                        if d.count > need.get(key, 0):
                            need[key] = d.count
                    for key, v in need.items():
                        if known.get(key, 0) >= v:
                            continue
                        known[key] = v
                        sem = dsem[key[1]] if key[0] == "d" else esem[key[1]]
                        eng.wait_ge(sem, v)
                    r = o.fn(eng)
                    if o.is_dma:
                        if not isinstance(r, (list, tuple)):
                            r = [r]
                        assert len(r) == o.dma_n
                        for ins in r:
                            ins.then_inc(dsem[o.dma_key], 16)
                    elif o.signal:
                        r.then_inc(esem[e], 1)
                last = {}
                for o in self.ops[e]:
                    if o.is_dma:
                        last[o.dma_key] = o.count
                for k, v in last.items():
                    if known.get(("d", k), 0) < v:
                        eng.wait_ge(dsem[k], v)
            return body

        block.tensor(run("pe"))
        block.scalar(run("act"))
        block.vector(run("dve"))
        block.gpsimd(run("pool"))
        block.sync(run("sp"))


class Arena:
    def __init__(self, ap, base=0):
        self.ap = ap
        self.off = base

    def f32(self, n):
        self.off = (self.off + 31) // 32 * 32
        a = self.ap[:, self.off // 4:self.off // 4 + n]
        self.off += 4 * n
        assert self.off <= ARENA_BYTES, self.off
        return a

    def bf16(self, n):
        n2 = (n + 1) // 2
        return self.f32(n2).bitcast(BF16)[:, 0:n]


def build_program(seq_cfg):
    nc = bass.Bass("TRN2", target_bir_lowering=False)

    def din(name, shape):
        return nc.dram_tensor(name, list(shape), F32, kind="ExternalInput").ap()

    xp = din("xp", [SEQ_P, D])
    xs = din("xs", [NSAMP, SEQ_S, D])
    meta_d = din("meta", [NMETA, D])
    w_in_d = din("w_in", [D, 2304])
    w_out_d = din("w_out", [D, D])
    w_gate_d = din("w_gate", [D, DFF])
    w_up_d = din("w_up", [D, DFF])
    w_down_d = din("w_down", [DFF, D])
    g_mix_d = din("g_mix", [8, 128])
    g_ffn_d = din("g_ffn", [8, 128])
    gq_d = din("gq", [1, 64])
    gk_d = din("gk", [1, 64])
    lam_d = din("lam", [4, 64])
    gsub_d = din("gsub", [1, 128])
    convw_d = din("convw", [66, 128])
    convb_d = din("convb", [22, 128])
    gfin_d = din("gfin", [1, D])
    tabs_d = din("tabs", [4, 128, LMAX])
    identf_d = din("identf", [128, 128])
    onesbd_d = din("onesbd", [128, 128])
    rmat_d = din("rmat", [128, 128])
    yp = nc.dram_tensor("yp", [SEQ_P, D], F32, kind="ExternalOutput").ap()
    ys = nc.dram_tensor("ys", [NSAMP, SEQ_S, D], F32, kind="ExternalOutput").ap()
    W = nc.dram_tensor("wscr", [NSLOT, 128, 1024], BF16, kind="Internal").ap()

    S = Sched()
    dcnt = [0]

    def dump(name, ap, keys):
        if not DBG_ON:
            return
        dcnt[0] += 1
        t = nc.dram_tensor("dbg_" + name, list(ap.shape), ap.dtype, kind="ExternalOutput").ap()
        S.op("sp", lambda e: e.dma_start(out=t, in_=ap), reads=keys, dma_key=("dbg", dcnt[0]))
    with ExitStack() as st:
        arena_t = st.enter_context(nc.sbuf_tensor("arena", [128, ARENA_BYTES // 4], F32))
        psum_t = st.enter_context(nc.psum_tensor("psum", [128, 4096], F32))
        PS = psum_t[:]
        AR = Arena(arena_t[:])

        def bank(b):
            return PS[:, b * 512:(b + 1) * 512]

        bank_ctr = [0]

        def next_bank():
            b = bank_ctr[0] % 8
            bank_ctr[0] += 1
            return b

        identb = AR.bf16(128)
        onesbd = AR.bf16(128)
        rmat = AR.bf16(128)
        cols = AR.f32(128)
        ncols = AR.f32(128)
        gfin = AR.f32(D)
        gsub = AR.f32(128)
        lams = AR.f32(8)
        epsc = AR.f32(1)
        CONST_END = AR.off
        identf = AR.f32(128)
        onesf = AR.f32(128)
        lamt = AR.f32(256)
        stg = AR.f32(128)
        SETUP_END = AR.off

        def dma(fn, reads=(), writes=(), key=None, n=1):
            S.op("sp", fn, reads=reads, writes=writes, dma_key=key, dma_n=n)

        dma(lambda e: e.dma_start(out=identf, in_=identf_d[:, :]), writes=["identf"], key="c_identf")
        dma(lambda e: e.dma_start(out=onesf, in_=onesbd_d[:, :]), writes=["onesf"], key="c_onesf")
        dma(lambda e: e.dma_start(out=gfin, in_=gfin_d[0:1, :].partition_broadcast(128)), writes=["gfin"], key="c_gfin")
        dma(lambda e: e.dma_start(out=gsub, in_=gsub_d[0:1, :].partition_broadcast(128)), writes=["gsub"], key="c_gsub")
        dma(lambda e: [e.dma_start(out=lamt[:, i * 64:(i + 1) * 64], in_=lam_d[i:i + 1, :].partition_broadcast(128))
                       for i in range(4)], writes=["lamt"], key="c_lamt", n=4)
        S.op("pool", lambda e: e.memset(stg, 0.0), writes=["stg"])
        S.op("pool", lambda e: e.memset(epsc, EPS), writes=["epsc"])

        def stg_loads(e):
            r = []
            r.append(e.dma_start(out=stg[0:66, :], in_=convw_d[:, :]))
            r.append(e.dma_start(out=stg[66:88, :], in_=convb_d[:, :]))
            r.append(e.dma_start(out=stg[88:96, :], in_=g_mix_d[:, :]))
            r.append(e.dma_start(out=stg[96:104, :], in_=g_ffn_d[:, :]))
            for row, src in ((104, gq_d), (106, gk_d)):
                for h in range(2):
                    r.append(e.dma_start(out=stg[row:row + 1, h * 64:(h + 1) * 64], in_=src[0:1, :]))
                    r.append(e.dma_start(out=stg[row + 1:row + 2, h * 64:h * 64 + 32], in_=src[0:1, 32:64]))
                    r.append(e.dma_start(out=stg[row + 1:row + 2, h * 64 + 32:h * 64 + 64], in_=src[0:1, 0:32]))
            return r
        dma(stg_loads, reads=[], writes=["stg"], key="c_stg", n=16)
        S.op("dve", lambda e: e.tensor_copy(out=identb, in_=identf), reads=["identf"], writes=["identb"])
        S.op("dve", lambda e: e.tensor_copy(out=onesbd, in_=onesf), reads=["onesf"], writes=["onesbd"])
        dma(lambda e: e.dma_start(out=onesf, in_=rmat_d[:, :]), reads=["onesbd"], writes=["onesf"], key="c_onesf")
        S.op("dve", lambda e: e.tensor_copy(out=rmat, in_=onesf), reads=["onesf"], writes=["rmat"])
        S.op("pe", lambda e: e.transpose(out=bank(0)[:, 0:128], in_=stg, identity=identf), reads=["stg", "identf"], writes=["ps0"])
        S.op("dve", lambda e: e.tensor_copy(out=cols, in_=bank(0)[:, 0:128]), writes=["ps0", "cols"])
        S.op("dve", lambda e: e.scalar_tensor_tensor(out=lamt[:, 0:64], in0=lamt[:, 0:64], scalar=1.0, in1=lamt[:, 64:128],
                                                      op0=ALU.mult, op1=ALU.mult, accum_out=lams[:, 0:1]),
             reads=["lamt"], writes=["lamt", "lams"])
        S.op("dve", lambda e: e.scalar_tensor_tensor(out=lamt[:, 128:192], in0=lamt[:, 128:192], scalar=1.0, in1=lamt[:, 192:256],
                                                      op0=ALU.mult, op1=ALU.mult, accum_out=lams[:, 1:2]),
             reads=["lamt", "lams"], writes=["lamt", "lams"])
        S.op("act", lambda e: e.activation(out=lams[:, 2:4], in_=lams[:, 0:2], func=AF.Exp), reads=["lams"], writes=["lams"])
        S.op("dve", lambda e: e.tensor_tensor(out=lams[:, 4:5], in0=lams[:, 2:3], in1=lams[:, 3:4], op=ALU.subtract),
             reads=["lams"], writes=["lams"])
        S.op("dve", lambda e: e.tensor_scalar(out=lams[:, 5:6], in0=lams[:, 4:5], scalar1=LAM_INIT, scalar2=-1.0,
                                               op0=ALU.add, op1=ALU.mult), reads=["lams"], writes=["lams"])
        neglam = lams[:, 5:6]
        dump("cols", cols, ["cols"])
        dump("lams", lams, ["lams"])

        def col(i):
            return cols[:, i:i + 1]

        S.op("dve", lambda e: e.tensor_scalar(out=ncols, in0=cols, scalar1=-1.0, scalar2=None, op0=ALU.mult),
             reads=["cols"], writes=["cols"])

        PA = Arena(arena_t[:], SETUP_END)
        NPB = 4
        wf = [PA.f32(DFF) for _ in range(NPB)]
        wb = [PA.bf16(4096) for _ in range(NPB)]
        pe_rr = [0]

        def ew(dst, src, gi, neg, rk, wk):
            eng = ("dve", "act")[pe_rr[0] % 2]
            pe_rr[0] += 1
            if gi is None:
                if eng == "dve":
                    S.op("dve", lambda e: e.tensor_copy(out=dst, in_=src), reads=[rk], writes=[wk])
                else:
                    S.op("act", lambda e: e.activation(out=dst, in_=src, func=AF.Copy), reads=[rk], writes=[wk])
                return
            g = (ncols if neg else cols)[:, gi:gi + 1]
            if eng == "dve":
                S.op("dve", lambda e: e.tensor_scalar(out=dst, in0=src, scalar1=g, scalar2=None, op0=ALU.mult),
                     reads=[rk, "cols"], writes=[wk])
            else:
                S.op("act", lambda e: e.activation(out=dst, in_=src, func=AF.Copy, scale=g), reads=[rk, "cols"], writes=[wk])

        pidx = [0]

        def prep_next():
            b = pidx[0] % NPB
            pidx[0] += 1
            return b

        def cp(dst, src, g, rk, wk):
            ew(dst, src, g, False, rk, wk)

        def cpneg(dst, src, g, rk, wk):
            ew(dst, src, g, True, rk, wk)

        def rot(dst, src, g, nh, rk, wk):
            dv = dst.rearrange("p (h t e) -> p h t e", h=nh, t=2)
            sv = src.rearrange("p (h t e) -> p h t e", h=nh, t=2)
            cpneg(dv[:, :, 0, :], sv[:, :, 1, :], g, rk, wk)
            cp(dv[:, :, 1, :], sv[:, :, 0, :], g, rk, wk)

        for k in range(8):
            b = prep_next()
            f, o = wf[b], wb[b]
            fk, ok = "wf%d" % b, "wb%d" % b
            dma(lambda e, f=f, k=k: e.dma_start(out=f[:, 0:2304], in_=w_in_d[k * 128:(k + 1) * 128, :]), writes=[fk], key=fk)
            g = 88 + k
            A0, A1, Q, QR = (o[:, i * 1024:(i + 1) * 1024] for i in range(4))
            cp(A0[:, 0:128], f[:, 512:640], g, fk, ok)
            cp(A0[:, 256:384], f[:, 640:768], g, fk, ok)
            cp(A0[:, 384:896], f[:, 1792:2304], g, fk, ok)
            cp(A0[:, 896:1024], f[:, 0:128], g, fk, ok)
            cp(A1[:, 0:512], f[:, 1280:1792], g, fk, ok)
            qd = Q[:, 0:512].rearrange("p (j t d) -> p j t d", j=4, t=2)
            qs = f[:, 0:512].rearrange("p (t j d) -> p j t d", t=2, j=4)
            for t in range(2):
                cp(qd[:, :, t, :], qs[:, :, t, :], g, fk, ok)
            cp(Q[:, 512:1024], f[:, 768:1280], g, fk, ok)
            dma(lambda e, o=o, k=k: [e.dma_start(out=W[base + k, :, :], in_=o[:, i * 1024:(i + 1) * 1024])
                                     for i, base in enumerate((SL_A0, SL_A1, SL_Q))],
                reads=[ok], writes=["W"], key=ok, n=3)
        S2_BASE = ARENA_BYTES - (2 * DFF * 4 + DFF * 2)
        PA2 = Arena(arena_t[:], S2_BASE)
        wf2 = [PA2.f32(DFF) for _ in range(2)]
        wb2 = PA2.bf16(DFF)
        prep_steps = []
        p2 = [0]

        def add_step(src_ap, ncols, gi, store_fn):
            def step():
                b = p2[0] % 2
                p2[0] += 1
                f, o = wf2[b], wb2
                fk, ok = "wf2_%d" % b, "wb2"
                dma(lambda e: e.dma_start(out=f[:, 0:ncols], in_=src_ap), writes=[fk], key=fk)
                ew(o[:, 0:ncols], f[:, 0:ncols], gi, False, fk, ok)
                dma(lambda e: store_fn(e, o), reads=[ok], writes=["W"], key=ok)
            prep_steps.append(step)

        for j in range(8):
            add_step(w_out_d[j * 128:(j + 1) * 128, :], 1024, None,
                     lambda e, o, j=j: e.dma_start(out=W[SL_O + j, :, :], in_=o[:, 0:1024]))
        for (wd, base) in ((w_gate_d, SL_G), (w_up_d, SL_U)):
            for k in range(8):
                add_step(wd[k * 128:(k + 1) * 128, :], DFF, 96 + k,
                         lambda e, o, k=k, base=base: e.dma_start(
                             out=W[base:base + NCH, :, k * 128:(k + 1) * 128].rearrange("c p j -> p c j"),
                             in_=o[:, 0:DFF].rearrange("p (c j) -> p c j", j=128)))
        for c in range(NCH):
            add_step(w_down_d[c * 128:(c + 1) * 128, :], 1024, None,
                     lambda e, o, c=c: e.dma_start(out=W[SL_D + c, :, :], in_=o[:, 0:1024]))

        def rstd_from_ss(ss_ap, n, scale, key, rows=128):
            S.op("act", lambda e: e.activation(out=ss_ap, in_=ss_ap, func=AF.Ln, scale=scale, bias=epsc[0:rows, :]),
                 reads=[key, "epsc"], writes=[key])
            S.op("act", lambda e: e.activation(out=ss_ap, in_=ss_ap, func=AF.Exp, scale=-0.5), reads=[key], writes=[key])

        def norm_T(srcs, dstT, dkeyf, NTt, ss, sskey, nb, junk, sq_eng):
            n = len(srcs)
            for i, (xa, xk, rows) in enumerate(srcs):
                if sq_eng == "act":
                    S.op("act", lambda e, xa=xa, i=i, rows=rows: e.activation(out=junk[0:rows, :], in_=xa, func=AF.Square,
                                                                               accum_out=ss[0:rows, i:i + 1]),
                         reads=[xk], writes=["junk", sskey])
                else:
                    S.op("dve", lambda e, xa=xa, i=i, rows=rows: e.scalar_tensor_tensor(
                        out=junk[0:rows, :], in0=xa, scalar=1.0, in1=xa, op0=ALU.mult, op1=ALU.mult,
                        accum_out=ss[0:rows, i:i + 1]), reads=[xk], writes=["junk", sskey])
            rows0 = srcs[0][2]
            rstd_from_ss(ss[0:rows0, 0:n], n, 1.0 / D, sskey, rows0)
            for i, (xa, xk, rows) in enumerate(srcs):
                nbi = nb[i % len(nb)]
                nk = "nb%d" % (i % len(nb))
                S.op("dve", lambda e, xa=xa, i=i, rows=rows, nbi=nbi: e.tensor_scalar(
                    out=nbi[0:rows, :], in0=xa, scalar1=ss[0:rows, i:i + 1], scalar2=None, op0=ALU.mult),
                    reads=[xk, sskey], writes=[nk])
                pb = next_bank()
                psb = bank(pb).bitcast(BF16)
                for j in range(8):
                    S.op("pe", lambda e, j=j, rows=rows, nbi=nbi, psb=psb: e.transpose(
                        out=psb[:, j * 128:j * 128 + rows], in_=nbi[0:rows, j * 128:(j + 1) * 128],
                        identity=identb[0:rows, 0:rows]), reads=[nk, "identb"], writes=["ps%d" % pb])
                ev = "act" if sq_eng == "act" else "dve"
                src_v = psb.rearrange("p (j t) -> p j t", j=8)[:, :, 0:rows]
                dst_v = dstT[:, :, i * 128:i * 128 + rows]
                if ev == "act":
                    S.op("act", lambda e, src_v=src_v, dst_v=dst_v: e.activation(out=dst_v, in_=src_v, func=AF.Copy),
                         writes=["ps%d" % pb] + dkeyf(i))
                else:
                    S.op("dve", lambda e, src_v=src_v, dst_v=dst_v: e.tensor_copy(out=dst_v, in_=src_v),
                         writes=["ps%d" % pb] + dkeyf(i))

        def proj_fm(wslots, wkeys, c0, srcT, skeys, ntok, pb, col_off=0):
            for k in range(8):
                S.op("pe", lambda e, k=k: e.matmul(bank(pb)[:, col_off:col_off + ntok], lhsT=wslots[k][:, c0:c0 + 128],
                                                   rhs=srcT[:, k, 0:ntok], start=(k == 0), stop=(k == 7)),
                     reads=[wkeys[k]] + skeys, writes=["ps%d" % pb])

        def qk_post(kind, pP, pR, ntok, tabs, dsts, gcol, grcol, T):
            P = bank(pP)[:, 0:ntok]
            R = bank(pR)[:, 0:ntok]
            tA, tB, sqb, rq = T["tA"], T["tB"], T["sqb"], T["rq"]
            kk = T.get("keys", {})
            ktA, ktB, ksq, krq = kk.get("tA", "tA"), kk.get("tB", "tB"), kk.get("sqb", "sqb"), kk.get("rq", "rq")
            if kind == "A":
                cosT, sinT = tabs[:, 0, 0:ntok], tabs[:, 1, 0:ntok]
                S.op("act", lambda e: e.activation(out=sqb[:, 0:ntok], in_=P, func=AF.Square), writes=["ps%d" % pP, ksq])
                pS = next_bank()
                S.op("pe", lambda e: e.matmul(bank(pS)[:, 0:ntok], lhsT=onesbd, rhs=sqb[:, 0:ntok], start=True, stop=True),
                     reads=[ksq, "onesbd"], writes=["ps%d" % pS])
                S.op("act", lambda e: e.activation(out=rq[:, 0:ntok], in_=bank(pS)[:, 0:ntok], func=AF.Ln, scale=1.0 / 64,
                                                   bias=epsc), reads=["epsc"], writes=["ps%d" % pS, krq])
                S.op("act", lambda e: e.activation(out=rq[:, 0:ntok], in_=rq[:, 0:ntok], func=AF.Exp, scale=-0.5),
                     reads=[krq], writes=[krq])
                S.op("dve", lambda e: e.scalar_tensor_tensor(out=tA[:, 0:ntok], in0=P, scalar=col(gcol), in1=cosT,
                                                              op0=ALU.mult, op1=ALU.mult),
                     reads=["tabs", "cols"], writes=["ps%d" % pP, ktA])
                S.op("dve", lambda e: e.scalar_tensor_tensor(out=tB[:, 0:ntok], in0=R, scalar=col(grcol), in1=sinT,
                                                              op0=ALU.mult, op1=ALU.mult),
                     reads=["tabs", "cols"], writes=["ps%d" % pR, ktB])
                S.op("pool", lambda e: e.tensor_tensor(out=tA[:, 0:ntok], in0=tA[:, 0:ntok], in1=tB[:, 0:ntok], op=ALU.add),
                     reads=[ktB], writes=[ktA])
                for (dst, p0, p1, dkey) in dsts:
                    S.op("pool", lambda e, dst=dst, p0=p0, p1=p1: e.tensor_tensor(out=dst, in0=tA[p0:p1, 0:ntok], in1=rq[p0:p1, 0:ntok],
                                                                                  op=ALU.mult), reads=[ktA, krq], writes=[dkey])
            else:
                cosT, sinT = tabs[:, 2, 0:ntok], tabs[:, 3, 0:ntok]
                S.op("dve", lambda e: e.tensor_tensor(out=tA[:, 0:ntok], in0=P, in1=cosT, op=ALU.mult),
                     reads=["tabs"], writes=["ps%d" % pP, ktA])
                S.op("dve", lambda e: e.tensor_tensor(out=tB[:, 0:ntok], in0=R, in1=sinT, op=ALU.mult),
                     reads=["tabs"], writes=["ps%d" % pR, ktB])
                for (dst, p0, p1, dkey) in dsts:
                    S.op("pool", lambda e, dst=dst, p0=p0, p1=p1: e.tensor_tensor(out=dst, in0=tA[p0:p1, 0:ntok], in1=tB[p0:p1, 0:ntok],
                                                                                  op=ALU.add), reads=[ktA, ktB], writes=[dkey])

        def do_sequence(sidx, xsrc, ydst, SEQ, NT, init):
            L = SEQ + NMETA
            Lp = L + (L % 2)
            nsb = NT // 128
            ntile = SEQ // NT
            nchunk = 1 + SEQ // 128
            sid = "q%d" % sidx

            KT, VA, VB, KV_END = phase_A(xsrc, SEQ, 512, L, Lp, 4, SEQ // 512, nchunk, init)
            phase_B(xsrc, ydst, SEQ, NT, L, Lp, nsb, ntile, nchunk, KT, VA, VB, KV_END, init)

        def phase_A(xsrc, SEQ, NT, L, Lp, nsb, ntile, nchunk, init):
            S.barrier()
            A = Arena(arena_t[:], CONST_END)
            KT = A.bf16(5 * Lp).rearrange("p (j l) -> p j l", j=5)
            VAf = A.bf16(nchunk * 2 * 72)
            VBf = A.bf16(nchunk * 4 * 136)
            VA = VAf.rearrange("p (c h e) -> p c h e", c=nchunk, h=2)
            VB = VBf.rearrange("p (c h e) -> p c h e", c=nchunk, h=4)
            KV_END = A.off
            wA = A.bf16(16 * 1024).rearrange("p (s c) -> p s c", s=16)
            xr = A.f32(4 * D).rearrange("p (s c) -> p s c", s=4)
            nT = [A.bf16(8 * NT).rearrange("p (k t) -> p k t", k=8) for _ in range(2)]
            nb = [A.bf16(D) for _ in range(2)]
            junk = A.bf16(D)
            tabs = A.f32(4 * NT).rearrange("p (a t) -> p a t", a=4)
            T = {"tA": A.f32(NT), "tB": A.f32(NT), "sqb": A.bf16(NT), "rq": A.f32(NT)}
            ss = A.f32(8)
            PbA = [A.bf16(NT) for _ in range(2)]
            TA2 = None
            if A.off + 14 * NT + 256 <= (S2_BASE if prep_steps else ARENA_BYTES):
                TA2 = {"tA": A.f32(NT), "tB": A.f32(NT), "sqb": A.bf16(NT), "rq": A.f32(NT),
                       "keys": {"tA": "tA2", "tB": "tB2", "rq": "rq2", "sqb": "sqb2"}}
            if prep_steps:
                assert A.off <= S2_BASE, (A.off, S2_BASE)
            per_tile = (len(prep_steps) + ntile) // (ntile + 1) if prep_steps else 0

            if init:
                S.op("pool", lambda e: e.memset(VAf, 1.0), writes=["VAall"])
                S.op("pool", lambda e: e.memset(VBf, 1.0), writes=["VBall"])
            for s in range(16):
                dma(lambda e, s=s: e.dma_start(out=wA[:, s, :], in_=W[s, :, :]), reads=["W"], writes=[("wA", s)], key=("wA", s))
            wA0 = [wA[:, k, :] for k in range(8)]
            wA1 = [wA[:, 8 + k, :] for k in range(8)]
            kA0 = [("wA", k) for k in range(8)]
            kA1 = [("wA", 8 + k) for k in range(8)]

            xctr = [0]
            tiles = [("meta", 0)] + [("real", i) for i in range(ntile)]

            def prepare(kind, ti):
                if kind == "meta":
                    ntok, col0 = NMETA, 0
                    sbs = [(meta_d[:, :], NMETA, 0)]
                else:
                    ntok, col0 = NT, NMETA + ti * NT
                    sbs = [(xsrc[ti * NT + s * 128: ti * NT + (s + 1) * 128, :], 128, 1 + ti * nsb + s) for s in range(nsb)]
                srcs = []
                for (src, rows, chunk) in sbs:
                    sl = xctr[0] % 4
                    xctr[0] += 1
                    dma(lambda e, sl=sl, src=src, rows=rows: e.dma_start(out=xr[0:rows, sl, :], in_=src),
                        writes=[("x", sl)], key=("x", sl))
                    srcs.append((xr[0:rows, sl, :], ("x", sl), rows))
                nTb = nT[(ti + 1) % 2 if kind == "real" else 0]
                nTk = "nT%d" % ((ti + 1) % 2 if kind == "real" else 0)
                norm_T(srcs, nTb, (lambda i, nTk=nTk: [(nTk, i)]), NT, ss, "ss", nb, junk, "act")
                nTkeys = [(nTk, i) for i in range(len(srcs))]
                return (kind, ti, ntok, col0, sbs, nTb, nTk, nTkeys)

            def process(kind, ti, ntok, col0, sbs, nTb, nTk, nTkeys):
                dma(lambda e, col0=col0, ntok=ntok: e.dma_start(
                    out=tabs[:, :, 0:ntok], in_=tabs_d[:, :, col0:col0 + ntok].rearrange("a p n -> p a n")),
                    writes=["tabs"], key="tabs")
                def k_finish(jt, pP, kind=kind, ti=ti, ntok=ntok, col0=col0):
                    pR = next_bank()
                    pb, pbk = PbA[jt % 2], "pbA%d" % (jt % 2)
                    S.op("pe", lambda e: e.matmul(bank(pR)[:, 0:ntok], lhsT=rmat, rhs=pb[:, 0:ntok], start=True, stop=True),
                         reads=[pbk, "rmat"], writes=["ps%d" % pR])
                    qk_post("A" if jt == 0 else "B", pP, pR, ntok, tabs,
                            [(KT[:, jt, col0:col0 + ntok], 0, 128, ("K", kind, ti))], 106, 107,
                            T if (jt % 2 == 0 or TA2 is None) else TA2)
                prev = None
                for jt in range(5):
                    pP = next_bank()
                    if jt == 0:
                        proj_fm(wA0, kA0, 0, nTb, nTkeys, ntok, pP)
                    else:
                        proj_fm(wA1, kA1, (jt - 1) * 128, nTb, nTkeys, ntok, pP)
                    S.op("act", lambda e, jt=jt, pP=pP, ntok=ntok: e.activation(out=PbA[jt % 2][:, 0:ntok], in_=bank(pP)[:, 0:ntok],
                                                                                func=AF.Copy), writes=["ps%d" % pP, "pbA%d" % (jt % 2)])
                    if prev is not None:
                        k_finish(*prev)
                    prev = (jt, pP)
                k_finish(*prev)
                for i, (src, rows, chunk) in enumerate(sbs):
                    pa, pbk = next_bank(), next_bank()
                    for k in range(8):
                        S.op("pe", lambda e, k=k, i=i, rows=rows, pa=pa, nTb=nTb: e.matmul(
                            bank(pa)[0:rows, 0:128], lhsT=nTb[:, k, i * 128:i * 128 + rows], rhs=wA0[k][:, 256:384],
                            start=(k == 0), stop=(k == 7)), reads=[kA0[k], (nTk, i)], writes=["ps%d" % pa])
                    for k in range(8):
                        S.op("pe", lambda e, k=k, i=i, rows=rows, pbk=pbk, nTb=nTb: e.matmul(
                            bank(pbk)[0:rows, 0:512], lhsT=nTb[:, k, i * 128:i * 128 + rows], rhs=wA0[k][:, 384:896],
                            start=(k == 0), stop=(k == 7)), reads=[kA0[k], (nTk, i)], writes=["ps%d" % pbk])
                    S.op("act", lambda e, rows=rows, chunk=chunk, pa=pa: e.activation(
                        out=VA[0:rows, chunk, :, 0:64], in_=bank(pa)[0:rows, 0:128].rearrange("p (h e) -> p h e", h=2),
                        func=AF.Copy), reads=["VAall"], writes=["ps%d" % pa, ("VA", chunk)])
                    S.op("act", lambda e, rows=rows, chunk=chunk, pbk=pbk: e.activation(
                        out=VB[0:rows, chunk, :, 0:128], in_=bank(pbk)[0:rows, 0:512].rearrange("p (h e) -> p h e", h=4),
                        func=AF.Copy), reads=["VBall"], writes=["ps%d" % pbk, ("VB", chunk)])
                for _ in range(per_tile):
                    if prep_steps:
                        prep_steps.pop(0)()

            stt = prepare(*tiles[0])
            for tidx in range(len(tiles)):
                nxt = prepare(*tiles[tidx + 1]) if tidx + 1 < len(tiles) else None
                process(*stt)
                stt = nxt

            while prep_steps:
                prep_steps.pop(0)()

            dump("KT", KT, [("K", k_, t_) for (k_, t_) in tiles])
            dump("wA", wA, [("wA", s_) for s_ in range(16)])
            dump("tA", T["tA"], ["tA"])
            dump("tB", T["tB"], ["tB"])
            dump("tabsA", tabs, ["tabs"])
            dump("VA", VA, [("VA", c_) for c_ in range(nchunk)])
            dump("VB", VB, [("VB", c_) for c_ in range(nchunk)])
            dump("nTA", nT[0], ["nT0", ("nT0", 0), ("nT0", 1)])
            return KT, VA, VB, KV_END

        def phase_B(xsrc, ydst, SEQ, NT, L, Lp, nsb, ntile, nchunk, KT, VA, VB, KV_END, init):
            S.barrier()
            B = Arena(arena_t[:], KV_END)
            NW = 8
            NPT = 3 if NT <= 256 else 2
            NX = 5
            PREFETCH_X = (2 * nsb + 1 <= NX)
            wr = B.bf16(NW * 1024).rearrange("p (s c) -> p s c", s=NW)
            act_bytes = NCH * NT * 2
            sqt_off = (act_bytes + 2047) // 2048 * 2048
            znt_off = 8 * 2048
            zbytes = max(znt_off + 8 * NT * 2, sqt_off + nsb * 512 * 4)
            nz = (zbytes + 2047) // 2048
            Zf = B.f32(nz * 512)
            Zq = Zf[:, 0:8 * 512].bitcast(BF16).rearrange("p (s c) -> p s c", s=8)
            act = Zf[:, 0:act_bytes // 4].bitcast(BF16).rearrange("p (c t) -> p c t", c=NCH)
            sqt = Zf[:, sqt_off // 4:sqt_off // 4 + nsb * 512]

            def zkeys(lo, hi):
                return [("Z", z) for z in range(lo // 2048, (hi - 1) // 2048 + 1)]

            def act_keys(c):
                return zkeys(c * NT * 2, (c + 1) * NT * 2)
            sqt_keys = zkeys(sqt_off, sqt_off + nsb * 512 * 4)
            ZnT = Zf[:, znt_off // 4:znt_off // 4 + 4 * NT].bitcast(BF16).rearrange("p (k t) -> p k t", k=8)
            znt_keys = zkeys(znt_off, znt_off + 8 * NT * 2)
            xr = B.f32(NX * D).rearrange("p (s c) -> p s c", s=NX)
            nT1 = B.bf16(8 * NT).rearrange("p (k t) -> p k t", k=8)
            QTf = B.bf16(16 * NT)
            QT = QTf.rearrange("p (j t) -> p j t", j=16)
            PT = [B.bf16(1024) for _ in range(NPT)]
            mix = B.bf16(nsb * D).rearrange("p (s c) -> p s c", s=nsb)
            oB = B.f32(nsb * 512).rearrange("p (s h e) -> p s h e", s=nsb, h=4)
            tB2 = B.f32(nsb * 128).rearrange("p (s e) -> p s e", s=nsb)
            nb = [B.bf16(D) for _ in range(2)]
            junk = PT[0]
            tabs = B.f32(4 * NT).rearrange("p (a t) -> p a t", a=4)
            T = {"tA": B.f32(NT), "tB": B.f32(NT), "sqb": B.bf16(NT), "rq": B.f32(NT)}
            pt_alias = False
            if NPT == 2 and NT * 4 >= 2048:
                PT.append(T["rq"][:, 0:512].bitcast(BF16))
                NPT = 3
                pt_alias = True

            def ptkeys(i):
                return [("PT", i)] + (["rq"] if (pt_alias and i == 2) else [])
            ff0 = B.off
            GW = NT + 132
            GU = [B.bf16(2 * GW).rearrange("p (a t) -> p a t", a=2) for _ in range(3)]
            cvall = B.f32(2 * NT)
            geall = B.bf16(2 * NT)
            cv = [cvall[:, i * NT:(i + 1) * NT] for i in range(2)]
            ge = [geall[:, i * NT:(i + 1) * NT] for i in range(2)]
            if B.off - ff0 < 4 * D:
                B.f32((4 * D - (B.off - ff0)) // 4 + 8)
            xm = arena_t[:][:, ff0 // 4: ff0 // 4 + D]
            GUcf = B.bf16(NCH * 2 * 130)
            GUc = GUcf.rearrange("p (c a t) -> p c a t", c=NCH, a=2)
            ss = B.f32(8)
            ssB = B.f32(4 * nsb)
            rs = B.f32(8)
            ssF = B.f32(8)
            print("phase B arena end: %d B of %d (NT=%d L=%d)" % (B.off, ARENA_BYTES, NT, L))

            S.op("dve", lambda e: e.memset(QTf, 0.0), writes=[("QT", j) for j in range(16)])
            S.op("dve", lambda e: e.memset(GUcf, 0.0), writes=[("Gc", c) for c in range(NCH)])

            wctr = [0]

            def wload(slot_idx):
                sl = wctr[0] % NW
                wctr[0] += 1
                dma(lambda e, sl=sl, slot_idx=slot_idx: e.dma_start(out=wr[:, sl, :], in_=W[slot_idx, :, :]),
                    reads=["W"], writes=[("w", sl)], key=("w", sl))
                return wr[:, sl, :], ("w", sl)

            def next_nT():
                return nT1, "nT1"

            def xslot(g):
                return g % NX

            def load_x_tile(ti):
                for s in range(nsb):
                    g = ti * nsb + s
                    sl = xslot(g)
                    dma(lambda e, sl=sl, ti=ti, s=s: e.dma_start(out=xr[:, sl, :], in_=xsrc[ti * NT + s * 128: ti * NT + (s + 1) * 128, :]),
                        writes=[("x", sl)], key=("x", sl))

            def attention(ntok, qrows, nsbq):
                hcs = []
                for h in range(8):
                    hcs.append(("A", h, h % 4, 0, (h // 4) * 64, VA, h // 4, 64))
                for hb in range(4):
                    for cp_ in range(2):
                        hcs.append(("B", (hb, cp_), 4 + hb, 1 + hb, cp_ * 64, VB, hb, 128))
                cstride = NT if ntok == NT else ntok
                G = 1024 // cstride
                groups = [[0]]
                real = list(range(1, nchunk))
                for i in range(0, len(real), G):
                    groups.append(real[i:i + G])
                steps = [(hi, gi) for hi in range(len(hcs)) for gi in range(len(groups))]

                def crow(c):
                    return NMETA if c == 0 else 128

                def ccol(c):
                    return 0 if c == 0 else NMETA + (c - 1) * 128

                def sb_base(si):
                    return (si % 2) * 1024

                def acc_base(hi):
                    return 2048 + (hi % 2) * 1024

                def acc_region(hi, dv, sb):
                    w = dv + 1
                    per = 512 // w
                    return acc_base(hi) + (sb // per) * 512 + (sb % per) * w

                def emit_qk(si):
                    hi, gi = steps[si]
                    _, _, qt, kt, r0, _, _, _ = hcs[hi]
                    base = sb_base(si)
                    for li, c in enumerate(groups[gi]):
                        rows = crow(c)
                        o0 = base + li * cstride
                        bk = o0 // 512
                        qi = 2 * qt + r0 // 64
                        S.op("pe", lambda e, rows=rows, o0=o0, kt=kt, c=c, qi=qi: e.matmul(
                            PS[0:rows, o0:o0 + ntok], lhsT=KT[:, kt, ccol(c):ccol(c) + rows],
                            rhs=QT[:, qi, 0:ntok], start=True, stop=True),
                            reads=[("QT", qi)], writes=["ps%d" % bk])

                def emit_exp(si):
                    hi, gi = steps[si]
                    grp = groups[gi]
                    rows = crow(grp[0])
                    base = sb_base(si)
                    pt = PT[si % NPT]
                    n = len(grp)
                    src = PS[0:rows, base:base + n * cstride].rearrange("p (g t) -> p g t", g=n)[:, :, 0:ntok]
                    dst = pt[0:rows, 0:n * cstride].rearrange("p (g t) -> p g t", g=n)[:, :, 0:ntok]
                    bks = sorted(set((base + li * cstride) // 512 for li in range(n)))
                    S.op("act", lambda e, src=src, dst=dst: e.activation(out=dst, in_=src, func=AF.Exp, scale=0.125),
                         writes=["ps%d" % b for b in bks] + ptkeys(si % NPT))

                def emit_pv(si):
                    hi, gi = steps[si]
                    kind, _, _, _, _, Vs, vh, dv = hcs[hi]
                    pt = PT[si % NPT]
                    for li, c in enumerate(groups[gi]):
                        rows = crow(c)
                        for sb in range(nsbq):
                            o0 = acc_region(hi, dv, sb)
                            bk = o0 // 512
                            first_in_bank = (gi == 0 and li == 0 and (o0 % 512) == 0)
                            last = (gi == len(groups) - 1 and li == len(groups[gi]) - 1)
                            S.op("pe", lambda e, rows=rows, o0=o0, li=li, sb=sb, c=c, dv=dv, vh=vh, Vs=Vs, pt=pt,
                                 fb=first_in_bank, last=last: e.matmul(
                                PS[0:qrows, o0:o0 + dv + 1], lhsT=pt[0:rows, li * cstride + sb * 128: li * cstride + sb * 128 + qrows],
                                rhs=Vs[0:rows, c, vh, 0:dv + 1], start=fb, stop=last, skip_group_check=True),
                                reads=ptkeys(si % NPT), writes=["ps%d" % bk])

                def emit_evac(hi):
                    kind, hid, _, _, _, _, _, dv = hcs[hi]
                    w = dv + 1
                    per = 512 // w
                    segs = []
                    sb = 0
                    while sb < nsbq:
                        cnt = min(per - (sb % per), nsbq - sb)
                        segs.append((sb, cnt))
                        sb += cnt
                    for (sb0, cnt) in segs:
                        o0 = acc_region(hi, dv, sb0)
                        bk = "ps%d" % (o0 // 512)
                        v = PS[0:qrows, o0:o0 + cnt * w].rearrange("p (s e) -> p s e", s=cnt)
                        rsv = rs[0:qrows, sb0:sb0 + cnt]
                        S.op("dve", lambda e, v=v, rsv=rsv, dv=dv: e.reciprocal(out=rsv, in_=v[:, :, dv]),
                             writes=[bk, "rs"])
                        if kind == "A":
                            h = hid
                            dst = mix[0:qrows, sb0:sb0 + cnt, h * 64:(h + 1) * 64]
                            S.op("dve", lambda e, v=v, rsv=rsv, dst=dst, cnt=cnt: e.tensor_tensor(
                                out=dst, in0=v[:, :, 0:64], in1=rsv[:, :, None].broadcast_to([qrows, cnt, 64]), op=ALU.mult),
                                reads=["rs"], writes=[bk, ("mix", h // 2)])
                        else:
                            hb, cp_ = hid
                            dst = oB[0:qrows, sb0:sb0 + cnt, hb, :]
                            if cp_ == 0:
                                S.op("dve", lambda e, v=v, rsv=rsv, dst=dst, cnt=cnt: e.tensor_tensor(
                                    out=dst, in0=v[:, :, 0:128], in1=rsv[:, :, None].broadcast_to([qrows, cnt, 128]), op=ALU.mult),
                                    reads=["rs"], writes=[bk, ("oB", hb)])
                            else:
                                S.op("dve", lambda e, rsv=rsv: e.tensor_scalar(out=rsv, in0=rsv, scalar1=neglam[0:qrows, :],
                                                                               scalar2=None, op0=ALU.mult),
                                     reads=["lams"], writes=["rs"])
                                t2 = tB2[0:qrows, sb0:sb0 + cnt, :]
                                S.op("dve", lambda e, v=v, rsv=rsv, t2=t2, cnt=cnt: e.tensor_tensor(
                                    out=t2, in0=v[:, :, 0:128], in1=rsv[:, :, None].broadcast_to([qrows, cnt, 128]), op=ALU.mult),
                                    reads=["rs"], writes=[bk, "tB2"])
                                S.op("pool", lambda e, dst=dst, t2=t2: e.tensor_tensor(out=dst, in0=dst, in1=t2, op=ALU.add),
                                     reads=["tB2"], writes=[("oB", hb)])

                ng = len(groups)
                emit_qk(0)
                for si in range(len(steps)):
                    if si + 1 < len(steps):
                        emit_qk(si + 1)
                    emit_exp(si)
                    emit_pv(si)
                    if steps[si][1] == ng - 1:
                        emit_evac(steps[si][0])
                n4 = nsbq * 4
                oBf = oB[0:qrows, 0:nsbq, :, :].rearrange("p s h e -> p (s h) e")
                sqv = sqt[0:qrows, 0:n4 * 128].rearrange("p (a e) -> p a e", a=n4)
                S.op("dve", lambda e: e.tensor_tensor(out=sqv, in0=oBf, in1=oBf, op=ALU.mult),
                     reads=[("oB", h) for h in range(4)], writes=sqt_keys)
                S.op("dve", lambda e: e.tensor_reduce(out=ssB[0:qrows, 0:n4], in_=sqv, axis=AX.X, op=ALU.add),
                     reads=sqt_keys, writes=["ssB"])
                rstd_from_ss(ssB[0:qrows, 0:n4], n4, 1.0 / 128, "ssB", qrows)
                S.op("dve", lambda e: e.tensor_tensor(out=oBf, in0=oBf, in1=ssB[0:qrows, 0:n4, None].broadcast_to([qrows, n4, 128]),
                                                      op=ALU.mult), reads=["ssB"], writes=[("oB", h) for h in range(4)])
                for sb in range(nsbq):
                    S.op("dve", lambda e, sb=sb: e.scalar_tensor_tensor(
                        out=mix[0:qrows, sb, 512:1024].rearrange("p (h e) -> p h e", h=4), in0=oB[0:qrows, sb, :, :],
                        scalar=1.0 - LAM_INIT, in1=gsub[0:qrows, None, :].broadcast_to([qrows, 4, 128]),
                        op0=ALU.mult, op1=ALU.mult), reads=[("oB", h) for h in range(4)] + ["gsub"],
                        writes=[("mix", 4 + h) for h in range(4)])

            gctr = [0]

            def ffn(mode, n2T, n2keys, ncur, wsbs, gbase, ybase):
                if mode != "flush":
                    pend = None
                    for c in range(NCH):
                        wg, wgk = wload(SL_G + c)
                        if mode == "real":
                            wu, wuk = wload(SL_U + c)
                        pG = next_bank()
                        for k in range(8):
                            S.op("pe", lambda e, k=k, wg=wg, pG=pG: e.matmul(
                                bank(pG)[:, 0:ncur], lhsT=wg[:, k * 128:(k + 1) * 128], rhs=n2T[:, k, 0:ncur],
                                start=(k == 0), stop=(k == 7)), reads=[wgk] + n2keys, writes=["ps%d" % pG])
                        if mode == "meta":
                            S.op("dve", lambda e, c=c, pG=pG: e.tensor_copy(out=GUc[:, c, 0, 128:129], in_=bank(pG)[:, NMETA - 1:NMETA]),
                                 writes=["ps%d" % pG, ("Gc", c)])
                            continue
                        pU = next_bank()
                        for k in range(8):
                            S.op("pe", lambda e, k=k, wu=wu, pU=pU: e.matmul(
                                bank(pU)[:, 0:ncur], lhsT=wu[:, k * 128:(k + 1) * 128], rhs=n2T[:, k, 0:ncur],
                                start=(k == 0), stop=(k == 7)), reads=[wuk] + n2keys, writes=["ps%d" % pU])
                        st = ffn_chunk_a(c, pG, pU, NT)
                        if pend is not None:
                            ffn_chunk_b(*pend)
                        pend = st
                    if pend is not None:
                        ffn_chunk_b(*pend)
                else:
                    gs = 2 * NT // 128
                    for c0 in range(0, NCH, gs):
                        n = min(gs, NCH - c0)
                        Gv = GUc[:, c0:c0 + n, 0, :]
                        Uv = GUc[:, c0:c0 + n, 1, 1:129]
                        t = cvall[:, 0:n * 128].rearrange("p (c t) -> p c t", c=n)
                        u = sqt[:, 0:n * 128].rearrange("p (c t) -> p c t", c=n)
                        gv = geall[:, 0:n * 128].rearrange("p (c t) -> p c t", c=n)
                        gck = [("Gc", c) for c in range(c0, c0 + n)]

                        def wb(base, c0=c0, n=n):
                            return cols[:, base + c0:base + c0 + n, None].broadcast_to([128, n, 128])
                        S.op("dve", lambda e, Gv=Gv, t=t, wb=wb: e.tensor_tensor(out=t, in0=Gv[:, :, 0:128], in1=wb(0), op=ALU.mult),
                             reads=gck + ["cols"], writes=["cv0", "cv1"])
                        for tap in (1, 2):
                            S.op("dve", lambda e, Gv=Gv, u=u, wb=wb, tap=tap: e.tensor_tensor(
                                out=u, in0=Gv[:, :, tap:tap + 128], in1=wb(22 * tap), op=ALU.mult), reads=gck + ["cols"], writes=sqt_keys)
                            S.op("dve", lambda e, t=t, u=u: e.tensor_tensor(out=t, in0=t, in1=u, op=ALU.add),
                                 reads=sqt_keys, writes=["cv0", "cv1"])
                        S.op("dve", lambda e, t=t, wb=wb: e.tensor_tensor(out=t, in0=t, in1=wb(66), op=ALU.add),
                             reads=["cols"], writes=["cv0", "cv1"])
                        S.op("act", lambda e, t=t, gv=gv: e.activation(out=gv, in_=t, func=AF.Gelu), reads=["cv0", "cv1"],
                             writes=["ge0", "ge1"])
                        ak = []
                        for c in range(c0, c0 + n):
                            ak += act_keys(c)
                        S.op("pool", lambda e, gv=gv, Uv=Uv, c0=c0, n=n: e.tensor_tensor(out=act[:, c0:c0 + n, 0:128], in0=gv, in1=Uv,
                                                                                       op=ALU.mult),
                             reads=["ge0", "ge1"] + gck, writes=sorted(set(ak)))
                if mode == "meta":
                    return
                banks = {}
                for w in wsbs:
                    for half in range(2):
                        banks[(w, half)] = next_bank()
                for c in range(NCH):
                    wd, wdk = wload(SL_D + c)
                    for w in wsbs:
                        for half in range(2):
                            pb = banks[(w, half)]
                            S.op("pe", lambda e, c=c, w=w, half=half, pb=pb, wd=wd: e.matmul(
                                bank(pb)[:, 0:512], lhsT=act[:, c, w * 128:(w + 1) * 128], rhs=wd[:, half * 512:(half + 1) * 512],
                                start=(c == 0), stop=(c == NCH - 1)), reads=[wdk] + act_keys(c), writes=["ps%d" % pb])
                fin = []
                for wi, w in enumerate(wsbs):
                    sl = xslot(gbase + w)
                    xa = xr[:, sl, :]
                    xk = ("x", sl)
                    fin.append((wi, w, sl, xa, xk))
                    for half in range(2):
                        pb = banks[(w, half)]
                        S.op("dve", lambda e, xa=xa, half=half, pb=pb: e.tensor_tensor(
                            out=xa[:, half * 512:(half + 1) * 512], in0=bank(pb)[:, 0:512], in1=xa[:, half * 512:(half + 1) * 512],
                            op=ALU.add), writes=["ps%d" % pb, xk])
                for (wi, w, sl, xa, xk) in fin:
                    S.op("act", lambda e, xa=xa, wi=wi: e.activation(out=junk, in_=xa, func=AF.Square, accum_out=ssF[:, wi:wi + 1]),
                         reads=[xk], writes=["junk", ("ssF", wi)])
                nw = len(fin)
                if nw:
                    sk = [("ssF", wi) for wi in range(nw)]
                    S.op("act", lambda e: e.activation(out=ssF[:, 0:nw], in_=ssF[:, 0:nw], func=AF.Ln, scale=1.0 / D, bias=epsc),
                         reads=["epsc"], writes=sk)
                    S.op("act", lambda e: e.activation(out=ssF[:, 0:nw], in_=ssF[:, 0:nw], func=AF.Exp, scale=-0.5), writes=sk)
                for (wi, w, sl, xa, xk) in fin:
                    S.op("dve", lambda e, xa=xa, wi=wi: e.scalar_tensor_tensor(
                        out=xa, in0=xa, scalar=ssF[:, wi:wi + 1], in1=gfin, op0=ALU.mult, op1=ALU.mult),
                        reads=[("ssF", wi), "gfin"], writes=[xk])
                    r0 = ybase + w * 128
                    dma(lambda e, xa=xa, r0=r0: e.dma_start(out=ydst[r0:r0 + 128, :], in_=xa), reads=[xk], key=("y", sl))

            def ffn_chunk_a(c, pG, pU, nwin):
                i3 = gctr[0] % 3
                i2 = gctr[0] % 2
                gctr[0] += 1
                gu, cvb, geb = GU[i3], cv[i2], ge[i2]
                kc, kg, ku, ck, ek = ("gu", i3, "c"), ("gu", i3, "g"), ("gu", i3, "u"), "cv%d" % i2, "ge%d" % i2
                S.op("pool", lambda e: e.tensor_copy(out=gu[:, :, 0:129], in_=GUc[:, c, :, 0:129]), reads=[("Gc", c)], writes=[kc])
                if pG is not None:
                    S.op("act", lambda e: e.activation(out=gu[:, 0, 129:129 + NT], in_=bank(pG)[:, 0:NT], func=AF.Copy),
                         writes=["ps%d" % pG, kg])
                    S.op("dve", lambda e: e.tensor_copy(out=gu[:, 1, 129:129 + NT], in_=bank(pU)[:, 0:NT]),
                         writes=["ps%d" % pU, ku])
                    S.op("pool", lambda e: e.tensor_copy(out=GUc[:, c, :, 0:129], in_=gu[:, :, NT:NT + 129]),
                         reads=[kc, kg, ku], writes=[("Gc", c)])
                else:
                    S.op("pool", lambda e: e.memset(gu[:, 0, 129:131], 0.0), writes=[kg])
                return (c, nwin, gu, cvb, geb, kc, kg, ku, ck, ek)

            def ffn_chunk_b(c, nwin, gu, cvb, geb, kc, kg, ku, ck, ek):
                w0, w1, w2, bb = col(c), col(22 + c), col(44 + c), col(66 + c)
                Gb = gu[:, 0, :]
                S.op("dve", lambda e: e.tensor_scalar(out=cvb[:, 0:nwin], in0=Gb[:, 0:nwin], scalar1=w0, scalar2=bb,
                                                      op0=ALU.mult, op1=ALU.add), reads=[kc, kg, "cols"], writes=[ck])
                S.op("dve", lambda e: e.scalar_tensor_tensor(out=cvb[:, 0:nwin], in0=Gb[:, 1:nwin + 1], scalar=w1, in1=cvb[:, 0:nwin],
                                                             op0=ALU.mult, op1=ALU.add), reads=[kc, kg, "cols"], writes=[ck])
                S.op("dve", lambda e: e.scalar_tensor_tensor(out=cvb[:, 0:nwin], in0=Gb[:, 2:nwin + 2], scalar=w2, in1=cvb[:, 0:nwin],
                                                             op0=ALU.mult, op1=ALU.add), reads=[kc, kg, "cols"], writes=[ck])
                S.op("act", lambda e: e.activation(out=geb[:, 0:nwin], in_=cvb[:, 0:nwin], func=AF.Gelu), reads=[ck], writes=[ek])
                S.op("pool", lambda e: e.tensor_tensor(out=act[:, c, 0:nwin], in0=geb[:, 0:nwin], in1=gu[:, 1, 1:nwin + 1], op=ALU.mult),
                     reads=[ek, kc, ku], writes=act_keys(c))

            def mixer_and_norm(kind, ti, srcs, ntok, qrows, nsbq, col0):
                nTb = ZnT
                norm_T(srcs, nTb, (lambda i: znt_keys), NT, ss, "ss", nb, junk, "act")
                nTkeys = znt_keys
                dma(lambda e: e.dma_start(out=tabs[:, :, 0:ntok], in_=tabs_d[:, :, col0:col0 + ntok].rearrange("a p n -> p a n")),
                    writes=["tabs"], key="tabs")
                wq = [wload(SL_Q + k) for k in range(8)]
                Pb = [Zf[:, i * 512:(i + 1) * 512].bitcast(BF16)[:, 0:NT] for i in range(2)]
                T2 = {"tA": Zf[:, 2 * 512:2 * 512 + NT], "tB": Zf[:, 3 * 512:3 * 512 + NT], "rq": Zf[:, 4 * 512:4 * 512 + NT],
                      "sqb": Zf[:, 5 * 512:6 * 512].bitcast(BF16)[:, 0:NT],
                      "keys": {"tA": ("Z", 2), "tB": ("Z", 3), "rq": ("Z", 4), "sqb": ("Z", 5)}}

                def q_finish(j, pP):
                    pR = next_bank()
                    pb, pbk = Pb[j % 2], ("Z", j % 2)
                    S.op("pe", lambda e: e.matmul(bank(pR)[:, 0:ntok], lhsT=rmat, rhs=pb[:, 0:ntok], start=True, stop=True),
                         reads=[pbk, "rmat"], writes=["ps%d" % pR])
                    qk_post("A" if j < 4 else "B", pP, pR, ntok, tabs,
                            [(QT[0:64, 2 * j, 0:ntok], 0, 64, ("QT", 2 * j)), (QT[64:128, 2 * j + 1, 0:ntok], 64, 128, ("QT", 2 * j + 1))],
                            104, 105, T if j % 2 == 0 else T2)
                prev = None
                for j in range(8):
                    pP = next_bank()
                    proj_fm([w[0] for w in wq], [w[1] for w in wq], j * 128, nTb, nTkeys, ntok, pP)
                    S.op("act", lambda e, j=j, pP=pP: e.activation(out=Pb[j % 2][:, 0:ntok], in_=bank(pP)[:, 0:ntok], func=AF.Copy),
                         writes=["ps%d" % pP, ("Z", j % 2)])
                    if prev is not None:
                        q_finish(*prev)
                    prev = (j, pP)
                q_finish(*prev)
                if kind == "real" and ti == 0:
                    dump("QT", QT, [("QT", j_) for j_ in range(16)])
                attention(ntok, qrows, nsbq)
                if kind == "real" and ti == 0:
                    dump("mix", mix, [("mix", j_) for j_ in range(8)])
                mTb, mTk = next_nT()
                for sb in range(nsbq):
                    pb = next_bank()
                    psb = bank(pb).bitcast(BF16)
                    for j in range(8):
                        S.op("pe", lambda e, j=j, sb=sb, psb=psb: e.transpose(
                            out=psb[:, j * 128:j * 128 + qrows], in_=mix[0:qrows, sb, j * 128:(j + 1) * 128],
                            identity=identb[0:qrows, 0:qrows]), reads=[("mix", j), "identb"], writes=["ps%d" % pb])
                    S.op("dve", lambda e, sb=sb, psb=psb: e.tensor_copy(
                        out=mTb[:, :, sb * 128:sb * 128 + qrows], in_=psb.rearrange("p (j t) -> p j t", j=8)[:, :, 0:qrows]),
                        writes=["ps%d" % pb, (mTk, sb)])
                wo = [wload(SL_O + j) for j in range(8)]
                for sb, (xa, xk, rows) in enumerate(srcs):
                    for half in range(2):
                        pb = next_bank()
                        for j in range(8):
                            S.op("pe", lambda e, j=j, sb=sb, half=half, pb=pb, rows=rows: e.matmul(
                                bank(pb)[0:rows, 0:512], lhsT=mTb[:, j, sb * 128:sb * 128 + rows],
                                rhs=wo[j][0][:, half * 512:(half + 1) * 512], start=(j == 0), stop=(j == 7)),
                                reads=[wo[j][1], (mTk, sb)], writes=["ps%d" % pb])
                        S.op("dve", lambda e, xa=xa, half=half, pb=pb, rows=rows: e.tensor_tensor(
                            out=xa[:, half * 512:(half + 1) * 512], in0=bank(pb)[0:rows, 0:512],
                            in1=xa[:, half * 512:(half + 1) * 512], op=ALU.add), writes=["ps%d" % pb, xk])
                if kind == "real" and ti == 0:
                    dump("h1", srcs[0][0], [srcs[0][1]])
                n2b, n2k = next_nT()
                norm_T(srcs, n2b, (lambda i, n2k=n2k: [(n2k, i)]), NT, ss, "ss", nb, junk, "act")
                if kind == "real" and ti == 0:
                    dump("n2T", n2b, [(n2k, i_) for i_ in range(len(srcs))])
                return n2b, [(n2k, i) for i in range(len(srcs))]

            dma(lambda e: e.dma_start(out=xm[0:NMETA, :], in_=meta_d[:, :]), writes=["xm"], key="xm")
            if PREFETCH_X:
                load_x_tile(0)
            n2b, n2keys = mixer_and_norm("meta", 0, [(xm[0:NMETA, :], "xm", NMETA)], NMETA, NMETA, 1, 0)
            ffn("meta", n2b, n2keys, NMETA, [], 0, 0)
            S.barrier()
            for ti in range(ntile):
                if PREFETCH_X:
                    if ti + 1 < ntile:
                        load_x_tile(ti + 1)
                else:
                    load_x_tile(ti)
                srcs = [(xr[:, xslot(ti * nsb + s), :], ("x", xslot(ti * nsb + s)), 128) for s in range(nsb)]
                n2b, n2keys = mixer_and_norm("real", ti, srcs, NT, 128, nsb, NMETA + ti * NT)
                wsbs = list(range(nsb)) if ti > 0 else list(range(1, nsb))
                ffn("real", n2b, n2keys, NT, wsbs, ti * nsb - 1, ti * NT - 128)
            ffn("flush", None, None, 0, [0], ntile * nsb - 1, SEQ - 128)

        seen = set()
        for sidx, (kind, i, NT) in enumerate(seq_cfg):
            init = (kind, NT) not in seen
            seen.add((kind, NT))
            if kind == "p":
                do_sequence(sidx, xp, yp, SEQ_P, NT, init)
            else:
                do_sequence(sidx, xs[i], ys[i], SEQ_S, NT, init)

        S.emit(nc, st)
    return nc


def _rope_tables():
    f32 = np.float32
    theta = f32(10000.0)
    t = np.arange(SEQ_P)
    row = (t // 64).astype(f32)
    colp = (t % 64).astype(f32)
    inv16 = (theta ** (-(np.arange(0, 32, 2, dtype=f32)) / f32(32))).astype(f32)
    ang = np.concatenate([row[:, None] * inv16[None], colp[:, None] * inv16[None]], axis=-1).astype(f32)
    ang_a = np.concatenate([np.zeros((NMETA, 32), f32), ang], axis=0)
    pos = np.arange(LMAX, dtype=f32)
    inv32 = (theta ** (-(np.arange(0, 64, 2, dtype=f32)) / f32(64))).astype(f32)
    ang_b = (pos[:, None] * inv32[None]).astype(f32)
    tabs = np.empty((4, 128, LMAX), f32)
    for a, src in enumerate((np.cos(ang_a), np.sin(ang_a), np.cos(ang_b), np.sin(ang_b))):
        tabs[a] = np.tile(src.T.astype(f32), (4, 1))
    return tabs


def _rot_matrix():
    r = np.zeros((128, 128), np.float32)
    for m in range(128):
        h, i = divmod(m, 64)
        if i < 32:
            r[h * 64 + i + 32, m] = -1.0
        else:
            r[h * 64 + i - 32, m] = 1.0
    return r


_CACHE = {}
DBG_ON = False


def kernel(x_prompt, x_sample, meta_tokens, g_mix, w_in, g_qnorm_a, g_knorm_a, lambda_q1, lambda_k1,
           lambda_q2, lambda_k2, g_subln, w_out, g_ffn, w_ff_gate, w_ff_up, conv_w, conv_b, w_ff_down, g_final):
    seq_cfg = [("p", 0, 256)] + [("s", i, 512) for i in range(NSAMP)]
    if "nc" not in _CACHE:
        _CACHE["nc"] = build_program(seq_cfg)
    nc = _CACHE["nc"]
    f = _f
    shared = make_shared(meta_tokens, g_mix, w_in, g_qnorm_a, g_knorm_a, lambda_q1, lambda_k1, lambda_q2, lambda_k2,
                         g_subln, w_out, g_ffn, w_ff_gate, w_ff_up, conv_w, conv_b, w_ff_down, g_final)
    xpf, xsf = f(x_prompt), f(x_sample)
    in_maps = []
    for c in range(8):
        m = dict(shared)
        m["xp"] = xpf[c]
        m["xs"] = xsf[c * NSAMP:(c + 1) * NSAMP]
        in_maps.append(m)
    res = run_bass_kernel_spmd(nc, in_maps, core_ids=list(range(8)))
    y_prompt = np.stack([np.asarray(res.results[c]["yp"], dtype=np.float32) for c in range(8)], axis=0)
    y_sample = np.concatenate([np.asarray(res.results[c]["ys"], dtype=np.float32) for c in range(8)], axis=0)
    return (y_prompt, y_sample)


def _f(a):
    return np.ascontiguousarray(np.asarray(a, dtype=np.float32))


def make_shared(meta_tokens, g_mix, w_in, g_qnorm_a, g_knorm_a, lambda_q1, lambda_k1, lambda_q2, lambda_k2,
                g_subln, w_out, g_ffn, w_ff_gate, w_ff_up, conv_w, conv_b, w_ff_down, g_final):
    f = _f
    if "tabs" not in _CACHE:
        _CACHE["tabs"] = _rope_tables()
    onesbd = np.zeros((128, 128), np.float32)
    onesbd[:64, :64] = 1.0
    onesbd[64:, 64:] = 1.0
    return {
        "meta": f(meta_tokens), "w_in": f(w_in[0]), "w_out": f(w_out[0]), "w_gate": f(w_ff_gate[0]), "w_up": f(w_ff_up[0]),
        "w_down": f(w_ff_down[0]), "g_mix": f(g_mix[0]).reshape(8, 128), "g_ffn": f(g_ffn[0]).reshape(8, 128),
        "gq": f(g_qnorm_a[0]).reshape(1, 64), "gk": f(g_knorm_a[0]).reshape(1, 64),
        "lam": np.stack([f(lambda_q1[0]), f(lambda_k1[0]), f(lambda_q2[0]), f(lambda_k2[0])], axis=0),
        "gsub": f(g_subln[0]).reshape(1, 128), "convw": f(conv_w[0]).reshape(66, 128), "convb": f(conv_b[0]).reshape(22, 128),
        "gfin": f(g_final).reshape(1, D), "tabs": _CACHE["tabs"], "identf": np.eye(128, dtype=np.float32), "onesbd": onesbd,
        "rmat": _rot_matrix(),
    }
```

```python
import math
from contextlib import ExitStack
import numpy as np
import concourse.bass as bass
import concourse.mybir as mybir
from concourse.bass_utils import run_bass_kernel_spmd

F32 = mybir.dt.float32
BF16 = mybir.dt.bfloat16
ALU = mybir.AluOpType
AF = mybir.ActivationFunctionType
AX = mybir.AxisListType

D = 1024
NMETA = 16
DFF = 2816
NCH = 22
EPS = 1e-6
LAM_INIT = 0.8 - 0.6 * math.exp(0.0)
SEQ_P = 4096
SEQ_S = 2048
NSAMP = 4
LMAX = SEQ_P + NMETA
ARENA_BYTES = 206 * 1024
NSLOT = 106
SL_A0, SL_A1, SL_Q, SL_QR, SL_O, SL_G, SL_U, SL_D = 0, 8, 16, 24, 32, 40, 62, 84

ENGS = ("pe", "act", "dve", "pool", "sp")


class Op:
    __slots__ = ("eng", "fn", "deps", "signal", "count", "dma_key", "dma_n", "is_dma")

    def __init__(self, eng, fn, is_dma, dma_key, dma_n):
        self.eng = eng
        self.fn = fn
        self.deps = []
        self.signal = False
        self.count = 0
        self.is_dma = is_dma
        self.dma_key = dma_key
        self.dma_n = dma_n


class Sched:
    def __init__(self):
        self.ops = {e: [] for e in ENGS}
        self.last_writer = {}
        self.readers = {}
        self.bar = []
        self.bar_pending = {e: False for e in ENGS}
        self.last_dma = {}

    def op(self, eng, fn, reads=(), writes=(), dma_key=None, dma_n=1):
        is_dma = dma_key is not None
        o = Op(eng, fn, is_dma, dma_key, dma_n)
        deps = {}
        lw = self.last_writer
        for b in reads:
            w = lw.get(b)
            if w is not None:
                deps[id(w)] = w
        for b in writes:
            w = lw.get(b)
            if w is not None:
                deps[id(w)] = w
            rs = self.readers.get(b)
            if rs:
                for r in rs:
                    deps[id(r)] = r
        if self.bar_pending[eng]:
            self.bar_pending[eng] = False
            for d in self.bar:
                deps[id(d)] = d
        for d in deps.values():
            if (not d.is_dma) and d.eng == "pe" and eng == "pe" and not is_dma:
                continue
            d.signal = True
            o.deps.append(d)
        for b in writes:
            lw[b] = o
            self.readers[b] = []
        for b in reads:
            rs = self.readers.setdefault(b, [])
            if not is_dma:
                for i, r in enumerate(rs):
                    if (not r.is_dma) and r.eng == eng:
                        rs[i] = o
                        break
                else:
                    rs.append(o)
            else:
                rs.append(o)
        self.ops[eng].append(o)
        if is_dma:
            self.last_dma[dma_key] = o
        return o

    def barrier(self):
        bar = []
        for e in ENGS:
            for o in reversed(self.ops[e]):
                if not o.is_dma:
                    bar.append(o)
                    break
        bar.extend(self.last_dma.values())
        self.bar = bar
        self.bar_pending = {e: True for e in ENGS}
        self.last_writer = {}
        self.readers = {}

    def emit(self, nc, stack):
        dma_cnt = {}
        for e in ENGS:
            c = 0
            for o in self.ops[e]:
                if o.is_dma:
                    dma_cnt[o.dma_key] = dma_cnt.get(o.dma_key, 0) + 16 * o.dma_n
                    o.count = dma_cnt[o.dma_key]
                elif o.signal:
                    c += 1
                    o.count = c
        esem = {e: stack.enter_context(nc.semaphore("s_" + e)) for e in ENGS if e != "sp"}
        dsem = {k: stack.enter_context(nc.semaphore("d_%d" % i)) for i, k in enumerate(dma_cnt)}
        block = stack.enter_context(nc.Block())

        def run(e):
            def body(eng):
                known = {}
                for o in self.ops[e]:
                    need = {}
                    for d in o.deps:
                        key = ("d", d.dma_key) if d.is_dma else ("e", d.eng)
                        if d.count > need.get(key, 0):
                            need[key] = d.count
                    for key, v in need.items():
                        if known.get(key, 0) >= v:
                            continue
                        known[key] = v
                        sem = dsem[key[1]] if key[0] == "d" else esem[key[1]]
                        eng.wait_ge(sem, v)
                    r = o.fn(eng)
                    if o.is_dma:
                        if not isinstance(r, (list, tuple)):
                            r = [r]
                        assert len(r) == o.dma_n
                        for ins in r:
                            ins.then_inc(dsem[o.dma_key], 16)
                    elif o.signal:
                        r.then_inc(esem[e], 1)
                last = {}
                for o in self.ops[e]:
                    if o.is_dma:
                        last[o.dma_key] = o.count
                for k, v in last.items():
                    if known.get(("d", k), 0) < v:
                        eng.wait_ge(dsem[k], v)
            return body

        block.tensor(run("pe"))
        block.scalar(run("act"))
        block.vector(run("dve"))
        block.gpsimd(run("pool"))
        block.sync(run("sp"))


class Arena:
    def __init__(self, ap, base=0):
        self.ap = ap
        self.off = base

    def f32(self, n):
        self.off = (self.off + 31) // 32 * 32
        a = self.ap[:, self.off // 4:self.off // 4 + n]
        self.off += 4 * n
        assert self.off <= ARENA_BYTES, self.off
        return a

    def bf16(self, n):
        n2 = (n + 1) // 2
        return self.f32(n2).bitcast(BF16)[:, 0:n]


def build_program(seq_cfg):
    nc = bass.Bass("TRN2", target_bir_lowering=False)

    def din(name, shape):
        return nc.dram_tensor(name, list(shape), F32, kind="ExternalInput").ap()

    xp = din("xp", [SEQ_P, D])
    xs = din("xs", [NSAMP, SEQ_S, D])
    meta_d = din("meta", [NMETA, D])
    w_in_d = din("w_in", [D, 2304])
    w_out_d = din("w_out", [D, D])
    w_gate_d = din("w_gate", [D, DFF])
    w_up_d = din("w_up", [D, DFF])
    w_down_d = din("w_down", [DFF, D])
    g_mix_d = din("g_mix", [8, 128])
    g_ffn_d = din("g_ffn", [8, 128])
    gq_d = din("gq", [1, 64])
    gk_d = din("gk", [1, 64])
    lam_d = din("lam", [4, 64])
    gsub_d = din("gsub", [1, 128])
    convw_d = din("convw", [66, 128])
    convb_d = din("convb", [22, 128])
    gfin_d = din("gfin", [1, D])
    tabs_d = din("tabs", [4, 128, LMAX])
    identf_d = din("identf", [128, 128])
    onesbd_d = din("onesbd", [128, 128])
    rmat_d = din("rmat", [128, 128])
    yp = nc.dram_tensor("yp", [SEQ_P, D], F32, kind="ExternalOutput").ap()
    ys = nc.dram_tensor("ys", [NSAMP, SEQ_S, D], F32, kind="ExternalOutput").ap()
    W = nc.dram_tensor("wscr", [NSLOT, 128, 1024], BF16, kind="Internal").ap()

    S = Sched()
    dcnt = [0]

    def dump(name, ap, keys):
        if not DBG_ON:
            return
        dcnt[0] += 1
        t = nc.dram_tensor("dbg_" + name, list(ap.shape), ap.dtype, kind="ExternalOutput").ap()
        S.op("sp", lambda e: e.dma_start(out=t, in_=ap), reads=keys, dma_key=("dbg", dcnt[0]))
    with ExitStack() as st:
        arena_t = st.enter_context(nc.sbuf_tensor("arena", [128, ARENA_BYTES // 4], F32))
        psum_t = st.enter_context(nc.psum_tensor("psum", [128, 4096], F32))
        PS = psum_t[:]
        AR = Arena(arena_t[:])

        def bank(b):
            return PS[:, b * 512:(b + 1) * 512]

        bank_ctr = [0]

        def next_bank():
            b = bank_ctr[0] % 8
            bank_ctr[0] += 1
            return b

        identb = AR.bf16(128)
        onesbd = AR.bf16(128)
        rmat = AR.bf16(128)
        cols = AR.f32(128)
        ncols = AR.f32(128)
        gfin = AR.f32(D)
        gsub = AR.f32(128)
        lams = AR.f32(8)
        epsc = AR.f32(1)
        CONST_END = AR.off
        identf = AR.f32(128)
        onesf = AR.f32(128)
        lamt = AR.f32(256)
        stg = AR.f32(128)
        SETUP_END = AR.off

        def dma(fn, reads=(), writes=(), key=None, n=1):
            S.op("sp", fn, reads=reads, writes=writes, dma_key=key, dma_n=n)

        dma(lambda e: e.dma_start(out=identf, in_=identf_d[:, :]), writes=["identf"], key="c_identf")
        dma(lambda e: e.dma_start(out=onesf, in_=onesbd_d[:, :]), writes=["onesf"], key="c_onesf")
        dma(lambda e: e.dma_start(out=gfin, in_=gfin_d[0:1, :].partition_broadcast(128)), writes=["gfin"], key="c_gfin")
        dma(lambda e: e.dma_start(out=gsub, in_=gsub_d[0:1, :].partition_broadcast(128)), writes=["gsub"], key="c_gsub")
        dma(lambda e: [e.dma_start(out=lamt[:, i * 64:(i + 1) * 64], in_=lam_d[i:i + 1, :].partition_broadcast(128))
                       for i in range(4)], writes=["lamt"], key="c_lamt", n=4)
        S.op("pool", lambda e: e.memset(stg, 0.0), writes=["stg"])
        S.op("pool", lambda e: e.memset(epsc, EPS), writes=["epsc"])

        def stg_loads(e):
            r = []
            r.append(e.dma_start(out=stg[0:66, :], in_=convw_d[:, :]))
            r.append(e.dma_start(out=stg[66:88, :], in_=convb_d[:, :]))
            r.append(e.dma_start(out=stg[88:96, :], in_=g_mix_d[:, :]))
            r.append(e.dma_start(out=stg[96:104, :], in_=g_ffn_d[:, :]))
            for row, src in ((104, gq_d), (106, gk_d)):
                for h in range(2):
                    r.append(e.dma_start(out=stg[row:row + 1, h * 64:(h + 1) * 64], in_=src[0:1, :]))
                    r.append(e.dma_start(out=stg[row + 1:row + 2, h * 64:h * 64 + 32], in_=src[0:1, 32:64]))
                    r.append(e.dma_start(out=stg[row + 1:row + 2, h * 64 + 32:h * 64 + 64], in_=src[0:1, 0:32]))
            return r
        dma(stg_loads, reads=[], writes=["stg"], key="c_stg", n=16)
        S.op("dve", lambda e: e.tensor_copy(out=identb, in_=identf), reads=["identf"], writes=["identb"])
        S.op("dve", lambda e: e.tensor_copy(out=onesbd, in_=onesf), reads=["onesf"], writes=["onesbd"])
        dma(lambda e: e.dma_start(out=onesf, in_=rmat_d[:, :]), reads=["onesbd"], writes=["onesf"], key="c_onesf")
        S.op("dve", lambda e: e.tensor_copy(out=rmat, in_=onesf), reads=["onesf"], writes=["rmat"])
        S.op("pe", lambda e: e.transpose(out=bank(0)[:, 0:128], in_=stg, identity=identf), reads=["stg", "identf"], writes=["ps0"])
        S.op("dve", lambda e: e.tensor_copy(out=cols, in_=bank(0)[:, 0:128]), writes=["ps0", "cols"])
        S.op("dve", lambda e: e.scalar_tensor_tensor(out=lamt[:, 0:64], in0=lamt[:, 0:64], scalar=1.0, in1=lamt[:, 64:128],
                                                      op0=ALU.mult, op1=ALU.mult, accum_out=lams[:, 0:1]),
             reads=["lamt"], writes=["lamt", "lams"])
        S.op("dve", lambda e: e.scalar_tensor_tensor(out=lamt[:, 128:192], in0=lamt[:, 128:192], scalar=1.0, in1=lamt[:, 192:256],
                                                      op0=ALU.mult, op1=ALU.mult, accum_out=lams[:, 1:2]),
             reads=["lamt", "lams"], writes=["lamt", "lams"])
        S.op("act", lambda e: e.activation(out=lams[:, 2:4], in_=lams[:, 0:2], func=AF.Exp), reads=["lams"], writes=["lams"])
        S.op("dve", lambda e: e.tensor_tensor(out=lams[:, 4:5], in0=lams[:, 2:3], in1=lams[:, 3:4], op=ALU.subtract),
             reads=["lams"], writes=["lams"])
        S.op("dve", lambda e: e.tensor_scalar(out=lams[:, 5:6], in0=lams[:, 4:5], scalar1=LAM_INIT, scalar2=-1.0,
                                               op0=ALU.add, op1=ALU.mult), reads=["lams"], writes=["lams"])
        neglam = lams[:, 5:6]
        dump("cols", cols, ["cols"])
        dump("lams", lams, ["lams"])

        def col(i):
            return cols[:, i:i + 1]

        S.op("dve", lambda e: e.tensor_scalar(out=ncols, in0=cols, scalar1=-1.0, scalar2=None, op0=ALU.mult),
             reads=["cols"], writes=["cols"])

        PA = Arena(arena_t[:], SETUP_END)
        NPB = 4
        wf = [PA.f32(DFF) for _ in range(NPB)]
        wb = [PA.bf16(4096) for _ in range(NPB)]
        pe_rr = [0]

        def ew(dst, src, gi, neg, rk, wk):
            eng = ("dve", "act")[pe_rr[0] % 2]
            pe_rr[0] += 1
            if gi is None:
                if eng == "dve":
                    S.op("dve", lambda e: e.tensor_copy(out=dst, in_=src), reads=[rk], writes=[wk])
                else:
                    S.op("act", lambda e: e.activation(out=dst, in_=src, func=AF.Copy), reads=[rk], writes=[wk])
                return
            g = (ncols if neg else cols)[:, gi:gi + 1]
            if eng == "dve":
                S.op("dve", lambda e: e.tensor_scalar(out=dst, in0=src, scalar1=g, scalar2=None, op0=ALU.mult),
                     reads=[rk, "cols"], writes=[wk])
            else:
                S.op("act", lambda e: e.activation(out=dst, in_=src, func=AF.Copy, scale=g), reads=[rk, "cols"], writes=[wk])

        pidx = [0]

        def prep_next():
            b = pidx[0] % NPB
            pidx[0] += 1
            return b

        def cp(dst, src, g, rk, wk):
            ew(dst, src, g, False, rk, wk)

        def cpneg(dst, src, g, rk, wk):
            ew(dst, src, g, True, rk, wk)

        def rot(dst, src, g, nh, rk, wk):
            dv = dst.rearrange("p (h t e) -> p h t e", h=nh, t=2)
            sv = src.rearrange("p (h t e) -> p h t e", h=nh, t=2)
            cpneg(dv[:, :, 0, :], sv[:, :, 1, :], g, rk, wk)
            cp(dv[:, :, 1, :], sv[:, :, 0, :], g, rk, wk)

        for k in range(8):
            b = prep_next()
            f, o = wf[b], wb[b]
            fk, ok = "wf%d" % b, "wb%d" % b
            dma(lambda e, f=f, k=k: e.dma_start(out=f[:, 0:2304], in_=w_in_d[k * 128:(k + 1) * 128, :]), writes=[fk], key=fk)
            g = 88 + k
            A0, A1, Q, QR = (o[:, i * 1024:(i + 1) * 1024] for i in range(4))
            cp(A0[:, 0:128], f[:, 512:640], g, fk, ok)
            cp(A0[:, 256:384], f[:, 640:768], g, fk, ok)
            cp(A0[:, 384:896], f[:, 1792:2304], g, fk, ok)
            cp(A1[:, 0:512], f[:, 1280:1792], g, fk, ok)
            qd = Q[:, 0:512].rearrange("p (j t d) -> p j t d", j=4, t=2)
            qs = f[:, 0:512].rearrange("p (t j d) -> p j t d", t=2, j=4)
            for t in range(2):
                cp(qd[:, :, t, :], qs[:, :, t, :], g, fk, ok)
            cp(Q[:, 512:1024], f[:, 768:1280], g, fk, ok)
            dma(lambda e, o=o, k=k: [e.dma_start(out=W[base + k, :, :], in_=o[:, i * 1024:(i + 1) * 1024])
                                     for i, base in enumerate((SL_A0, SL_A1, SL_Q))],
                reads=[ok], writes=["W"], key=ok, n=3)
        S2_BASE = ARENA_BYTES - (2 * DFF * 4 + DFF * 2)
        PA2 = Arena(arena_t[:], S2_BASE)
        wf2 = [PA2.f32(DFF) for _ in range(2)]
        wb2 = PA2.bf16(DFF)
        prep_steps = []
        p2 = [0]

        def add_step(src_ap, ncols, gi, store_fn):
            def step():
                b = p2[0] % 2
                p2[0] += 1
                f, o = wf2[b], wb2
                fk, ok = "wf2_%d" % b, "wb2"
                dma(lambda e: e.dma_start(out=f[:, 0:ncols], in_=src_ap), writes=[fk], key=fk)
                ew(o[:, 0:ncols], f[:, 0:ncols], gi, False, fk, ok)
                dma(lambda e: store_fn(e, o), reads=[ok], writes=["W"], key=ok)
            prep_steps.append(step)

        for j in range(8):
            add_step(w_out_d[j * 128:(j + 1) * 128, :], 1024, None,
                     lambda e, o, j=j: e.dma_start(out=W[SL_O + j, :, :], in_=o[:, 0:1024]))
        for (wd, base) in ((w_gate_d, SL_G), (w_up_d, SL_U)):
            for k in range(8):
                add_step(wd[k * 128:(k + 1) * 128, :], DFF, 96 + k,
                         lambda e, o, k=k, base=base: e.dma_start(
                             out=W[base:base + NCH, :, k * 128:(k + 1) * 128].rearrange("c p j -> p c j"),
                             in_=o[:, 0:DFF].rearrange("p (c j) -> p c j", j=128)))
        for c in range(NCH):
            add_step(w_down_d[c * 128:(c + 1) * 128, :], 1024, None,
                     lambda e, o, c=c: e.dma_start(out=W[SL_D + c, :, :], in_=o[:, 0:1024]))

        def rstd_from_ss(ss_ap, n, scale, key, rows=128):
            S.op("act", lambda e: e.activation(out=ss_ap, in_=ss_ap, func=AF.Ln, scale=scale, bias=epsc[0:rows, :]),
                 reads=[key, "epsc"], writes=[key])
            S.op("act", lambda e: e.activation(out=ss_ap, in_=ss_ap, func=AF.Exp, scale=-0.5), reads=[key], writes=[key])

        def norm_T(srcs, dstT, dkeyf, NTt, ss, sskey, nb, junk, sq_eng):
            n = len(srcs)
            for i, (xa, xk, rows) in enumerate(srcs):
                if sq_eng == "act":
                    S.op("act", lambda e, xa=xa, i=i, rows=rows: e.activation(out=junk[0:rows, :], in_=xa, func=AF.Square,
                                                                               accum_out=ss[0:rows, i:i + 1]),
                         reads=[xk], writes=["junk", sskey])
                else:
                    S.op("dve", lambda e, xa=xa, i=i, rows=rows: e.scalar_tensor_tensor(
                        out=junk[0:rows, :], in0=xa, scalar=1.0, in1=xa, op0=ALU.mult, op1=ALU.mult,
                        accum_out=ss[0:rows, i:i + 1]), reads=[xk], writes=["junk", sskey])
            rows0 = srcs[0][2]
            rstd_from_ss(ss[0:rows0, 0:n], n, 1.0 / D, sskey, rows0)
            for i, (xa, xk, rows) in enumerate(srcs):
                nbi = nb[i % len(nb)]
                nk = "nb%d" % (i % len(nb))
                S.op("dve", lambda e, xa=xa, i=i, rows=rows, nbi=nbi: e.tensor_scalar(
                    out=nbi[0:rows, :], in0=xa, scalar1=ss[0:rows, i:i + 1], scalar2=None, op0=ALU.mult),
                    reads=[xk, sskey], writes=[nk])
                pb = next_bank()
                psb = bank(pb).bitcast(BF16)
                for j in range(8):
                    S.op("pe", lambda e, j=j, rows=rows, nbi=nbi, psb=psb: e.transpose(
                        out=psb[:, j * 128:j * 128 + rows], in_=nbi[0:rows, j * 128:(j + 1) * 128],
                        identity=identb[0:rows, 0:rows]), reads=[nk, "identb"], writes=["ps%d" % pb])
                ev = "act" if sq_eng == "act" else "dve"
                src_v = psb.rearrange("p (j t) -> p j t", j=8)[:, :, 0:rows]
                dst_v = dstT[:, :, i * 128:i * 128 + rows]
                if ev == "act":
                    S.op("act", lambda e, src_v=src_v, dst_v=dst_v: e.activation(out=dst_v, in_=src_v, func=AF.Copy),
                         writes=["ps%d" % pb] + dkeyf(i))
                else:
                    S.op("dve", lambda e, src_v=src_v, dst_v=dst_v: e.tensor_copy(out=dst_v, in_=src_v),
                         writes=["ps%d" % pb] + dkeyf(i))

        def proj_fm(wslots, wkeys, c0, srcT, skeys, ntok, pb, col_off=0):
            for k in range(8):
                S.op("pe", lambda e, k=k: e.matmul(bank(pb)[:, col_off:col_off + ntok], lhsT=wslots[k][:, c0:c0 + 128],
                                                   rhs=srcT[:, k, 0:ntok], start=(k == 0), stop=(k == 7)),
                     reads=[wkeys[k]] + skeys, writes=["ps%d" % pb])

        def qk_post(kind, pP, pR, ntok, tabs, dsts, gcol, grcol, T):
            P = bank(pP)[:, 0:ntok]
            R = bank(pR)[:, 0:ntok]
            tA, tB, sqb, rq = T["tA"], T["tB"], T["sqb"], T["rq"]
            kk = T.get("keys", {})
            ktA, ktB, ksq, krq = kk.get("tA", "tA"), kk.get("tB", "tB"), kk.get("sqb", "sqb"), kk.get("rq", "rq")
            if kind == "A":
                cosT, sinT = tabs[:, 0, 0:ntok], tabs[:, 1, 0:ntok]
                S.op("act", lambda e: e.activation(out=sqb[:, 0:ntok], in_=P, func=AF.Square), writes=["ps%d" % pP, ksq])
                pS = next_bank()
                S.op("pe", lambda e: e.matmul(bank(pS)[:, 0:ntok], lhsT=onesbd, rhs=sqb[:, 0:ntok], start=True, stop=True),
                     reads=[ksq, "onesbd"], writes=["ps%d" % pS])
                S.op("act", lambda e: e.activation(out=rq[:, 0:ntok], in_=bank(pS)[:, 0:ntok], func=AF.Ln, scale=1.0 / 64,
                                                   bias=epsc), reads=["epsc"], writes=["ps%d" % pS, krq])
                S.op("act", lambda e: e.activation(out=rq[:, 0:ntok], in_=rq[:, 0:ntok], func=AF.Exp, scale=-0.5),
                     reads=[krq], writes=[krq])
                S.op("dve", lambda e: e.scalar_tensor_tensor(out=tA[:, 0:ntok], in0=P, scalar=col(gcol), in1=cosT,
                                                              op0=ALU.mult, op1=ALU.mult),
                     reads=["tabs", "cols"], writes=["ps%d" % pP, ktA])
                S.op("dve", lambda e: e.scalar_tensor_tensor(out=tB[:, 0:ntok], in0=R, scalar=col(grcol), in1=sinT,
                                                              op0=ALU.mult, op1=ALU.mult),
                     reads=["tabs", "cols"], writes=["ps%d" % pR, ktB])
                S.op("pool", lambda e: e.tensor_tensor(out=tA[:, 0:ntok], in0=tA[:, 0:ntok], in1=tB[:, 0:ntok], op=ALU.add),
                     reads=[ktB], writes=[ktA])
                for (dst, p0, p1, dkey) in dsts:
                    S.op("pool", lambda e, dst=dst, p0=p0, p1=p1: e.tensor_tensor(out=dst, in0=tA[p0:p1, 0:ntok], in1=rq[p0:p1, 0:ntok],
                                                                                  op=ALU.mult), reads=[ktA, krq], writes=[dkey])
            else:
                cosT, sinT = tabs[:, 2, 0:ntok], tabs[:, 3, 0:ntok]
                S.op("dve", lambda e: e.tensor_tensor(out=tA[:, 0:ntok], in0=P, in1=cosT, op=ALU.mult),
                     reads=["tabs"], writes=["ps%d" % pP, ktA])
                S.op("dve", lambda e: e.tensor_tensor(out=tB[:, 0:ntok], in0=R, in1=sinT, op=ALU.mult),
                     reads=["tabs"], writes=["ps%d" % pR, ktB])
                for (dst, p0, p1, dkey) in dsts:
                    S.op("pool", lambda e, dst=dst, p0=p0, p1=p1: e.tensor_tensor(out=dst, in0=tA[p0:p1, 0:ntok], in1=tB[p0:p1, 0:ntok],
                                                                                  op=ALU.add), reads=[ktA, ktB], writes=[dkey])

        def do_sequence(sidx, xsrc, ydst, SEQ, NT, init):
            L = SEQ + NMETA
            Lp = L + (L % 2)
            nsb = NT // 128
            ntile = SEQ // NT
            nchunk = 1 + SEQ // 128
            sid = "q%d" % sidx

            KT, VA, VB, KV_END = phase_A(xsrc, SEQ, 512, L, Lp, 4, SEQ // 512, nchunk, init)
            phase_B(xsrc, ydst, SEQ, NT, L, Lp, nsb, ntile, nchunk, KT, VA, VB, KV_END, init)

        def phase_A(xsrc, SEQ, NT, L, Lp, nsb, ntile, nchunk, init):
            S.barrier()
            A = Arena(arena_t[:], CONST_END)
            KT = A.bf16(5 * Lp).rearrange("p (j l) -> p j l", j=5)
            VAf = A.bf16(nchunk * 2 * 72)
            VBf = A.bf16(nchunk * 4 * 136)
            VA = VAf.rearrange("p (c h e) -> p c h e", c=nchunk, h=2)
            VB = VBf.rearrange("p (c h e) -> p c h e", c=nchunk, h=4)
            KV_END = A.off
            wA = A.bf16(16 * 1024).rearrange("p (s c) -> p s c", s=16)
            xr = A.f32(4 * D).rearrange("p (s c) -> p s c", s=4)
            nT = [A.bf16(8 * NT).rearrange("p (k t) -> p k t", k=8) for _ in range(2)]
            nb = [A.bf16(D) for _ in range(2)]
            junk = A.bf16(D)
            tabs = A.f32(4 * NT).rearrange("p (a t) -> p a t", a=4)
            T = {"tA": A.f32(NT), "tB": A.f32(NT), "sqb": A.bf16(NT), "rq": A.f32(NT)}
            ss = A.f32(8)
            PbA = [A.bf16(NT) for _ in range(2)]
            TA2 = None
            if A.off + 14 * NT + 256 <= (S2_BASE if prep_steps else ARENA_BYTES):
                TA2 = {"tA": A.f32(NT), "tB": A.f32(NT), "sqb": A.bf16(NT), "rq": A.f32(NT),
                       "keys": {"tA": "tA2", "tB": "tB2", "rq": "rq2", "sqb": "sqb2"}}
            if prep_steps:
                assert A.off <= S2_BASE, (A.off, S2_BASE)
            per_tile = (len(prep_steps) + ntile) // (ntile + 1) if prep_steps else 0

            if init:
                S.op("pool", lambda e: e.memset(VAf, 1.0), writes=["VAall"])
                S.op("pool", lambda e: e.memset(VBf, 1.0), writes=["VBall"])
            for s in range(16):
                ncl = 896 if s < 8 else 512
                dma(lambda e, s=s, ncl=ncl: e.dma_start(out=wA[:, s, 0:ncl], in_=W[s, :, 0:ncl]), reads=["W"], writes=[("wA", s)],
                    key=("wA", s))
            wA0 = [wA[:, k, :] for k in range(8)]
            wA1 = [wA[:, 8 + k, :] for k in range(8)]
            kA0 = [("wA", k) for k in range(8)]
            kA1 = [("wA", 8 + k) for k in range(8)]

            xctr = [0]
            tiles = [("meta", 0)] + [("real", i) for i in range(ntile)]

            def prepare(kind, ti):
                if kind == "meta":
                    ntok, col0 = NMETA, 0
                    sbs = [(meta_d[:, :], NMETA, 0)]
                else:
                    ntok, col0 = NT, NMETA + ti * NT
                    sbs = [(xsrc[ti * NT + s * 128: ti * NT + (s + 1) * 128, :], 128, 1 + ti * nsb + s) for s in range(nsb)]
                srcs = []
                for (src, rows, chunk) in sbs:
                    sl = xctr[0] % 4
                    xctr[0] += 1
                    dma(lambda e, sl=sl, src=src, rows=rows: e.dma_start(out=xr[0:rows, sl, :], in_=src),
                        writes=[("x", sl)], key=("x", sl))
                    srcs.append((xr[0:rows, sl, :], ("x", sl), rows))
                nTb = nT[(ti + 1) % 2 if kind == "real" else 0]
                nTk = "nT%d" % ((ti + 1) % 2 if kind == "real" else 0)
                norm_T(srcs, nTb, (lambda i, nTk=nTk: [(nTk, i)]), NT, ss, "ss", nb, junk, "act")
                nTkeys = [(nTk, i) for i in range(len(srcs))]
                return (kind, ti, ntok, col0, sbs, nTb, nTk, nTkeys)

            def process(kind, ti, ntok, col0, sbs, nTb, nTk, nTkeys):
                dma(lambda e, col0=col0, ntok=ntok: e.dma_start(
                    out=tabs[:, :, 0:ntok], in_=tabs_d[:, :, col0:col0 + ntok].rearrange("a p n -> p a n")),
                    writes=["tabs"], key="tabs")
                def k_finish(jt, pP, kind=kind, ti=ti, ntok=ntok, col0=col0):
                    pR = next_bank()
                    pb, pbk = PbA[jt % 2], "pbA%d" % (jt % 2)
                    S.op("pe", lambda e: e.matmul(bank(pR)[:, 0:ntok], lhsT=rmat, rhs=pb[:, 0:ntok], start=True, stop=True),
                         reads=[pbk, "rmat"], writes=["ps%d" % pR])
                    qk_post("A" if jt == 0 else "B", pP, pR, ntok, tabs,
                            [(KT[:, jt, col0:col0 + ntok], 0, 128, ("K", kind, ti))], 106, 107,
                            T if (jt % 2 == 0 or TA2 is None) else TA2)
                prev = None
                for jt in range(5):
                    pP = next_bank()
                    if jt == 0:
                        proj_fm(wA0, kA0, 0, nTb, nTkeys, ntok, pP)
                    else:
                        proj_fm(wA1, kA1, (jt - 1) * 128, nTb, nTkeys, ntok, pP)
                    S.op("act", lambda e, jt=jt, pP=pP, ntok=ntok: e.activation(out=PbA[jt % 2][:, 0:ntok], in_=bank(pP)[:, 0:ntok],
                                                                                func=AF.Copy), writes=["ps%d" % pP, "pbA%d" % (jt % 2)])
                    if prev is not None:
                        k_finish(*prev)
                    prev = (jt, pP)
                k_finish(*prev)
                for i, (src, rows, chunk) in enumerate(sbs):
                    pa, pbk = next_bank(), next_bank()
                    for k in range(8):
                        S.op("pe", lambda e, k=k, i=i, rows=rows, pa=pa, nTb=nTb: e.matmul(
                            bank(pa)[0:rows, 0:128], lhsT=nTb[:, k, i * 128:i * 128 + rows], rhs=wA0[k][:, 256:384],
                            start=(k == 0), stop=(k == 7)), reads=[kA0[k], (nTk, i)], writes=["ps%d" % pa])
                    for k in range(8):
                        S.op("pe", lambda e, k=k, i=i, rows=rows, pbk=pbk, nTb=nTb: e.matmul(
                            bank(pbk)[0:rows, 0:512], lhsT=nTb[:, k, i * 128:i * 128 + rows], rhs=wA0[k][:, 384:896],
                            start=(k == 0), stop=(k == 7)), reads=[kA0[k], (nTk, i)], writes=["ps%d" % pbk])
                    S.op("act", lambda e, rows=rows, chunk=chunk, pa=pa: e.activation(
                        out=VA[0:rows, chunk, :, 0:64], in_=bank(pa)[0:rows, 0:128].rearrange("p (h e) -> p h e", h=2),
                        func=AF.Copy), reads=["VAall"], writes=["ps%d" % pa, ("VA", chunk)])
                    S.op("act", lambda e, rows=rows, chunk=chunk, pbk=pbk: e.activation(
                        out=VB[0:rows, chunk, :, 0:128], in_=bank(pbk)[0:rows, 0:512].rearrange("p (h e) -> p h e", h=4),
                        func=AF.Copy), reads=["VBall"], writes=["ps%d" % pbk, ("VB", chunk)])
                for _ in range(per_tile):
                    if prep_steps:
                        prep_steps.pop(0)()

            stt = prepare(*tiles[0])
            for tidx in range(len(tiles)):
                nxt = prepare(*tiles[tidx + 1]) if tidx + 1 < len(tiles) else None
                process(*stt)
                stt = nxt

            while prep_steps:
                prep_steps.pop(0)()

            dump("KT", KT, [("K", k_, t_) for (k_, t_) in tiles])
            dump("wA", wA, [("wA", s_) for s_ in range(16)])
            dump("tA", T["tA"], ["tA"])
            dump("tB", T["tB"], ["tB"])
            dump("tabsA", tabs, ["tabs"])
            dump("VA", VA, [("VA", c_) for c_ in range(nchunk)])
            dump("VB", VB, [("VB", c_) for c_ in range(nchunk)])
            dump("nTA", nT[0], ["nT0", ("nT0", 0), ("nT0", 1)])
            return KT, VA, VB, KV_END

        def phase_B(xsrc, ydst, SEQ, NT, L, Lp, nsb, ntile, nchunk, KT, VA, VB, KV_END, init):
            S.barrier()
            B = Arena(arena_t[:], KV_END)
            NW = 8
            NPT = 3 if NT <= 256 else 2
            NX = 5
            PREFETCH_X = (2 * nsb + 1 <= NX)
            wr = B.bf16(NW * 1024).rearrange("p (s c) -> p s c", s=NW)
            act_bytes = NCH * NT * 2
            sqt_off = (act_bytes + 2047) // 2048 * 2048
            znt_off = 8 * 2048
            zbytes = max(znt_off + 8 * NT * 2, sqt_off + nsb * 512 * 4)
            nz = (zbytes + 2047) // 2048
            Zf = B.f32(nz * 512)
            Zq = Zf[:, 0:8 * 512].bitcast(BF16).rearrange("p (s c) -> p s c", s=8)
            act = Zf[:, 0:act_bytes // 4].bitcast(BF16).rearrange("p (c t) -> p c t", c=NCH)
            sqt = Zf[:, sqt_off // 4:sqt_off // 4 + nsb * 512]

            def zkeys(lo, hi):
                return [("Z", z) for z in range(lo // 2048, (hi - 1) // 2048 + 1)]

            def act_keys(c):
                return zkeys(c * NT * 2, (c + 1) * NT * 2)
            sqt_keys = zkeys(sqt_off, sqt_off + nsb * 512 * 4)
            ZnT = Zf[:, znt_off // 4:znt_off // 4 + 4 * NT].bitcast(BF16).rearrange("p (k t) -> p k t", k=8)
            znt_keys = zkeys(znt_off, znt_off + 8 * NT * 2)
            xr = B.f32(NX * D).rearrange("p (s c) -> p s c", s=NX)
            nT1 = B.bf16(8 * NT).rearrange("p (k t) -> p k t", k=8)
            QTf = B.bf16(16 * NT)
            QT = QTf.rearrange("p (j t) -> p j t", j=16)
            PT = [B.bf16(1024) for _ in range(NPT)]
            mix = B.bf16(nsb * D).rearrange("p (s c) -> p s c", s=nsb)
            oB = B.f32(nsb * 512).rearrange("p (s h e) -> p s h e", s=nsb, h=4)
            tB2 = B.f32(nsb * 128).rearrange("p (s e) -> p s e", s=nsb)
            nb = [B.bf16(D) for _ in range(2)]
            junk = PT[0]
            tabs = B.f32(4 * NT).rearrange("p (a t) -> p a t", a=4)
            T = {"tA": B.f32(NT), "tB": B.f32(NT), "sqb": B.bf16(NT), "rq": B.f32(NT)}
            pt_alias = False
            if NPT == 2 and NT * 4 >= 2048:
                PT.append(T["rq"][:, 0:512].bitcast(BF16))
                NPT = 3
                pt_alias = True

            def ptkeys(i):
                return [("PT", i)] + (["rq"] if (pt_alias and i == 2) else [])
            ff0 = B.off
            GW = NT + 132
            GU = [B.bf16(2 * GW).rearrange("p (a t) -> p a t", a=2) for _ in range(3)]
            cvall = B.f32(2 * NT)
            geall = B.bf16(2 * NT)
            cv = [cvall[:, i * NT:(i + 1) * NT] for i in range(2)]
            ge = [geall[:, i * NT:(i + 1) * NT] for i in range(2)]
            if B.off - ff0 < 4 * D:
                B.f32((4 * D - (B.off - ff0)) // 4 + 8)
            xm = arena_t[:][:, ff0 // 4: ff0 // 4 + D]
            GUcf = B.bf16(NCH * 2 * 130)
            GUc = GUcf.rearrange("p (c a t) -> p c a t", c=NCH, a=2)
            ss = B.f32(8)
            ssB = B.f32(4 * nsb)
            rs = B.f32(8)
            ssF = B.f32(8)
            print("phase B arena end: %d B of %d (NT=%d L=%d)" % (B.off, ARENA_BYTES, NT, L))

            S.op("dve", lambda e: e.memset(QTf, 0.0), writes=[("QT", j) for j in range(16)])
            S.op("dve", lambda e: e.memset(GUcf, 0.0), writes=[("Gc", c) for c in range(NCH)])

            wctr = [0]

            def wload(slot_idx):
                sl = wctr[0] % NW
                wctr[0] += 1
                dma(lambda e, sl=sl, slot_idx=slot_idx: e.dma_start(out=wr[:, sl, :], in_=W[slot_idx, :, :]),
                    reads=["W"], writes=[("w", sl)], key=("w", sl))
                return wr[:, sl, :], ("w", sl)

            def next_nT():
                return nT1, "nT1"

            def xslot(g):
                return g % NX

            def load_x_tile(ti):
                for s in range(nsb):
                    g = ti * nsb + s
                    sl = xslot(g)
                    dma(lambda e, sl=sl, ti=ti, s=s: e.dma_start(out=xr[:, sl, :], in_=xsrc[ti * NT + s * 128: ti * NT + (s + 1) * 128, :]),
                        writes=[("x", sl)], key=("x", sl))

            def attention(ntok, qrows, nsbq):
                hcs = []
                for h in range(8):
                    hcs.append(("A", h, h % 4, 0, (h // 4) * 64, VA, h // 4, 64))
                for hb in range(4):
                    for cp_ in range(2):
                        hcs.append(("B", (hb, cp_), 4 + hb, 1 + hb, cp_ * 64, VB, hb, 128))
                cstride = NT if ntok == NT else ntok
                G = 1024 // cstride
                groups = [[0]]
                real = list(range(1, nchunk))
                for i in range(0, len(real), G):
                    groups.append(real[i:i + G])
                steps = [(hi, gi) for hi in range(len(hcs)) for gi in range(len(groups))]

                def crow(c):
                    return NMETA if c == 0 else 128

                def ccol(c):
                    return 0 if c == 0 else NMETA + (c - 1) * 128

                def sb_base(si):
                    return (si % 2) * 1024

                def acc_base(hi):
                    return 2048 + (hi % 2) * 1024

                def acc_region(hi, dv, sb):
                    w = dv + 1
                    per = 512 // w
                    return acc_base(hi) + (sb // per) * 512 + (sb % per) * w

                def emit_qk(si):
                    hi, gi = steps[si]
                    _, _, qt, kt, r0, _, _, _ = hcs[hi]
                    base = sb_base(si)
                    for li, c in enumerate(groups[gi]):
                        rows = crow(c)
                        o0 = base + li * cstride
                        bk = o0 // 512
                        qi = 2 * qt + r0 // 64
                        S.op("pe", lambda e, rows=rows, o0=o0, kt=kt, c=c, qi=qi: e.matmul(
                            PS[0:rows, o0:o0 + ntok], lhsT=KT[:, kt, ccol(c):ccol(c) + rows],
                            rhs=QT[:, qi, 0:ntok], start=True, stop=True),
                            reads=[("QT", qi)], writes=["ps%d" % bk])

                def emit_exp(si):
                    hi, gi = steps[si]
                    grp = groups[gi]
                    rows = crow(grp[0])
                    base = sb_base(si)
                    pt = PT[si % NPT]
                    n = len(grp)
                    src = PS[0:rows, base:base + n * cstride].rearrange("p (g t) -> p g t", g=n)[:, :, 0:ntok]
                    dst = pt[0:rows, 0:n * cstride].rearrange("p (g t) -> p g t", g=n)[:, :, 0:ntok]
                    bks = sorted(set((base + li * cstride) // 512 for li in range(n)))
                    S.op("act", lambda e, src=src, dst=dst: e.activation(out=dst, in_=src, func=AF.Exp, scale=0.125),
                         writes=["ps%d" % b for b in bks] + ptkeys(si % NPT))

                def emit_pv(si):
                    hi, gi = steps[si]
                    kind, _, _, _, _, Vs, vh, dv = hcs[hi]
                    pt = PT[si % NPT]
                    for li, c in enumerate(groups[gi]):
                        rows = crow(c)
                        for sb in range(nsbq):
                            o0 = acc_region(hi, dv, sb)
                            bk = o0 // 512
                            first_in_bank = (gi == 0 and li == 0 and (o0 % 512) == 0)
                            last = (gi == len(groups) - 1 and li == len(groups[gi]) - 1)
                            S.op("pe", lambda e, rows=rows, o0=o0, li=li, sb=sb, c=c, dv=dv, vh=vh, Vs=Vs, pt=pt,
                                 fb=first_in_bank, last=last: e.matmul(
                                PS[0:qrows, o0:o0 + dv + 1], lhsT=pt[0:rows, li * cstride + sb * 128: li * cstride + sb * 128 + qrows],
                                rhs=Vs[0:rows, c, vh, 0:dv + 1], start=fb, stop=last, skip_group_check=True),
                                reads=ptkeys(si % NPT), writes=["ps%d" % bk])

                def emit_evac(hi):
                    kind, hid, _, _, _, _, _, dv = hcs[hi]
                    w = dv + 1
                    per = 512 // w
                    segs = []
                    sb = 0
                    while sb < nsbq:
                        cnt = min(per - (sb % per), nsbq - sb)
                        segs.append((sb, cnt))
                        sb += cnt
                    for (sb0, cnt) in segs:
                        o0 = acc_region(hi, dv, sb0)
                        bk = "ps%d" % (o0 // 512)
                        v = PS[0:qrows, o0:o0 + cnt * w].rearrange("p (s e) -> p s e", s=cnt)
                        rsv = rs[0:qrows, sb0:sb0 + cnt]
                        S.op("dve", lambda e, v=v, rsv=rsv, dv=dv: e.reciprocal(out=rsv, in_=v[:, :, dv]),
                             writes=[bk, "rs"])
                        if kind == "A":
                            h = hid
                            dst = mix[0:qrows, sb0:sb0 + cnt, h * 64:(h + 1) * 64]
                            S.op("dve", lambda e, v=v, rsv=rsv, dst=dst, cnt=cnt: e.tensor_tensor(
                                out=dst, in0=v[:, :, 0:64], in1=rsv[:, :, None].broadcast_to([qrows, cnt, 64]), op=ALU.mult),
                                reads=["rs"], writes=[bk, ("mix", h // 2)])
                        else:
                            hb, cp_ = hid
                            dst = oB[0:qrows, sb0:sb0 + cnt, hb, :]
                            if cp_ == 0:
                                S.op("dve", lambda e, v=v, rsv=rsv, dst=dst, cnt=cnt: e.tensor_tensor(
                                    out=dst, in0=v[:, :, 0:128], in1=rsv[:, :, None].broadcast_to([qrows, cnt, 128]), op=ALU.mult),
                                    reads=["rs"], writes=[bk, ("oB", hb)])
                            else:
                                S.op("dve", lambda e, rsv=rsv: e.tensor_scalar(out=rsv, in0=rsv, scalar1=neglam[0:qrows, :],
                                                                               scalar2=None, op0=ALU.mult),
                                     reads=["lams"], writes=["rs"])
                                t2 = tB2[0:qrows, sb0:sb0 + cnt, :]
                                S.op("dve", lambda e, v=v, rsv=rsv, t2=t2, cnt=cnt: e.tensor_tensor(
                                    out=t2, in0=v[:, :, 0:128], in1=rsv[:, :, None].broadcast_to([qrows, cnt, 128]), op=ALU.mult),
                                    reads=["rs"], writes=[bk, "tB2"])
                                S.op("pool", lambda e, dst=dst, t2=t2: e.tensor_tensor(out=dst, in0=dst, in1=t2, op=ALU.add),
                                     reads=["tB2"], writes=[("oB", hb)])

                ng = len(groups)
                emit_qk(0)
                for si in range(len(steps)):
                    if si + 1 < len(steps):
                        emit_qk(si + 1)
                    emit_exp(si)
                    emit_pv(si)
                    if steps[si][1] == ng - 1:
                        emit_evac(steps[si][0])
                n4 = nsbq * 4
                oBf = oB[0:qrows, 0:nsbq, :, :].rearrange("p s h e -> p (s h) e")
                sqv = sqt[0:qrows, 0:n4 * 128].rearrange("p (a e) -> p a e", a=n4)
                S.op("dve", lambda e: e.tensor_tensor(out=sqv, in0=oBf, in1=oBf, op=ALU.mult),
                     reads=[("oB", h) for h in range(4)], writes=sqt_keys)
                S.op("dve", lambda e: e.tensor_reduce(out=ssB[0:qrows, 0:n4], in_=sqv, axis=AX.X, op=ALU.add),
                     reads=sqt_keys, writes=["ssB"])
                rstd_from_ss(ssB[0:qrows, 0:n4], n4, 1.0 / 128, "ssB", qrows)
                S.op("dve", lambda e: e.tensor_tensor(out=oBf, in0=oBf, in1=ssB[0:qrows, 0:n4, None].broadcast_to([qrows, n4, 128]),
                                                      op=ALU.mult), reads=["ssB"], writes=[("oB", h) for h in range(4)])
                for sb in range(nsbq):
                    S.op("dve", lambda e, sb=sb: e.scalar_tensor_tensor(
                        out=mix[0:qrows, sb, 512:1024].rearrange("p (h e) -> p h e", h=4), in0=oB[0:qrows, sb, :, :],
                        scalar=1.0 - LAM_INIT, in1=gsub[0:qrows, None, :].broadcast_to([qrows, 4, 128]),
                        op0=ALU.mult, op1=ALU.mult), reads=[("oB", h) for h in range(4)] + ["gsub"],
                        writes=[("mix", 4 + h) for h in range(4)])

            gctr = [0]

            def ffn(mode, n2T, n2keys, ncur, wsbs, gbase, ybase):
                if mode != "flush":
                    pend = None
                    for c in range(NCH):
                        wg, wgk = wload(SL_G + c)
                        if mode == "real":
                            wu, wuk = wload(SL_U + c)
                        pG = next_bank()
                        for k in range(8):
                            S.op("pe", lambda e, k=k, wg=wg, pG=pG: e.matmul(
                                bank(pG)[:, 0:ncur], lhsT=wg[:, k * 128:(k + 1) * 128], rhs=n2T[:, k, 0:ncur],
                                start=(k == 0), stop=(k == 7)), reads=[wgk] + n2keys, writes=["ps%d" % pG])
                        if mode == "meta":
                            S.op("dve", lambda e, c=c, pG=pG: e.tensor_copy(out=GUc[:, c, 0, 128:129], in_=bank(pG)[:, NMETA - 1:NMETA]),
                                 writes=["ps%d" % pG, ("Gc", c)])
                            continue
                        pU = next_bank()
                        for k in range(8):
                            S.op("pe", lambda e, k=k, wu=wu, pU=pU: e.matmul(
                                bank(pU)[:, 0:ncur], lhsT=wu[:, k * 128:(k + 1) * 128], rhs=n2T[:, k, 0:ncur],
                                start=(k == 0), stop=(k == 7)), reads=[wuk] + n2keys, writes=["ps%d" % pU])
                        st = ffn_chunk_a(c, pG, pU, NT)
                        if pend is not None:
                            ffn_chunk_b(*pend)
                        pend = st
                    if pend is not None:
                        ffn_chunk_b(*pend)
                else:
                    gs = 2 * NT // 128
                    for c0 in range(0, NCH, gs):
                        n = min(gs, NCH - c0)
                        Gv = GUc[:, c0:c0 + n, 0, :]
                        Uv = GUc[:, c0:c0 + n, 1, 1:129]
                        t = cvall[:, 0:n * 128].rearrange("p (c t) -> p c t", c=n)
                        u = sqt[:, 0:n * 128].rearrange("p (c t) -> p c t", c=n)
                        gv = geall[:, 0:n * 128].rearrange("p (c t) -> p c t", c=n)
                        gck = [("Gc", c) for c in range(c0, c0 + n)]

                        def wb(base, c0=c0, n=n):
                            return cols[:, base + c0:base + c0 + n, None].broadcast_to([128, n, 128])
                        S.op("dve", lambda e, Gv=Gv, t=t, wb=wb: e.tensor_tensor(out=t, in0=Gv[:, :, 0:128], in1=wb(0), op=ALU.mult),
                             reads=gck + ["cols"], writes=["cv0", "cv1"])
                        for tap in (1, 2):
                            S.op("dve", lambda e, Gv=Gv, u=u, wb=wb, tap=tap: e.tensor_tensor(
                                out=u, in0=Gv[:, :, tap:tap + 128], in1=wb(22 * tap), op=ALU.mult), reads=gck + ["cols"], writes=sqt_keys)
                            S.op("dve", lambda e, t=t, u=u: e.tensor_tensor(out=t, in0=t, in1=u, op=ALU.add),
                                 reads=sqt_keys, writes=["cv0", "cv1"])
                        S.op("dve", lambda e, t=t, wb=wb: e.tensor_tensor(out=t, in0=t, in1=wb(66), op=ALU.add),
                             reads=["cols"], writes=["cv0", "cv1"])
                        S.op("act", lambda e, t=t, gv=gv: e.activation(out=gv, in_=t, func=AF.Gelu), reads=["cv0", "cv1"],
                             writes=["ge0", "ge1"])
                        ak = []
                        for c in range(c0, c0 + n):
                            ak += act_keys(c)
                        S.op("pool", lambda e, gv=gv, Uv=Uv, c0=c0, n=n: e.tensor_tensor(out=act[:, c0:c0 + n, 0:128], in0=gv, in1=Uv,
                                                                                       op=ALU.mult),
                             reads=["ge0", "ge1"] + gck, writes=sorted(set(ak)))
                if mode == "meta":
                    return
                banks = {}
                for w in wsbs:
                    for half in range(2):
                        banks[(w, half)] = next_bank()
                for c in range(NCH):
                    wd, wdk = wload(SL_D + c)
                    for w in wsbs:
                        for half in range(2):
                            pb = banks[(w, half)]
                            S.op("pe", lambda e, c=c, w=w, half=half, pb=pb, wd=wd: e.matmul(
                                bank(pb)[:, 0:512], lhsT=act[:, c, w * 128:(w + 1) * 128], rhs=wd[:, half * 512:(half + 1) * 512],
                                start=(c == 0), stop=(c == NCH - 1)), reads=[wdk] + act_keys(c), writes=["ps%d" % pb])
                fin = []
                for wi, w in enumerate(wsbs):
                    sl = xslot(gbase + w)
                    xa = xr[:, sl, :]
                    xk = ("x", sl)
                    fin.append((wi, w, sl, xa, xk))
                    for half in range(2):
                        pb = banks[(w, half)]
                        S.op("dve", lambda e, xa=xa, half=half, pb=pb: e.tensor_tensor(
                            out=xa[:, half * 512:(half + 1) * 512], in0=bank(pb)[:, 0:512], in1=xa[:, half * 512:(half + 1) * 512],
                            op=ALU.add), writes=["ps%d" % pb, xk])
                for (wi, w, sl, xa, xk) in fin:
                    S.op("act", lambda e, xa=xa, wi=wi: e.activation(out=junk, in_=xa, func=AF.Square, accum_out=ssF[:, wi:wi + 1]),
                         reads=[xk], writes=["junk", ("ssF", wi)])
                nw = len(fin)
                if nw:
                    sk = [("ssF", wi) for wi in range(nw)]
                    S.op("act", lambda e: e.activation(out=ssF[:, 0:nw], in_=ssF[:, 0:nw], func=AF.Ln, scale=1.0 / D, bias=epsc),
                         reads=["epsc"], writes=sk)
                    S.op("act", lambda e: e.activation(out=ssF[:, 0:nw], in_=ssF[:, 0:nw], func=AF.Exp, scale=-0.5), writes=sk)
                for (wi, w, sl, xa, xk) in fin:
                    S.op("dve", lambda e, xa=xa, wi=wi: e.scalar_tensor_tensor(
                        out=xa, in0=xa, scalar=ssF[:, wi:wi + 1], in1=gfin, op0=ALU.mult, op1=ALU.mult),
                        reads=[("ssF", wi), "gfin"], writes=[xk])
                    r0 = ybase + w * 128
                    dma(lambda e, xa=xa, r0=r0: e.dma_start(out=ydst[r0:r0 + 128, :], in_=xa), reads=[xk], key=("y", sl))

            def ffn_chunk_a(c, pG, pU, nwin):
                i3 = gctr[0] % 3
                i2 = gctr[0] % 2
                gctr[0] += 1
                gu, cvb, geb = GU[i3], cv[i2], ge[i2]
                kc, kg, ku, ck, ek = ("gu", i3, "c"), ("gu", i3, "g"), ("gu", i3, "u"), "cv%d" % i2, "ge%d" % i2
                S.op("pool", lambda e: e.tensor_copy(out=gu[:, :, 0:129], in_=GUc[:, c, :, 0:129]), reads=[("Gc", c)], writes=[kc])
                if pG is not None:
                    S.op("act", lambda e: e.activation(out=gu[:, 0, 129:129 + NT], in_=bank(pG)[:, 0:NT], func=AF.Copy),
                         writes=["ps%d" % pG, kg])
                    S.op("dve", lambda e: e.tensor_copy(out=gu[:, 1, 129:129 + NT], in_=bank(pU)[:, 0:NT]),
                         writes=["ps%d" % pU, ku])
                    S.op("pool", lambda e: e.tensor_copy(out=GUc[:, c, :, 0:129], in_=gu[:, :, NT:NT + 129]),
                         reads=[kc, kg, ku], writes=[("Gc", c)])
                else:
                    S.op("pool", lambda e: e.memset(gu[:, 0, 129:131], 0.0), writes=[kg])
                return (c, nwin, gu, cvb, geb, kc, kg, ku, ck, ek)

            def ffn_chunk_b(c, nwin, gu, cvb, geb, kc, kg, ku, ck, ek):
                w0, w1, w2, bb = col(c), col(22 + c), col(44 + c), col(66 + c)
                Gb = gu[:, 0, :]
                S.op("dve", lambda e: e.tensor_scalar(out=cvb[:, 0:nwin], in0=Gb[:, 0:nwin], scalar1=w0, scalar2=bb,
                                                      op0=ALU.mult, op1=ALU.add), reads=[kc, kg, "cols"], writes=[ck])
                S.op("dve", lambda e: e.scalar_tensor_tensor(out=cvb[:, 0:nwin], in0=Gb[:, 1:nwin + 1], scalar=w1, in1=cvb[:, 0:nwin],
                                                             op0=ALU.mult, op1=ALU.add), reads=[kc, kg, "cols"], writes=[ck])
                S.op("dve", lambda e: e.scalar_tensor_tensor(out=cvb[:, 0:nwin], in0=Gb[:, 2:nwin + 2], scalar=w2, in1=cvb[:, 0:nwin],
                                                             op0=ALU.mult, op1=ALU.add), reads=[kc, kg, "cols"], writes=[ck])
                S.op("act", lambda e: e.activation(out=geb[:, 0:nwin], in_=cvb[:, 0:nwin], func=AF.Gelu), reads=[ck], writes=[ek])
                S.op("pool", lambda e: e.tensor_tensor(out=act[:, c, 0:nwin], in0=geb[:, 0:nwin], in1=gu[:, 1, 1:nwin + 1], op=ALU.mult),
                     reads=[ek, kc, ku], writes=act_keys(c))

            def mixer_and_norm(kind, ti, srcs, ntok, qrows, nsbq, col0):
                nTb = ZnT
                norm_T(srcs, nTb, (lambda i: znt_keys), NT, ss, "ss", nb, junk, "act")
                nTkeys = znt_keys
                dma(lambda e: e.dma_start(out=tabs[:, :, 0:ntok], in_=tabs_d[:, :, col0:col0 + ntok].rearrange("a p n -> p a n")),
                    writes=["tabs"], key="tabs")
                wq = [wload(SL_Q + k) for k in range(8)]
                Pb = [Zf[:, i * 512:(i + 1) * 512].bitcast(BF16)[:, 0:NT] for i in range(2)]
                T2 = {"tA": Zf[:, 2 * 512:2 * 512 + NT], "tB": Zf[:, 3 * 512:3 * 512 + NT], "rq": Zf[:, 4 * 512:4 * 512 + NT],
                      "sqb": Zf[:, 5 * 512:6 * 512].bitcast(BF16)[:, 0:NT],
                      "keys": {"tA": ("Z", 2), "tB": ("Z", 3), "rq": ("Z", 4), "sqb": ("Z", 5)}}

                def q_finish(j, pP):
                    pR = next_bank()
                    pb, pbk = Pb[j % 2], ("Z", j % 2)
                    S.op("pe", lambda e: e.matmul(bank(pR)[:, 0:ntok], lhsT=rmat, rhs=pb[:, 0:ntok], start=True, stop=True),
                         reads=[pbk, "rmat"], writes=["ps%d" % pR])
                    qk_post("A" if j < 4 else "B", pP, pR, ntok, tabs,
                            [(QT[0:64, 2 * j, 0:ntok], 0, 64, ("QT", 2 * j)), (QT[64:128, 2 * j + 1, 0:ntok], 64, 128, ("QT", 2 * j + 1))],
                            104, 105, T if j % 2 == 0 else T2)
                prev = None
                for j in range(8):
                    pP = next_bank()
                    proj_fm([w[0] for w in wq], [w[1] for w in wq], j * 128, nTb, nTkeys, ntok, pP)
                    S.op("act", lambda e, j=j, pP=pP: e.activation(out=Pb[j % 2][:, 0:ntok], in_=bank(pP)[:, 0:ntok], func=AF.Copy),
                         writes=["ps%d" % pP, ("Z", j % 2)])
                    if prev is not None:
                        q_finish(*prev)
                    prev = (j, pP)
                q_finish(*prev)
                if kind == "real" and ti == 0:
                    dump("QT", QT, [("QT", j_) for j_ in range(16)])
                attention(ntok, qrows, nsbq)
                if kind == "real" and ti == 0:
                    dump("mix", mix, [("mix", j_) for j_ in range(8)])
                mTb, mTk = next_nT()
                for sb in range(nsbq):
                    pb = next_bank()
                    psb = bank(pb).bitcast(BF16)
                    for j in range(8):
                        S.op("pe", lambda e, j=j, sb=sb, psb=psb: e.transpose(
                            out=psb[:, j * 128:j * 128 + qrows], in_=mix[0:qrows, sb, j * 128:(j + 1) * 128],
                            identity=identb[0:qrows, 0:qrows]), reads=[("mix", j), "identb"], writes=["ps%d" % pb])
                    S.op("dve", lambda e, sb=sb, psb=psb: e.tensor_copy(
                        out=mTb[:, :, sb * 128:sb * 128 + qrows], in_=psb.rearrange("p (j t) -> p j t", j=8)[:, :, 0:qrows]),
                        writes=["ps%d" % pb, (mTk, sb)])
                wo = [wload(SL_O + j) for j in range(8)]
                for sb, (xa, xk, rows) in enumerate(srcs):
                    for half in range(2):
                        pb = next_bank()
                        for j in range(8):
                            S.op("pe", lambda e, j=j, sb=sb, half=half, pb=pb, rows=rows: e.matmul(
                                bank(pb)[0:rows, 0:512], lhsT=mTb[:, j, sb * 128:sb * 128 + rows],
                                rhs=wo[j][0][:, half * 512:(half + 1) * 512], start=(j == 0), stop=(j == 7)),
                                reads=[wo[j][1], (mTk, sb)], writes=["ps%d" % pb])
                        S.op("dve", lambda e, xa=xa, half=half, pb=pb, rows=rows: e.tensor_tensor(
                            out=xa[:, half * 512:(half + 1) * 512], in0=bank(pb)[0:rows, 0:512],
                            in1=xa[:, half * 512:(half + 1) * 512], op=ALU.add), writes=["ps%d" % pb, xk])
                if kind == "real" and ti == 0:
                    dump("h1", srcs[0][0], [srcs[0][1]])
                n2b, n2k = next_nT()
                norm_T(srcs, n2b, (lambda i, n2k=n2k: [(n2k, i)]), NT, ss, "ss", nb, junk, "act")
                if kind == "real" and ti == 0:
                    dump("n2T", n2b, [(n2k, i_) for i_ in range(len(srcs))])
                return n2b, [(n2k, i) for i in range(len(srcs))]

            dma(lambda e: e.dma_start(out=xm[0:NMETA, :], in_=meta_d[:, :]), writes=["xm"], key="xm")
            if PREFETCH_X:
                load_x_tile(0)
            n2b, n2keys = mixer_and_norm("meta", 0, [(xm[0:NMETA, :], "xm", NMETA)], NMETA, NMETA, 1, 0)
            ffn("meta", n2b, n2keys, NMETA, [], 0, 0)
            S.barrier()
            for ti in range(ntile):
                if PREFETCH_X:
                    if ti + 1 < ntile:
                        load_x_tile(ti + 1)
                else:
                    load_x_tile(ti)
                srcs = [(xr[:, xslot(ti * nsb + s), :], ("x", xslot(ti * nsb + s)), 128) for s in range(nsb)]
                n2b, n2keys = mixer_and_norm("real", ti, srcs, NT, 128, nsb, NMETA + ti * NT)
                wsbs = list(range(nsb)) if ti > 0 else list(range(1, nsb))
                ffn("real", n2b, n2keys, NT, wsbs, ti * nsb - 1, ti * NT - 128)
            ffn("flush", None, None, 0, [0], ntile * nsb - 1, SEQ - 128)

        seen = set()
        for sidx, (kind, i, NT) in enumerate(seq_cfg):
            init = (kind, NT) not in seen
            seen.add((kind, NT))
            if kind == "p":
                do_sequence(sidx, xp, yp, SEQ_P, NT, init)
            else:
                do_sequence(sidx, xs[i], ys[i], SEQ_S, NT, init)

        S.emit(nc, st)
    return nc


def _rope_tables():
    f32 = np.float32
    theta = f32(10000.0)
    t = np.arange(SEQ_P)
    row = (t // 64).astype(f32)
    colp = (t % 64).astype(f32)
    inv16 = (theta ** (-(np.arange(0, 32, 2, dtype=f32)) / f32(32))).astype(f32)
    ang = np.concatenate([row[:, None] * inv16[None], colp[:, None] * inv16[None]], axis=-1).astype(f32)
    ang_a = np.concatenate([np.zeros((NMETA, 32), f32), ang], axis=0)
    pos = np.arange(LMAX, dtype=f32)
    inv32 = (theta ** (-(np.arange(0, 64, 2, dtype=f32)) / f32(64))).astype(f32)
    ang_b = (pos[:, None] * inv32[None]).astype(f32)
    tabs = np.empty((4, 128, LMAX), f32)
    for a, src in enumerate((np.cos(ang_a), np.sin(ang_a), np.cos(ang_b), np.sin(ang_b))):
        tabs[a] = np.tile(src.T.astype(f32), (4, 1))
    return tabs


def _rot_matrix():
    r = np.zeros((128, 128), np.float32)
    for m in range(128):
        h, i = divmod(m, 64)
        if i < 32:
            r[h * 64 + i + 32, m] = -1.0
        else:
            r[h * 64 + i - 32, m] = 1.0
    return r


_CACHE = {}
DBG_ON = False


def kernel(x_prompt, x_sample, meta_tokens, g_mix, w_in, g_qnorm_a, g_knorm_a, lambda_q1, lambda_k1,
           lambda_q2, lambda_k2, g_subln, w_out, g_ffn, w_ff_gate, w_ff_up, conv_w, conv_b, w_ff_down, g_final):
    seq_cfg = [("p", 0, 256)] + [("s", i, 512) for i in range(NSAMP)]
    if "nc" not in _CACHE:
        _CACHE["nc"] = build_program(seq_cfg)
    nc = _CACHE["nc"]
    f = _f
    shared = make_shared(meta_tokens, g_mix, w_in, g_qnorm_a, g_knorm_a, lambda_q1, lambda_k1, lambda_q2, lambda_k2,
                         g_subln, w_out, g_ffn, w_ff_gate, w_ff_up, conv_w, conv_b, w_ff_down, g_final)
    xpf, xsf = f(x_prompt), f(x_sample)
    in_maps = []
    for c in range(8):
        m = dict(shared)
        m["xp"] = xpf[c]
        m["xs"] = xsf[c * NSAMP:(c + 1) * NSAMP]
        in_maps.append(m)
    res = run_bass_kernel_spmd(nc, in_maps, core_ids=list(range(8)))
    y_prompt = np.stack([np.asarray(res.results[c]["yp"], dtype=np.float32) for c in range(8)], axis=0)
    y_sample = np.concatenate([np.asarray(res.results[c]["ys"], dtype=np.float32) for c in range(8)], axis=0)
    return (y_prompt, y_sample)


def _f(a):
    return np.ascontiguousarray(np.asarray(a, dtype=np.float32))


def make_shared(meta_tokens, g_mix, w_in, g_qnorm_a, g_knorm_a, lambda_q1, lambda_k1, lambda_q2, lambda_k2,
                g_subln, w_out, g_ffn, w_ff_gate, w_ff_up, conv_w, conv_b, w_ff_down, g_final):
    f = _f
    if "tabs" not in _CACHE:
        _CACHE["tabs"] = _rope_tables()
    onesbd = np.zeros((128, 128), np.float32)
    onesbd[:64, :64] = 1.0
    onesbd[64:, 64:] = 1.0
    return {
        "meta": f(meta_tokens), "w_in": f(w_in[0]), "w_out": f(w_out[0]), "w_gate": f(w_ff_gate[0]), "w_up": f(w_ff_up[0]),
        "w_down": f(w_ff_down[0]), "g_mix": f(g_mix[0]).reshape(8, 128), "g_ffn": f(g_ffn[0]).reshape(8, 128),
        "gq": f(g_qnorm_a[0]).reshape(1, 64), "gk": f(g_knorm_a[0]).reshape(1, 64),
        "lam": np.stack([f(lambda_q1[0]), f(lambda_k1[0]), f(lambda_q2[0]), f(lambda_k2[0])], axis=0),
        "gsub": f(g_subln[0]).reshape(1, 128), "convw": f(conv_w[0]).reshape(66, 128), "convb": f(conv_b[0]).reshape(22, 128),
        "gfin": f(g_final).reshape(1, D), "tabs": _CACHE["tabs"], "identf": np.eye(128, dtype=np.float32), "onesbd": onesbd,
        "rmat": _rot_matrix(),
    }
```
